# Optimizing a Trainium2 kernel written in Bass

```python
import math
import jax, jax.numpy as jnp
from jax import lax
import numpy as np

D_MODEL = 2048
BATCH = 1
SEQ = 8192
DEPTH = 4
DEC_BATCH = 8
DEC_SEQ = 2048
PAST_LEN = 128

HEAD_DIM = 128
D_HYENA = D_MODEL // 2
HYENA_GROUPS = D_HYENA // HEAD_DIM
D_ATTN = D_MODEL // 4
N_ATTN_HEADS = D_ATTN // HEAD_DIM
D_MEMX = D_MODEL // 4
N_MEM_HEADS = D_MEMX // HEAD_DIM
N_MEM = 256
D_IN = 3 * D_HYENA + 3 * D_ATTN + D_MEMX
DILATED_PATTERNS = ((128, 1), (512, 4), (2048, 16))
ROPE_THETA = 500000.0
ROT_DIM = HEAD_DIM // 4
FILTER_EMB = 33
FILTER_HIDDEN = 64
HYENA_ORDER = 2
N_DIRS = 2
DECAY_FAST = 0.3
DECAY_SLOW = 1.5
DECAY_TARGET = 1e-2
DECAY_SHIFT = 0.05
D_FF = 11 * D_MODEL // 4
CONV_WIDTH = 3
EPS = 1e-6
NEG = -1e30

kernel_name = "hybrid_hyena_dilated_memx_encoder"

F32 = jnp.float32


def rmsnorm(x, g):
    xf = x.astype(F32)
    y = xf * lax.rsqrt(jnp.mean(xf * xf, axis=-1, keepdims=True) + EPS)
    return (y * g.astype(F32)).astype(x.dtype)


def group_rmsnorm(x, g):
    xf = x.astype(F32)
    parts = jnp.split(xf, [D_HYENA, D_HYENA + D_ATTN], axis=-1)
    normed = [p * lax.rsqrt(jnp.mean(p * p, axis=-1, keepdims=True) + EPS) for p in parts]
    return (jnp.concatenate(normed, axis=-1) * g.astype(F32)).astype(x.dtype)


def dwconv3(u, w, b):
    up = jnp.pad(u, ((0, 0), (1, 1), (0, 0)))
    return up[:, :-2] * w[0] + up[:, 1:-1] * w[1] + up[:, 2:] * w[2] + b


def hyena_filters(L, w1, b1, w2, b2, w3, b3, w4, freq):
    t = jnp.linspace(0.0, 1.0, L, dtype=F32)[:, None]
    bands = (FILTER_EMB - 1) // 2
    w = 2.0 * math.pi * jnp.arange(L, dtype=F32) / L
    f = jnp.linspace(1e-4, bands - 1, bands, dtype=F32)
    ang = w[:, None] * f[None, :]
    z = jnp.concatenate([t, jnp.cos(ang), -jnp.sin(ang)], axis=-1)
    fr = freq.astype(F32)
    hdn = jnp.sin(fr * (z @ w1.astype(F32) + b1.astype(F32)))
    hdn = jnp.sin(fr * (hdn @ w2.astype(F32) + b2.astype(F32)))
    hdn = jnp.sin(fr * (hdn @ w3.astype(F32) + b3.astype(F32)))
    k = (hdn @ w4.astype(F32)).reshape(L, HYENA_ORDER, N_DIRS, D_HYENA)
    deltas = jnp.abs(jnp.linspace(math.log(DECAY_TARGET) / DECAY_SLOW,
                                  math.log(DECAY_TARGET) / DECAY_FAST, D_HYENA, dtype=F32))
    decay = jnp.exp(-t * deltas[None, :]) + DECAY_SHIFT
    k = k * decay[:, None, None, :]
    k = k.at[0, :, 1, :].set(0.0)
    k = k / jnp.sum(jnp.abs(k), axis=(0, 2), keepdims=True)
    return k.transpose(1, 2, 0, 3)


def long_conv(u, h_fwd, h_bwd, bias):
    L, C = u.shape[1], u.shape[2]
    kern = jnp.concatenate([h_fwd, jnp.zeros((1, C), F32), h_bwd[1:][::-1]], axis=0)
    K = jnp.fft.rfft(kern, axis=0)
    uf = u.astype(F32)
    U = jnp.fft.rfft(uf, n=2 * L, axis=1)
    y = jnp.fft.irfft(U * K[None], n=2 * L, axis=1)[:, :L]
    return (y + uf * bias.astype(F32)).astype(u.dtype)


def rope_partial(x):
    L = x.shape[1]
    half = ROT_DIM // 2
    inv_freq = jnp.exp(-math.log(ROPE_THETA) * jnp.arange(0, ROT_DIM, 2, dtype=F32) / ROT_DIM)
    ang = jnp.arange(L, dtype=F32)[:, None] * inv_freq[None, :]
    c = jnp.cos(ang)[None, :, None, :]
    s = jnp.sin(ang)[None, :, None, :]
    xf = x.astype(F32)
    x1, x2, rest = xf[..., :half], xf[..., half:ROT_DIM], xf[..., ROT_DIM:]
    out = jnp.concatenate([x1 * c - x2 * s, x2 * c + x1 * s, rest], axis=-1)
    return out.astype(x.dtype)


def dilated_branch(q, k, v, dil, radius):
    B, L, H, hd = q.shape
    n = L // dil
    blk = radius
    nb = -(-n // blk)
    n_pad = nb * blk

    def split(t):
        return t.reshape(B, n, dil, H, hd).transpose(0, 2, 3, 1, 4).astype(F32)

    qs, ks, vs = split(q) * (1.0 / math.sqrt(hd)), split(k), split(v)
    qb = jnp.pad(qs, ((0, 0), (0, 0), (0, 0), (0, n_pad - n), (0, 0))).reshape(B, dil, H, nb, blk, hd)

    def windows(t):
        tp = jnp.pad(t, ((0, 0), (0, 0), (0, 0), (blk, n_pad - n + blk), (0, 0)))
        tp = tp.reshape(B, dil, H, nb + 2, blk, hd)
        return jnp.concatenate([tp[:, :, :, :-2], tp[:, :, :, 1:-1], tp[:, :, :, 2:]], axis=4)

    kw, vw = windows(ks), windows(vs)
    qpos = jnp.arange(nb)[:, None] * blk + jnp.arange(blk)[None, :]
    kpos = (jnp.arange(nb)[:, None] - 1) * blk + jnp.arange(3 * blk)[None, :]
    rel = kpos[:, None, :] - qpos[:, :, None]
    mask = (jnp.abs(rel) <= radius) & (kpos[:, None, :] >= 0) & (kpos[:, None, :] < n)
    s = jnp.einsum('bghnid,bghnjd->bghnij', qb, kw)
    s = jnp.where(mask, s, NEG)
    m = jnp.max(s, axis=-1, keepdims=True)
    p = jnp.exp(s - m)
    l = jnp.sum(p, axis=-1, keepdims=True)
    o = jnp.einsum('bghnij,bghnjd->bghnid', p, vw) / l
    lse = (m + jnp.log(l))[..., 0]
    o = o.reshape(B, dil, H, n_pad, hd)[:, :, :, :n].transpose(0, 3, 1, 2, 4).reshape(B, L, H, hd)
    lse = lse.reshape(B, dil, H, n_pad)[:, :, :, :n].transpose(0, 3, 1, 2).reshape(B, L, H)
    return o, lse


def layer(x, mem, lp):
    (g_pre_mix, w_in, conv_w, conv_b, f_w1, f_b1, f_w2, f_b2, f_w3, f_b3, f_w4, f_freq,
     f_bias, g_mem, w_mem_kv, g_grp, w_out, g_post_mix, g_pre_ffn, w_up, ffn_conv_w,
     ffn_conv_b, w_down, g_post_ffn) = lp
    B, L, _ = x.shape
    dt = x.dtype

    h = rmsnorm(x, g_pre_mix)
    proj = h @ w_in
    hy, qkv_a, q_m = jnp.split(proj, [3 * D_HYENA, 3 * D_HYENA + 3 * D_ATTN], axis=-1)

    hy = dwconv3(hy, conv_w, conv_b)
    v0, x1, x2 = jnp.split(hy, 3, axis=-1)
    filt = hyena_filters(L, f_w1, f_b1, f_w2, f_b2, f_w3, f_b3, f_w4, f_freq)
    z = x1 * long_conv(v0, filt[0, 0], filt[0, 1], f_bias[0])
    y_h = (x2 * long_conv(z, filt[1, 0], filt[1, 1], f_bias[1])).astype(dt)

    q_a, k_a, v_a = [t.reshape(B, L, N_ATTN_HEADS, HEAD_DIM) for t in jnp.split(qkv_a, 3, axis=-1)]
    q_a, k_a = rope_partial(q_a), rope_partial(k_a)
    outs, lses = [], []
    for window, dil in DILATED_PATTERNS:
        o, lse = dilated_branch(q_a, k_a, v_a, dil, window // (2 * dil))
        outs.append(o)
        lses.append(lse)
    wts = jax.nn.softmax(jnp.stack(lses, axis=0), axis=0)
    y_a = jnp.sum(wts[..., None] * jnp.stack(outs, axis=0), axis=0).reshape(B, L, D_ATTN).astype(dt)

    mh = rmsnorm(mem, g_mem)
    kv = (mh @ w_mem_kv).reshape(B, mem.shape[1], 2, N_MEM_HEADS, HEAD_DIM)
    km, vm = kv[:, :, 0].astype(F32), kv[:, :, 1].astype(F32)
    qm = q_m.reshape(B, L, N_MEM_HEADS, HEAD_DIM).astype(F32)
    sm = jnp.einsum('blhd,bmhd->bhlm', qm, km) * (1.0 / math.sqrt(HEAD_DIM))
    pm = jax.nn.softmax(sm, axis=-1)
    y_m = jnp.einsum('bhlm,bmhd->blhd', pm, vm).reshape(B, L, D_MEMX).astype(dt)

    mix = group_rmsnorm(jnp.concatenate([y_h, y_a, y_m], axis=-1), g_grp)
    x = x + rmsnorm(mix @ w_out, g_post_mix).astype(dt)

    hf = rmsnorm(x, g_pre_ffn)
    gate, val = jnp.split(hf @ w_up, 2, axis=-1)
    gate = dwconv3(gate, ffn_conv_w, ffn_conv_b)
    ff = jax.nn.gelu(gate, approximate=True) * val
    x = x + rmsnorm(ff @ w_down, g_post_ffn).astype(dt)
    return x


def setup_inputs(seed: int = 0) -> dict:
    key = jax.random.key(seed)
    ks = jax.random.split(key, 32)

    def nrm(k, shape, scale):
        return jax.random.normal(k, shape, F32) * scale

    def gain(k, n):
        return 1.0 + nrm(k, (DEPTH, n), 0.02)

    D = D_MODEL
    return {
        "x_prompt": nrm(ks[0], (BATCH, SEQ, D), 1.0),
        "x_sample": nrm(ks[1], (DEC_BATCH, DEC_SEQ, D), 1.0),
        "mem_prompt": nrm(ks[2], (BATCH, N_MEM, D), 1.0),
        "mem_sample": nrm(ks[3], (DEC_BATCH, N_MEM, D), 1.0),
        "g_pre_mix": gain(ks[4], D),
        "w_in": nrm(ks[5], (DEPTH, D, D_IN), D ** -0.5),
        "conv_w": nrm(ks[6], (DEPTH, CONV_WIDTH, 3 * D_HYENA), CONV_WIDTH ** -0.5),
        "conv_b": nrm(ks[7], (DEPTH, 3 * D_HYENA), 0.01),
        "f_w1": nrm(ks[8], (DEPTH, FILTER_EMB, FILTER_HIDDEN), FILTER_EMB ** -0.5),
        "f_b1": nrm(ks[9], (DEPTH, FILTER_HIDDEN), 0.1),
        "f_w2": nrm(ks[10], (DEPTH, FILTER_HIDDEN, FILTER_HIDDEN), FILTER_HIDDEN ** -0.5),
        "f_b2": nrm(ks[11], (DEPTH, FILTER_HIDDEN), 0.1),
        "f_w3": nrm(ks[12], (DEPTH, FILTER_HIDDEN, FILTER_HIDDEN), FILTER_HIDDEN ** -0.5),
        "f_b3": nrm(ks[13], (DEPTH, FILTER_HIDDEN), 0.1),
        "f_w4": nrm(ks[14], (DEPTH, FILTER_HIDDEN, HYENA_ORDER * N_DIRS * D_HYENA), FILTER_HIDDEN ** -0.5),
        "f_freq": 1.0 + nrm(ks[15], (DEPTH, FILTER_HIDDEN), 0.02),
        "f_bias": nrm(ks[16], (DEPTH, HYENA_ORDER, D_HYENA), 0.1),
        "g_mem": gain(ks[17], D),
        "w_mem_kv": nrm(ks[18], (DEPTH, D, 2 * D_MEMX), D ** -0.5),
        "g_grp": gain(ks[19], D),
        "w_out": nrm(ks[20], (DEPTH, D, D), D ** -0.5),
        "g_post_mix": gain(ks[21], D),
        "g_pre_ffn": gain(ks[22], D),
        "w_up": nrm(ks[23], (DEPTH, D, 2 * D_FF), D ** -0.5),
        "ffn_conv_w": nrm(ks[24], (DEPTH, CONV_WIDTH, D_FF), CONV_WIDTH ** -0.5),
        "ffn_conv_b": nrm(ks[25], (DEPTH, D_FF), 0.01),
        "w_down": nrm(ks[26], (DEPTH, D_FF, D), D_FF ** -0.5),
        "g_post_ffn": gain(ks[27], D),
    }


def reference(x_prompt, x_sample, mem_prompt, mem_sample, g_pre_mix, w_in, conv_w, conv_b,
              f_w1, f_b1, f_w2, f_b2, f_w3, f_b3, f_w4, f_freq, f_bias, g_mem, w_mem_kv,
              g_grp, w_out, g_post_mix, g_pre_ffn, w_up, ffn_conv_w, ffn_conv_b, w_down,
              g_post_ffn):
    y_prompt = x_prompt
    y_sample = x_sample
    for i in range(DEPTH):
        lp = (g_pre_mix[i], w_in[i], conv_w[i], conv_b[i], f_w1[i], f_b1[i], f_w2[i], f_b2[i],
              f_w3[i], f_b3[i], f_w4[i], f_freq[i], f_bias[i], g_mem[i], w_mem_kv[i], g_grp[i],
              w_out[i], g_post_mix[i], g_pre_ffn[i], w_up[i], ffn_conv_w[i], ffn_conv_b[i],
              w_down[i], g_post_ffn[i])
        y_prompt = layer(y_prompt, mem_prompt, lp)
        y_sample = layer(y_sample, mem_sample, lp)
    return (y_prompt, y_sample)
```

```python
import math
from contextlib import ExitStack
import numpy as np
import ml_dtypes
import concourse.bass as bass
import concourse.mybir as mybir
from concourse.bass_utils import run_bass_kernel_spmd

F32 = mybir.dt.float32
BF16 = mybir.dt.bfloat16
AF = mybir.ActivationFunctionType
ALU = mybir.AluOpType
AX = mybir.AxisListType
NPBF = ml_dtypes.bfloat16

D = 2048; DIN = 5120; DH = 1024; DFF = 5632; NMEM = 256; EPS = 1e-6
DEPTH = 4
LP = 8192; LS = 2048
NCORES = 8


class Sem:
    def __init__(self, h, idx):
        self.h = h; self.idx = idx; self.cnt = 0


class Eng:
    def __init__(self, P, name, h, same_wait):
        self.P = P; self.name = name; self.h = h; self.same_wait = same_wait
        self.sem = P.new_sem("e_" + name)
        self.seen = {}

    def wait(self, tok):
        sem, val = tok
        if sem is self.sem and not self.same_wait:
            return
        if self.seen.get(sem.idx, 0) >= val:
            return
        self.h.wait_ge(sem.h, val)
        self.seen[sem.idx] = val


class Buf:
    def __init__(self):
        self.w = {}; self.r = {}; self.dsem = None


def _merge(d, toks):
    for k, t in toks.items():
        if k not in d or d[k][1] < t[1]:
            d[k] = t


class Prog:
    def __init__(self):
        self.nc = bass.Bass("TRN2", target_bir_lowering=False)
        self.es = ExitStack()
        self.sems = []
        nc = self.nc
        self.pe = Eng(self, "pe", nc.tensor, False)
        self.act = Eng(self, "act", nc.scalar, True)
        self.dve = Eng(self, "dve", nc.vector, True)
        self.pool = Eng(self, "pool", nc.gpsimd, True)
        self.sp = Eng(self, "sp", nc.sync, False)
        self.engs = [self.pe, self.act, self.dve, self.pool, self.sp]
        self.free_dsems = []
        self.dsems = []
        self.phase_bufs = []
        self.uid = 0
        self.ddsem = self.new_sem("dd")
        self.ninst = 0

    def new_sem(self, name):
        h = self.es.enter_context(self.nc.semaphore(name))
        s = Sem(h, len(self.sems)); self.sems.append(s)
        return s

    def dram(self, name, shape, dt, kind="Internal"):
        return self.nc.dram_tensor(name, list(shape), dt, kind=kind).ap()

    def op(self, eng, fn, reads=(), writes=()):
        deps = {}
        for b in reads:
            _merge(deps, b.w)
        for b in writes:
            _merge(deps, b.w); _merge(deps, b.r)
        for t in deps.values():
            eng.wait(t)
        ins = fn(eng.h)
        eng.sem.cnt += 1
        ins.then_inc(eng.sem.h, 1)
        tok = (eng.sem, eng.sem.cnt)
        for b in reads:
            b.r[eng.sem.idx] = tok
        for b in writes:
            b.w = {eng.sem.idx: tok}; b.r = {}
        self.ninst += 1
        return tok

    def _dsem(self, b):
        if b.dsem is None:
            if self.free_dsems:
                b.dsem = self.free_dsems.pop()
            else:
                b.dsem = self.new_sem("d%d" % len(self.dsems))
                self.dsems.append(b.dsem)
            self.phase_bufs.append(b)
        return b.dsem

    def load(self, buf, pairs, q=None, extra_reads=()):
        q = q or self.sp
        s = self._dsem(buf)
        deps = {}
        _merge(deps, buf.w); _merge(deps, buf.r)
        for b in extra_reads:
            _merge(deps, b.w)
        for t in deps.values():
            q.wait(t)
        for o, i in pairs:
            q.h.dma_start(out=o, in_=i).then_inc(s.h, 16)
            s.cnt += 16
            self.ninst += 1
        buf.w = {s.idx: (s, s.cnt)}; buf.r = {}

    def store(self, bufs, pairs, q=None):
        q = q or self.pool
        s = self._dsem(bufs[0])
        deps = {}
        for b in bufs:
            _merge(deps, b.w)
        for t in deps.values():
            q.wait(t)
        for o, i in pairs:
            q.h.dma_start(out=o, in_=i).then_inc(s.h, 16)
            s.cnt += 16
            self.ninst += 1
        for b in bufs:
            b.r[s.idx] = (s, s.cnt)

    def dram_dma(self, o, i, q=None):
        q = q or self.pool
        q.h.dma_start(out=o, in_=i).then_inc(self.ddsem.h, 16)
        self.ddsem.cnt += 16

    def barrier(self):
        toks = [(e.sem, e.sem.cnt) for e in self.engs]
        toks += [(s, s.cnt) for s in self.dsems]
        toks.append((self.ddsem, self.ddsem.cnt))
        for e in self.engs:
            for t in toks:
                if t[1] > 0:
                    e.wait(t)

    class _Phase:
        def __init__(self, P):
            self.P = P; self.es = ExitStack()

        def __enter__(self):
            self.es.__enter__(); return self

        def sb(self, name, shape, dt):
            self.P.uid += 1
            return self.es.enter_context(self.P.nc.sbuf_tensor("%s_%d" % (name, self.P.uid), list(shape), dt))

        def ps(self, name, shape, dt):
            self.P.uid += 1
            return self.es.enter_context(self.P.nc.psum_tensor("%s_%d" % (name, self.P.uid), list(shape), dt))

        def __exit__(self, *a):
            P = self.P
            P.barrier()
            for b in P.phase_bufs:
                P.free_dsems.append(b.dsem); b.dsem = None
            P.phase_bufs = []
            return self.es.__exit__(*a)

    def phase(self):
        return Prog._Phase(self)


def bufs(n):
    return [Buf() for _ in range(n)]


def seq_params(L):
    nJ = L // 128
    return dict(L=L, nJ=nJ, MJ=2 * nJ, gB=128 // (2 * nJ), gA=128 // nJ, CB=32 * (128 // (2 * nJ)), nblk=DH // (32 * (128 // (2 * nJ))))


def dft_tables(nJ):
    MJ = 2 * nJ; gB = 128 // MJ
    a = np.arange(128)[:, None]; f = np.arange(128)[None, :]
    thj = -2 * np.pi / 255
    Ffw = np.exp(1j * thj * (a * f)); Fbw = np.exp(1j * thj * ((127 - a) * f))
    fa = np.stack([np.concatenate([Ffw.real, Ffw.imag], 1), np.concatenate([Fbw.real, Fbw.imag], 1)])
    B = np.arange(nJ)[:, None]; fJ = np.arange(MJ)[None, :]
    thJ = -2 * np.pi / MJ
    Jfw = np.exp(1j * thJ * (B * fJ)); Jbw = np.exp(1j * thJ * (-(B + 1) * fJ))
    sb = np.zeros((2, 2, 3, 128, 128))
    for d, Jm in enumerate((Jfw, Jbw)):
        for h in range(2):
            for c in range(gB):
                r0 = 64 * h + c * nJ; m0 = c * MJ
                sb[d, h, 0, r0:r0 + nJ, m0:m0 + MJ] = Jm.real
                sb[d, h, 1, r0:r0 + nJ, m0:m0 + MJ] = Jm.imag
                sb[d, h, 2, r0:r0 + nJ, m0:m0 + MJ] = -Jm.imag
    I = np.arange(nJ)[None, :]; fJc = np.arange(MJ)[:, None]
    GJlo = np.exp(1j * thJ * (-(I * fJc))); GJhi = np.exp(1j * thJ * (-((I - 1) * fJc)))
    Gre = np.zeros((128, 2, gB * nJ)); Gim = np.zeros((128, 2, gB * nJ))
    for c in range(gB):
        for k, Gm in enumerate((GJlo, GJhi)):
            Gre[c * MJ:(c + 1) * MJ, k, c * nJ:(c + 1) * nJ] = Gm.real
            Gim[c * MJ:(c + 1) * MJ, k, c * nJ:(c + 1) * nJ] = Gm.imag
    Gre = Gre.reshape(128, 128); Gim = Gim.reshape(128, 128)
    R = np.stack([np.concatenate([Gre, Gim], 1), np.concatenate([-Gim, Gre], 1), np.concatenate([-Gre, -Gim], 1)])
    cf = np.where(np.arange(128) == 0, 1.0, 2.0)[:, None] / (MJ * 255.0)
    e = np.arange(255)[None, :]; ff = np.arange(128)[:, None]
    th = 2 * np.pi * e * ff / 255
    gre = cf * np.cos(th); gim = -cf * np.sin(th)
    z1 = np.zeros((128, 1))
    ga = np.stack([gre[:, :128], gim[:, :128], np.concatenate([gre[:, 128:], z1], 1), np.concatenate([gim[:, 128:], z1], 1)])
    return fa, sb.reshape(12, 128, 128), R, ga


def decay_tables(L):
    sp = seq_params(L); nJ = sp["nJ"]; CB = sp["CB"]; nblk = sp["nblk"]
    t = np.linspace(0.0, 1.0, L, dtype=np.float32)
    deltas = np.abs(np.linspace(math.log(1e-2) / 1.5, math.log(1e-2) / 0.3, DH, dtype=np.float32))
    dec = (np.exp(-t[:, None] * deltas[None, :]) + np.float32(0.05)).astype(np.float32)
    decb = np.concatenate([dec[1:], np.zeros((1, DH), np.float32)], 0)
    out = []
    for dd in (dec, decb):
        x = dd.reshape(nJ, 128, nblk, CB).transpose(1, 2, 0, 3)
        out.append(np.ascontiguousarray(x))
    return np.stack(out)


def zfeat_T(L):
    t = np.linspace(0.0, 1.0, L, dtype=np.float32)[:, None]
    w = (np.float32(2.0 * math.pi) * np.arange(L, dtype=np.float32) / np.float32(L)).astype(np.float32)
    f = np.linspace(1e-4, 15, 16, dtype=np.float32)
    ang = (w[:, None] * f[None, :]).astype(np.float32)
    z = np.concatenate([t, np.cos(ang), -np.sin(ang)], -1).astype(np.float32)
    return np.ascontiguousarray(z.T)


def rope_tables():
    inv = np.exp(np.float32(-math.log(500000.0)) * np.arange(0, 32, 2, dtype=np.float32) / np.float32(32)).astype(np.float32)
    ang = (np.arange(LP, dtype=np.float32)[:, None] * inv[None, :]).astype(np.float32)
    c = np.cos(ang).T; s = np.sin(ang).T
    C = np.concatenate([c, c], 0); S = np.concatenate([s, s], 0)
    pm = np.zeros((32, 32), np.float32)
    for m in range(16):
        pm[m + 16, m] = -1.0; pm[m, m + 16] = 1.0
    return np.ascontiguousarray(np.stack([C, S]).astype(np.float32)), pm


def mask_tiles():
    out = np.zeros((128, 20, 512), np.float32)
    tk = np.arange(128)[:, None]; tq = np.arange(512)[None, :]
    for o in range(20):
        dlt = (-1024 + 128 * o + tk) - tq
        m = np.zeros_like(dlt, dtype=np.float32)
        for dil in (1, 4, 16):
            m += ((dlt % dil == 0) & (np.abs(dlt) <= 64 * dil)).astype(np.float32)
        out[:, o, :] = m
    return out


class Ctx:
    pass


def rstd_from_ss(P, ss_ap, ss_buf, n):
    P.op(P.dve, lambda h: h.tensor_scalar(out=ss_ap, in0=ss_ap, scalar1=1.0 / n, scalar2=EPS, op0=ALU.mult, op1=ALU.add), [ss_buf], [ss_buf])
    P.op(P.act, lambda h: h.activation(out=ss_ap, in_=ss_ap, func=AF.Sqrt), [ss_buf], [ss_buf])
    P.op(P.dve, lambda h: h.reciprocal(out=ss_ap, in_=ss_ap), [ss_buf], [ss_buf])


def phase_up(P, C, x_ap, ntok, gcol_ap, wblk_ap, nblk, BW, kind, out_ap=None, tok0=0, ffn=None, seq_edges=None):
    TT = min(512, ntok); ns = TT // 128; ntiles = ntok // TT
    halo = (kind == "ffn")
    NH = TT + (2 if halo else 0)
    with P.phase() as ph:
        xt = ph.sb("xt", [128, ns, D], F32); xb = bufs(ns)
        sq = ph.sb("sq", [128, D], BF16); sqb = Buf()
        ss = ph.sb("ss", [128, ns + 1], F32); ssb = bufs(ns + 1)
        hb = ph.sb("hb", [128, ns, D], BF16); hbb = bufs(ns)
        hT = ph.sb("hT", [128, 16, NH], BF16); hTb = Buf()
        gcol = ph.sb("gcol", [128, 16], F32); gcb = Buf()
        wsz = 16 * BW if kind == "proj" else 2 * 16 * 256
        wb = ph.sb("wb", [128, 2, wsz], BF16); wbb = bufs(2)
        ost = ph.sb("ost", [128, 4, TT], F32 if kind == "proj" else BF16); ostb = bufs(4)
        tp = [ph.ps("tp%d" % i, [128, 8, 128], BF16) for i in range(2)]; tpb = bufs(2)
        acc = [ph.ps("acc%d" % i, [128, 512], F32) for i in range(4)]; accb = bufs(4)
        if halo:
            xh = ph.sb("xh", [2, D], F32); xhb = Buf()
            hh = ph.sb("hh", [2, D], BF16); hhb = Buf()
            tph = ph.ps("tph", [128, 16, 2], BF16); tphb = Buf()
            acch = ph.ps("acch", [128, 2, 2], F32); acchb = bufs(2)
            ue = ph.sb("ue", [128, 2, TT + 2], F32); ueb = bufs(2)
            tt_ = ph.sb("tt", [128, 2, TT], F32); ttb = bufs(2)
            fcw = ph.sb("fcw", [128, 44, 4], F32); fcwb = Buf()
            P.load(fcwb, [(fcw[:], ffn["fcw"])])
        P.load(gcb, [(gcol[:], gcol_ap)])
        widx = 0
        nwb = nblk
        def wload(i):
            if kind == "proj":
                P.load(wbb[i % 2], [(wb[:, i % 2, :], wblk_ap[i % nwb])])
            else:
                P.load(wbb[i % 2], [(wb[:, i % 2, :], wblk_ap[i % nwb])])
        total_w = ntiles * nwb
        wload(0)
        acci = 0; osti = 0; evi = 0
        for ti in range(ntiles):
            t0 = ti * TT
            for s in range(ns):
                P.load(xb[s], [(xt[:, s, :], x_ap[t0 + s * 128:t0 + (s + 1) * 128, :])])
            if halo:
                lo_ok = not (seq_edges and t0 == 0); hi_ok = not (seq_edges and t0 + TT == ntok)
                prs = []
                if lo_ok:
                    prs.append((xh[0:1, :], x_ap[t0 - 1:t0, :]))
                if hi_ok:
                    prs.append((xh[1:2, :], x_ap[t0 + TT:t0 + TT + 1, :]))
                if not (lo_ok and hi_ok):
                    P.op(P.dve, lambda h: h.memset(xh[:], 1.0), [], [xhb])
                if prs:
                    P.load(xhb, prs)
            srcs = [(xt[:, s, :], xb[s], hb[:, s, :], hbb[s], 128, s) for s in range(ns)]
            if halo:
                srcs.append((xh[:], xhb, hh[:], hhb, 2, ns))
            for (xa, xbuf, ha, hbuf, npart, si) in srcs:
                P.op(P.act, lambda h: h.activation(out=sq[0:npart, :], in_=xa, func=AF.Square, accum_out=ss[0:npart, si:si + 1]), [xbuf], [sqb, ssb[si]])
                rstd_from_ss(P, ss[0:npart, si:si + 1], ssb[si], D)
                P.op(P.dve, lambda h: h.tensor_scalar(out=ha, in0=xa, scalar1=ss[0:npart, si:si + 1], scalar2=None, op0=ALU.mult), [xbuf, ssb[si]], [hbuf])
            for s in range(ns):
                for c8 in range(2):
                    k = (s * 2 + c8) % 2
                    for c in range(8):
                        cc = c8 * 8 + c
                        P.op(P.pe, lambda h: h.transpose(out=tp[k][:, c, :], in_=hb[:, s, cc * 128:(cc + 1) * 128], identity=C.identb[:]), [hbb[s]], [tpb[k]])
                    for c in range(8):
                        cc = c8 * 8 + c
                        eng = P.act if (evi % 2 == 0) else P.dve
                        evi += 1
                        if eng is P.act:
                            P.op(eng, lambda h: h.activation(out=hT[:, cc, s * 128:(s + 1) * 128], in_=tp[k][:, c, :], func=AF.Copy, scale=gcol[:, cc:cc + 1]), [tpb[k], gcb], [hTb])
                        else:
                            P.op(eng, lambda h: h.tensor_scalar(out=hT[:, cc, s * 128:(s + 1) * 128], in0=tp[k][:, c, :], scalar1=gcol[:, cc:cc + 1], scalar2=None, op0=ALU.mult), [tpb[k], gcb], [hTb])
            if halo:
                for cc in range(16):
                    P.op(P.pe, lambda h: h.transpose(out=tph[:, cc, :], in_=hh[0:2, cc * 128:(cc + 1) * 128], identity=C.identb[0:2, 0:2]), [hhb], [tphb])
                for cc in range(16):
                    P.op(P.dve, lambda h: h.tensor_scalar(out=hT[:, cc, TT:TT + 2], in0=tph[:, cc, :], scalar1=gcol[:, cc:cc + 1], scalar2=None, op0=ALU.mult), [tphb, gcb], [hTb])
            for bi in range(nwb):
                if widx + 1 < total_w:
                    wload(widx + 1)
                wcur = widx % 2
                widx += 1
                if kind == "proj":
                    wv = wb[:, wcur, :].rearrange("p (c f) -> p c f", c=16)
                    for fb in range(BW // 128):
                        a = acci % 4; acci += 1
                        for kc in range(16):
                            P.op(P.pe, lambda h: h.matmul(acc[a][:, 0:TT], wv[:, kc, fb * 128:(fb + 1) * 128], hT[:, kc, 0:TT], start=(kc == 0), stop=(kc == 15)), [wbb[wcur], hTb], [accb[a]])
                        o = osti % 4; osti += 1
                        eng = P.act if (o % 2 == 0) else P.dve
                        if eng is P.act:
                            P.op(eng, lambda h: h.activation(out=ost[:, o, :], in_=acc[a][:, 0:TT], func=AF.Copy), [accb[a]], [ostb[o]])
                        else:
                            P.op(eng, lambda h: h.tensor_copy(out=ost[:, o, :], in_=acc[a][:, 0:TT]), [accb[a]], [ostb[o]])
                        f0 = bi * BW + fb * 128
                        P.store([ostb[o]], [(out_ap[f0:f0 + 128, tok0 + t0:tok0 + t0 + TT], ost[:, o, :])])
                else:
                    wv = wb[:, wcur, :].rearrange("p (g c f) -> p g c f", g=2, c=16)
                    for fb in range(2):
                        fidx = bi * 2 + fb
                        ag = acci % 4; av = (acci + 1) % 4; acci += 2
                        hk = fb % 2
                        for kc in range(16):
                            P.op(P.pe, lambda h: h.matmul(acc[ag][:, 0:TT], wv[:, 0, kc, fb * 128:(fb + 1) * 128], hT[:, kc, 0:TT], start=(kc == 0), stop=(kc == 15)), [wbb[wcur], hTb], [accb[ag]])
                        for kc in range(16):
                            P.op(P.pe, lambda h: h.matmul(acch[:, hk, :], wv[:, 0, kc, fb * 128:(fb + 1) * 128], hT[:, kc, TT:TT + 2], start=(kc == 0), stop=(kc == 15)), [wbb[wcur], hTb], [acchb[hk]])
                        for kc in range(16):
                            P.op(P.pe, lambda h: h.matmul(acc[av][:, 0:TT], wv[:, 1, kc, fb * 128:(fb + 1) * 128], hT[:, kc, 0:TT], start=(kc == 0), stop=(kc == 15)), [wbb[wcur], hTb], [accb[av]])
                        u = fidx % 2
                        P.op(P.act, lambda h: h.activation(out=ue[:, u, 1:TT + 1], in_=acc[ag][:, 0:TT], func=AF.Copy), [accb[ag]], [ueb[u]])
                        if lo_ok:
                            P.op(P.dve, lambda h: h.tensor_copy(out=ue[:, u, 0:1], in_=acch[:, hk, 0:1]), [acchb[hk]], [ueb[u]])
                        else:
                            P.op(P.dve, lambda h: h.memset(ue[:, u, 0:1], 0.0), [acchb[hk]], [ueb[u]])
                        if hi_ok:
                            P.op(P.dve, lambda h: h.tensor_copy(out=ue[:, u, TT + 1:TT + 2], in_=acch[:, hk, 1:2]), [acchb[hk]], [ueb[u]])
                        else:
                            P.op(P.dve, lambda h: h.memset(ue[:, u, TT + 1:TT + 2], 0.0), [acchb[hk]], [ueb[u]])
                        P.op(P.act, lambda h: h.activation(out=tt_[:, u, :], in_=ue[:, u, 1:TT + 1], func=AF.Identity, scale=fcw[:, fidx, 1:2], bias=fcw[:, fidx, 3:4]), [ueb[u], fcwb], [ttb[u]])
                        P.op(P.dve, lambda h: h.scalar_tensor_tensor(out=tt_[:, u, :], in0=ue[:, u, 0:TT], scalar=fcw[:, fidx, 0:1], in1=tt_[:, u, :], op0=ALU.mult, op1=ALU.add), [ueb[u], fcwb, ttb[u]], [ttb[u]])
                        P.op(P.dve, lambda h: h.scalar_tensor_tensor(out=tt_[:, u, :], in0=ue[:, u, 2:TT + 2], scalar=fcw[:, fidx, 2:3], in1=tt_[:, u, :], op0=ALU.mult, op1=ALU.add), [ueb[u], fcwb, ttb[u]], [ttb[u]])
                        P.op(P.act, lambda h: h.activation(out=tt_[:, u, :], in_=tt_[:, u, :], func=AF.Gelu_apprx_tanh), [ttb[u]], [ttb[u]])
                        o = osti % 4; osti += 1
                        P.op(P.dve, lambda h: h.tensor_tensor(out=ost[:, o, :], in0=tt_[:, u, :], in1=acc[av][:, 0:TT], op=ALU.mult), [ttb[u], accb[av]], [ostb[o]])
                        P.store([ostb[o]], [(out_ap[fidx * 128:(fidx + 1) * 128, tok0 + t0:tok0 + t0 + TT], ost[:, o, :])])


def phase_down(P, C, aT_ap, tok0, ntok, wblk_ap, ncb, KC, BW, gbc_ap, xres_ap, xout_ap):
    TT = 512; ntiles = ntok // TT
    aview = aT_ap.rearrange("(kc p) t -> p kc t", p=128)
    with P.phase() as ph:
        aT = ph.sb("aT", [128, KC, TT], BF16); aTb = Buf()
        wb = ph.sb("wb", [128, 2, KC * BW], BF16); wbb = bufs(2)
        yo = ph.sb("yo", [128, 4, D], F32); yob = bufs(4)
        xr = ph.sb("xr", [128, 2, D], F32); xrb = bufs(2)
        gbc = ph.sb("gbc", [128, D], F32); gbb = Buf()
        sq = ph.sb("sq", [128, D], BF16); sqb = Buf()
        ss = ph.sb("ss", [128, 4], F32); ssb = bufs(4)
        acc = [ph.ps("acc%d" % i, [128, 512], F32) for i in range(4)]; accb = bufs(4)
        P.load(gbb, [(gbc[:], gbc_ap.partition_broadcast(128))])
        total_w = ntiles * ncb
        P.load(wbb[0], [(wb[:, 0, :], wblk_ap[0])])
        widx = 0; acci = 0; xi = 0
        for ti in range(ntiles):
            t0 = ti * TT
            P.load(aTb, [(aT[:], aview[:, :, tok0 + t0:tok0 + t0 + TT])])
            for cb in range(ncb):
                if widx + 1 < total_w:
                    P.load(wbb[(widx + 1) % 2], [(wb[:, (widx + 1) % 2, :], wblk_ap[(widx + 1) % ncb])])
                wcur = widx % 2; widx += 1
                wv = wb[:, wcur, :].rearrange("p (k f) -> p k f", k=KC)
                for s in range(4):
                    a = acci % 4; acci += 1
                    for kc in range(KC):
                        P.op(P.pe, lambda h: h.matmul(acc[a][:, 0:BW], aT[:, kc, s * 128:(s + 1) * 128], wv[:, kc, :], start=(kc == 0), stop=(kc == KC - 1)), [aTb, wbb[wcur]], [accb[a]])
                    if a % 2 == 0:
                        P.op(P.act, lambda h: h.activation(out=yo[:, s, cb * BW:(cb + 1) * BW], in_=acc[a][:, 0:BW], func=AF.Copy), [accb[a]], [yob[s]])
                    else:
                        P.op(P.dve, lambda h: h.tensor_copy(out=yo[:, s, cb * BW:(cb + 1) * BW], in_=acc[a][:, 0:BW]), [accb[a]], [yob[s]])
            for s in range(4):
                r0 = t0 + s * 128
                x = xi % 2; xi += 1
                P.load(xrb[x], [(xr[:, x, :], xres_ap[r0:r0 + 128, :])])
                P.op(P.act, lambda h: h.activation(out=sq[:], in_=yo[:, s, :], func=AF.Square, accum_out=ss[:, s:s + 1]), [yob[s]], [sqb, ssb[s]])
                rstd_from_ss(P, ss[:, s:s + 1], ssb[s], D)
                P.op(P.dve, lambda h: h.scalar_tensor_tensor(out=yo[:, s, :], in0=yo[:, s, :], scalar=ss[:, s:s + 1], in1=gbc[:], op0=ALU.mult, op1=ALU.mult), [yob[s], ssb[s], gbb], [yob[s]])
                P.op(P.dve, lambda h: h.tensor_tensor(out=yo[:, s, :], in0=yo[:, s, :], in1=xr[:, x, :], op=ALU.add), [yob[s], xrb[x]], [yob[s]])
                P.store([yob[s]], [(xout_ap[r0:r0 + 128, :], yo[:, s, :])])


def phase_mixprep(P, C, mixT_ap, tok0, ntok, ggcol_ap, mixn_ap):
    TT = 512; ntiles = ntok // TT
    mview = mixT_ap.rearrange("(c p) t -> p c t", p=128)
    oview = mixn_ap.rearrange("(c p) t -> p c t", p=128)
    groups = [(0, 8), (8, 12), (12, 16)]
    with P.phase() as ph:
        mx = ph.sb("mx", [128, 16, TT], F32); mxb = Buf()
        sq = ph.sb("sq", [128, 4, TT], BF16); sqb = bufs(4)
        rs = ph.sb("rs", [128, 3, TT], F32); rsb = bufs(3)
        mo = ph.sb("mo", [128, 16, TT], BF16); mob = Buf()
        gg = ph.sb("gg", [128, 16], F32); ggb = Buf()
        acc = [ph.ps("acc%d" % i, [128, 512], F32) for i in range(3)]; accb = bufs(3)
        P.load(ggb, [(gg[:], ggcol_ap)])
        qi = 0
        for ti in range(ntiles):
            t0 = tok0 + ti * TT
            P.load(mxb, [(mx[:], mview[:, :, t0:t0 + TT])])
            for gi, (c0, c1) in enumerate(groups):
                for c in range(c0, c1):
                    q = qi % 4; qi += 1
                    P.op(P.act, lambda h: h.activation(out=sq[:, q, :], in_=mx[:, c, :], func=AF.Square), [mxb], [sqb[q]])
                    P.op(P.pe, lambda h: h.matmul(acc[gi][:], C.onesb[:], sq[:, q, :], start=(c == c0), stop=(c == c1 - 1)), [sqb[q]], [accb[gi]])
                n = (c1 - c0) * 128
                P.op(P.dve, lambda h: h.tensor_scalar(out=rs[:, gi, :], in0=acc[gi][:], scalar1=1.0 / n, scalar2=EPS, op0=ALU.mult, op1=ALU.add), [accb[gi]], [rsb[gi]])
                P.op(P.act, lambda h: h.activation(out=rs[:, gi, :], in_=rs[:, gi, :], func=AF.Sqrt), [rsb[gi]], [rsb[gi]])
                P.op(P.dve, lambda h: h.reciprocal(out=rs[:, gi, :], in_=rs[:, gi, :]), [rsb[gi]], [rsb[gi]])
                for c in range(c0, c1):
                    P.op(P.dve, lambda h: h.scalar_tensor_tensor(out=mo[:, c, :], in0=mx[:, c, :], scalar=gg[:, c:c + 1], in1=rs[:, gi, :], op0=ALU.mult, op1=ALU.mult), [mxb, ggb, rsb[gi]], [mob])
            P.store([mob], [(oview[:, :, t0:t0 + TT], mo[:])])


def attn_core(P, C, ph, qr, qrb, kr, krb, vt, vtb, Lq, Lk, masked, mixT_ap, row0, tok0, st):
    sc = 1.0 / math.sqrt(128.0)
    for q0 in range(0, Lq, 512):
        if masked:
            kbs = [(kb, (kb * 128 - q0 + 1024) // 128) for kb in range(Lk // 128) if -1024 <= kb * 128 - q0 < 1536]
        else:
            kbs = [(kb, None) for kb in range(Lk // 128)]
        oi = st["oi"] % 2; st["oi"] += 1
        for n, (kb, mo) in enumerate(kbs):
            si = st["si"] % 2; st["si"] += 1
            P.op(P.pe, lambda h: h.matmul(st["sps"][si][:], kr[:, kb * 128:(kb + 1) * 128], qr[:, q0:q0 + 512], start=True, stop=True), [krb, qrb], [st["spsb"][si]])
            pi = st["pi"] % 3; st["pi"] += 1
            if masked:
                P.op(P.act, lambda h: h.activation(out=st["pe_"][:, pi, :], in_=st["sps"][si][:], func=AF.Exp, scale=sc), [st["spsb"][si]], [st["peb"][pi]])
                P.op(P.dve, lambda h: h.tensor_tensor(out=st["pm"][:, pi, :], in0=st["pe_"][:, pi, :], in1=C.mask[:, mo, :], op=ALU.mult), [st["peb"][pi], C.maskb], [st["pmb"][pi]])
            else:
                P.op(P.act, lambda h: h.activation(out=st["pm"][:, pi, :], in_=st["sps"][si][:], func=AF.Exp, scale=sc), [st["spsb"][si]], [st["pmb"][pi]])
            P.op(P.pe, lambda h: h.matmul(st["ops"][oi][:], vt[:, kb, :], st["pm"][:, pi, :], start=(n == 0), stop=(n == len(kbs) - 1)), [vtb, st["pmb"][pi]], [st["opsb"][oi]])
            P.op(P.pe, lambda h: h.matmul(st["dps"][oi][:], C.onesb[:], st["pm"][:, pi, :], start=(n == 0), stop=(n == len(kbs) - 1)), [st["pmb"][pi]], [st["dpsb"][oi]])
        r = st["ri"] % 2; st["ri"] += 1
        P.op(P.dve, lambda h: h.reciprocal(out=st["rc"][:, r, :], in_=st["dps"][oi][:]), [st["dpsb"][oi]], [st["rcb"][r]])
        P.op(P.dve, lambda h: h.tensor_tensor(out=st["yo"][:, r, :], in0=st["ops"][oi][:], in1=st["rc"][:, r, :], op=ALU.mult), [st["opsb"][oi], st["rcb"][r]], [st["yob"][r]])
        P.store([st["yob"][r]], [(mixT_ap[row0:row0 + 128, tok0 + q0:tok0 + q0 + 512], st["yo"][:, r, :])])


def attn_state(ph):
    st = dict(oi=0, si=0, pi=0, ri=0)
    st["sps"] = [ph.ps("sps%d" % i, [128, 512], F32) for i in range(2)]; st["spsb"] = bufs(2)
    st["ops"] = [ph.ps("ops%d" % i, [128, 512], F32) for i in range(2)]; st["opsb"] = bufs(2)
    st["dps"] = [ph.ps("dps%d" % i, [128, 512], F32) for i in range(2)]; st["dpsb"] = bufs(2)
    st["pe_"] = ph.sb("pex", [128, 3, 512], BF16); st["peb"] = bufs(3)
    st["pm"] = ph.sb("pmx", [128, 3, 512], BF16); st["pmb"] = bufs(3)
    st["rc"] = ph.sb("rcx", [128, 2, 512], F32); st["rcb"] = bufs(2)
    st["yo"] = ph.sb("yox", [128, 2, 512], F32); st["yob"] = bufs(2)
    return st


def load_feat_bf16(P, C, ph, src_ap, row0, tok0, L, dst, dstb, stage, stageb, rope=None, tpv=None):
    for ci, c0 in enumerate(range(0, L, 2048)):
        n = min(2048, L - c0)
        k = C.stg_i % 2; C.stg_i += 1
        P.load(stageb[k], [(stage[:, k, 0:n], src_ap[row0:row0 + 128, tok0 + c0:tok0 + c0 + n])])
        if tpv is not None:
            tps, tpsb, vb16, vb16b = tpv
            P.op(P.act, lambda h: h.activation(out=vb16[:, 0:n], in_=stage[:, k, 0:n], func=AF.Copy), [stageb[k]], [vb16b])
            for j in range(n // 128):
                kk = C.tp_i % 2; C.tp_i += 1
                P.op(P.pe, lambda h: h.transpose(out=tps[kk], in_=vb16[:, j * 128:(j + 1) * 128], identity=C.identb[:]), [vb16b], [tpsb[kk]])
                P.op(P.dve, lambda h: h.tensor_copy(out=dst[:, c0 // 128 + j, :], in_=tps[kk]), [tpsb[kk]], [dstb])
            continue
        if rope is None:
            P.op(P.act, lambda h: h.activation(out=dst[:, c0:c0 + n], in_=stage[:, k, 0:n], func=AF.Copy), [stageb[k]], [dstb])
        else:
            rps, rpsb, cs, csb, t1, t1b = rope
            P.op(P.act, lambda h: h.activation(out=dst[:, c0:c0 + n], in_=stage[:, k, 0:n], func=AF.Copy), [stageb[k]], [dstb])
            for j in range(n // 512):
                kk = C.tp_i % 2; C.tp_i += 1
                cc = c0 + j * 512
                P.load(csb, [(cs[:, 0, :], C.rope_cs[0, :, cc:cc + 512]), (cs[:, 1, :], C.rope_cs[1, :, cc:cc + 512])])
                P.op(P.pe, lambda h: h.matmul(rps[kk][0:32, :], C.ropepm[:], stage[0:32, k, j * 512:(j + 1) * 512], start=True, stop=True), [stageb[k]], [rpsb[kk]])
                P.op(P.dve, lambda h: h.tensor_tensor(out=t1[:], in0=rps[kk][0:32, :], in1=cs[:, 1, :], op=ALU.mult), [rpsb[kk], csb], [t1b])
                P.op(P.dve, lambda h: h.tensor_tensor(out=cs[:, 0, :], in0=stage[0:32, k, j * 512:(j + 1) * 512], in1=cs[:, 0, :], op=ALU.mult), [stageb[k], csb], [csb])
                P.op(P.dve, lambda h: h.tensor_tensor(out=dst[0:32, cc:cc + 512], in0=t1[:], in1=cs[:, 0, :], op=ALU.add), [t1b, csb], [dstb])


def phase_attn(P, C, projT, mixT, tok0, L, kvT=None):
    memx = kvT is not None
    Lk = NMEM if memx else L
    with P.phase() as ph:
        stage = ph.sb("stage", [128, 2, 2048], F32); stageb = bufs(2)
        qr = ph.sb("qr", [128, L], BF16); qrb = Buf()
        kr = ph.sb("kr", [128, Lk], BF16); krb = Buf()
        vt = ph.sb("vt", [128, Lk // 128, 128], BF16); vtb = Buf()
        vb16 = ph.sb("vb16", [128, 2048], BF16); vb16b = Buf()
        misc = ph.ps("misc", [128, 512], F32); miscb = Buf()
        tps = [misc[:, 0:64].bitcast(BF16)] * 2; tpsb = [miscb] * 2
        st = attn_state(ph)
        if not memx:
            rps = [misc] * 2; rpsb = [miscb] * 2
            mask = ph.sb("mask", [128, 20, 512], BF16); maskb = Buf()
            P.load(maskb, [(mask[:], C.mask_d)])
            C.mask = mask; C.maskb = maskb
            cs = ph.sb("cs", [32, 2, 512], F32); csb = Buf()
            t1 = ph.sb("t1", [32, 512], F32); t1b = Buf()
            rope = (rps, rpsb, cs, csb, t1, t1b)
        for hd in range(4):
            if memx:
                load_feat_bf16(P, C, ph, projT, 4608 + hd * 128, tok0, L, qr, qrb, stage, stageb)
                load_feat_bf16(P, C, ph, kvT, hd * 128, 0, NMEM, kr, krb, stage, stageb)
                load_feat_bf16(P, C, ph, kvT, 512 + hd * 128, 0, NMEM, vt, vtb, stage, stageb, tpv=(tps, tpsb, vb16, vb16b))
                attn_core(P, C, ph, qr, qrb, kr, krb, vt, vtb, L, NMEM, False, mixT, 1536 + hd * 128, tok0, st)
            else:
                load_feat_bf16(P, C, ph, projT, 3072 + hd * 128, tok0, L, qr, qrb, stage, stageb, rope=rope)
                load_feat_bf16(P, C, ph, projT, 3584 + hd * 128, tok0, L, kr, krb, stage, stageb, rope=rope)
                load_feat_bf16(P, C, ph, projT, 4096 + hd * 128, tok0, L, vt, vtb, stage, stageb, tpv=(tps, tpsb, vb16, vb16b))
                attn_core(P, C, ph, qr, qrb, kr, krb, vt, vtb, L, L, True, mixT, 1024 + hd * 128, tok0, st)


def phase_filter_mlp(P, C, l, sp, zT_ap, hd3_ap):
    L = sp["L"]
    with P.phase() as ph:
        prm = ph.sb("prm", [64, 8], F32); prmb = Buf()
        w1 = ph.sb("w1", [33, 64], F32); w2 = ph.sb("w2", [64, 2, 64], F32); wb_ = Buf()
        zt = ph.sb("zt", [33, 2, 512], F32); ztb = bufs(2)
        h = ph.sb("h", [64, 3, 512], F32); hbf = bufs(3)
        t = ph.sb("t", [64, 2, 512], F32); tb = bufs(2)
        ho = ph.sb("ho", [64, 2, 512], BF16); hob = bufs(2)
        zz = ph.sb("zz", [64, 128], BF16); zzb = Buf()
        ps_ = [ph.ps("mps%d" % i, [64, 512], F32) for i in range(2)]; psb = bufs(2)
        P.load(prmb, [(prm[:, 0:4], C.fprm[l])])
        P.load(wb_, [(w1[:], C.f_w1[l]), (w2[:, 0, :], C.f_w2[l]), (w2[:, 1, :], C.f_w3[l])])
        for j in range(3):
            P.op(P.dve, lambda h_: h_.scalar_tensor_tensor(out=prm[:, 4 + j:5 + j], in0=prm[:, j:j + 1], scalar=1.0 / 9.0, in1=prm[:, 3:4], op0=ALU.mult, op1=ALU.mult), [prmb], [prmb])
        P.op(P.dve, lambda h_: h_.tensor_scalar(out=prm[:, 7:8], in0=prm[:, 3:4], scalar1=1.0 / 9.0, scalar2=None, op0=ALU.mult), [prmb], [prmb])
        P.op(P.dve, lambda h_: h_.memset(zz[:], 0.0), [], [zzb])
        P.store([zzb], [(hd3_ap[:, L:L + 128], zz[:])])
        pi = 0
        for ti in range(L // 512):
            c0 = ti * 512
            k = ti % 2
            P.load(ztb[k], [(zt[:, k, :], zT_ap[:, c0:c0 + 512])])
            for lay in range(3):
                p = pi % 2; pi += 1
                if lay == 0:
                    P.op(P.pe, lambda h_: h_.matmul(ps_[p][:], w1[:], zt[:, k, :], start=True, stop=True), [wb_, ztb[k]], [psb[p]])
                else:
                    P.op(P.pe, lambda h_: h_.matmul(ps_[p][:], w2[:, lay - 1, :], h[:, lay - 1, :], start=True, stop=True), [wb_, hbf[lay - 1]], [psb[p]])
                P.op(P.act, lambda h_: h_.activation(out=h[:, lay, :], in_=ps_[p][:], func=AF.Sin, scale=prm[:, 7:8], bias=prm[:, 4 + lay:5 + lay]), [psb[p], prmb], [hbf[lay]])
                for rep in range(2):
                    q = rep
                    P.op(P.dve, lambda h_: h_.tensor_tensor(out=t[:, q, :], in0=h[:, lay, :], in1=h[:, lay, :], op=ALU.mult), [hbf[lay]], [tb[q]])
                    P.op(P.dve, lambda h_: h_.tensor_scalar(out=t[:, q, :], in0=t[:, q, :], scalar1=-4.0, scalar2=3.0, op0=ALU.mult, op1=ALU.add), [tb[q]], [tb[q]])
                    P.op(P.dve, lambda h_: h_.tensor_tensor(out=h[:, lay, :], in0=h[:, lay, :], in1=t[:, q, :], op=ALU.mult), [hbf[lay], tb[q]], [hbf[lay]])
            P.op(P.act, lambda h_: h_.activation(out=ho[:, k, :], in_=h[:, 2, :], func=AF.Copy), [hbf[2]], [hob[k]])
            P.store([hob[k]], [(hd3_ap[:, c0:c0 + 512], ho[:, k, :])])


def phase_hyprep(P, C, l, projT, tok0, sp, hv0, hx1, hx2):
    L = sp["L"]; nJ = sp["nJ"]; CB = sp["CB"]; nbg = 128 // CB
    with P.phase() as ph:
        xin = ph.sb("xin", [128, L + 2], F32); xinb = Buf()
        xc = ph.sb("xc", [128, L], F32); xcb = Buf()
        tk = ph.sb("tk", [128, 128, nJ], F32); tkb = Buf()
        tkh = ph.sb("tkh", [128, 128, nJ], BF16); tkhb = Buf()
        cw = ph.sb("cw", [128, 3, 8, 4], F32); cwb = Buf()
        tps = [ph.ps("tpf%d" % i, [128, 4, 128], F32) for i in range(2)]; tpsb = bufs(2)
        P.load(cwb, [(cw[:], C.cw[l])])
        P.op(P.dve, lambda h: h.memset(xin[:, 0:1], 0.0), [], [xinb])
        P.op(P.dve, lambda h: h.memset(xin[:, L + 1:L + 2], 0.0), [], [xinb])
        ti_ = 0
        for tn in range(3):
            for cg in range(8):
                row0 = tn * 1024 + cg * 128
                P.load(xinb, [(xin[:, 1 + c0:1 + c0 + 2048], projT[row0:row0 + 128, tok0 + c0:tok0 + c0 + 2048]) for c0 in range(0, L, 2048)])
                P.op(P.act, lambda h: h.activation(out=xc[:], in_=xin[:, 1:L + 1], func=AF.Identity, scale=cw[:, tn, cg, 1:2], bias=cw[:, tn, cg, 3:4]), [xinb, cwb], [xcb])
                P.op(P.dve, lambda h: h.scalar_tensor_tensor(out=xc[:], in0=xin[:, 0:L], scalar=cw[:, tn, cg, 0:1], in1=xc[:], op0=ALU.mult, op1=ALU.add), [xinb, cwb, xcb], [xcb])
                P.op(P.dve, lambda h: h.scalar_tensor_tensor(out=xc[:], in0=xin[:, 2:L + 2], scalar=cw[:, tn, cg, 2:3], in1=xc[:], op0=ALU.mult, op1=ALU.add), [xinb, cwb, xcb], [xcb])
                if tn == 2:
                    P.store([xcb], [(hx2[cg * 128:(cg + 1) * 128, c0:c0 + 2048], xc[:, c0:c0 + 2048]) for c0 in range(0, L, 2048)])
                    continue
                dst = tkh if tn == 0 else tk; dstb = tkhb if tn == 0 else tkb
                for J4 in range(nJ // 4):
                    k = ti_ % 2; ti_ += 1
                    for jj in range(4):
                        J = J4 * 4 + jj
                        P.op(P.pe, lambda h: h.transpose(out=tps[k][:, jj, :], in_=xc[:, J * 128:(J + 1) * 128], identity=C.identf[:]), [xcb], [tpsb[k]])
                    ov = dst[:, :, J4 * 4:J4 * 4 + 4].rearrange("p c j -> p j c")
                    if k == 0:
                        P.op(P.act, lambda h: h.activation(out=ov, in_=tps[k][:], func=AF.Copy), [tpsb[k]], [dstb])
                    else:
                        P.op(P.dve, lambda h: h.tensor_copy(out=ov, in_=tps[k][:]), [tpsb[k]], [dstb])
                tgt = hv0 if tn == 0 else hx1
                prs = []
                for b in range(nbg):
                    blk = cg * nbg + b
                    prs.append((tgt[blk], dst[:, b * CB:(b + 1) * CB, :]))
                P.store([dstb], prs)


def phase_hyena(P, C, l, sp, si, hd3_ap, hv0, hx1, hx2, mixT, tok0):
    L = sp["L"]; nJ = sp["nJ"]; MJ = sp["MJ"]; gB = sp["gB"]; gA = sp["gA"]; CB = sp["CB"]; nblk = sp["nblk"]
    NG = 32
    NA = 16
    with P.phase() as ph:
        hd3 = ph.sb("hd3", [64, L + 128], BF16); hd3b = Buf()
        w4 = ph.sb("w4", [64, 4096], BF16); w4b = Buf()
        fa = ph.sb("fa", [128, 2, 256], BF16); sbm = ph.sb("sbm", [128, 12, 128], BF16)
        Rm = ph.sb("Rm", [128, 3, 256], BF16); ga = ph.sb("ga", [128, 4, 128], BF16); cstb = Buf()
        dec = ph.sb("dec", [128, 2, nJ * CB], F32); decb = Buf()
        kfraw = ph.sb("kfraw", [128, nJ * 4 * CB], F32); kfb_ = Buf()
        kf = kfraw[:].rearrange("p (b d o c) -> p b d o c", b=nJ, d=2, o=2)
        Zs = kfraw[:, 0:2 * CB * nJ].bitcast(BF16).rearrange("p (q n) -> p q n", q=4)
        Zsb = kfb_
        kb = ph.sb("kb", [128, 2, 2, CB * nJ], BF16); kbb = Buf()
        nrm = ph.sb("nrm", [128, 2 * CB], F32); nrmb = Buf()
        As = ph.sb("As", [128, 2, 2, 256], BF16); Asb = [bufs(2) for _ in range(2)]
        Kh = ph.sb("Kh", [128, NG, 2, 128], BF16); Khb = Buf()
        fb = ph.sb("fb", [128, 2, NG], F32); fbb = Buf()
        X1 = ph.sb("X1", [128, CB * nJ], BF16); X1b = Buf()
        x1t = ph.sb("x1t", [128, CB * nJ], F32); x1tb = Buf()
        x2c = ph.sb("x2c", [CB, L], F32); x2cb = Buf()
        Pp = ph.sb("Pp", [128, 2, 4, 128], BF16); Ppb = bufs(2)
        psA = [ph.ps("psA%d" % i, [128, 512], F32) for i in range(2)]; psAb = bufs(2)
        psB = [ph.ps("psB%d" % i, [128, 512], F32) for i in range(2)]; psBb = bufs(2)
        psZ = [ph.ps("psZ%d" % i, [128, 512], F32) for i in range(2)]; psZb = bufs(2)
        psY = [ph.ps("psY%d" % i, [128, 512], F32) for i in range(2)]; psYb = bufs(2)
        P.load(hd3b, [(hd3[:], hd3_ap)])
        P.load(w4b, [(w4[:], C.f_w4[l])], q=P.pool)
        P.load(cstb, [(fa[:], C.dft_fa), (sbm[:], C.dft_sb[si]), (Rm[:], C.dft_R[si]), (ga[:], C.dft_ga[si])])
        cnt = dict(a=0, b=0, z=0, y=0, ev=0, p=0)

        def evac(out_ap, in_ap, rd, wr):
            cnt["ev"] += 1
            if cnt["ev"] % 2 == 0:
                P.op(P.act, lambda h: h.activation(out=out_ap, in_=in_ap, func=AF.Copy), rd, wr)
            else:
                P.op(P.dve, lambda h: h.tensor_copy(out=out_ap, in_=in_ap), rd, wr)

        for blk in range(nblk):
            ch0 = blk * CB
            P.load(decb, [(dec[:, 0, :], C.dec[si][0, :, blk].rearrange("p b c -> p (b c)")), (dec[:, 1, :], C.dec[si][1, :, blk].rearrange("p b c -> p (b c)"))])
            P.load(fbb, [(fb[:, 0, :], C.fbias[si][l, 0, :, blk * NG:(blk + 1) * NG]), (fb[:, 1, :], C.fbias[si][l, 1, :, blk * NG:(blk + 1) * NG])])
            for B in range(nJ):
                for d in range(2):
                    a = cnt["a"] % 2; cnt["a"] += 1
                    for o in range(2):
                        col0 = o * 2048 + d * 1024 + ch0
                        P.op(P.pe, lambda h: h.matmul(psA[a][:, o * CB:(o + 1) * CB], hd3[:, B * 128 + d:B * 128 + d + 128], w4[:, col0:col0 + CB], start=True, stop=True), [hd3b, w4b], [psAb[a]])
                    dv = dec[:, d, B * CB:(B + 1) * CB].unsqueeze(1).to_broadcast([128, 2, CB])
                    P.op(P.dve, lambda h: h.tensor_tensor(out=kf[:, B, d, :, :], in0=psA[a][:, 0:2 * CB].rearrange("p (o c) -> p o c", o=2), in1=dv, op=ALU.mult), [psAb[a], decb], [kfb_])
            kview = kf.rearrange("p b d o c -> p (o c) (b d)")
            P.op(P.dve, lambda h: h.tensor_reduce(out=nrm[:], in_=kview, axis=AX.X, op=ALU.add, apply_absolute_value=True), [kfb_], [nrmb])
            a = cnt["a"] % 2; cnt["a"] += 1
            P.op(P.pe, lambda h: h.matmul(psA[a][:, 0:2 * CB], C.onesf[:], nrm[:], start=True, stop=True), [nrmb], [psAb[a]])
            P.op(P.dve, lambda h: h.reciprocal(out=nrm[:], in_=psA[a][:, 0:2 * CB]), [psAb[a]], [nrmb])
            for o in range(2):
                for d in range(2):
                    iv = kf[:, :, d, o, :].rearrange("p b c -> p c b")
                    nv = nrm[:, o * CB:(o + 1) * CB].unsqueeze(2).to_broadcast([128, CB, nJ])
                    P.op(P.dve, lambda h: h.tensor_tensor(out=kb[:, o, d, :].rearrange("p (c b) -> p c b", c=CB), in0=iv, in1=nv, op=ALU.mult), [kfb_, nrmb], [kbb])
            P.load(X1b, [(X1[:], hv0[blk].rearrange("p c j -> p (c j)"))])
            P.load(x1tb, [(x1t[:], hx1[blk].rearrange("p c j -> p (c j)"))])
            P.load(x2cb, [(x2c[:, c0:c0 + 2048], hx2[ch0:ch0 + CB, c0:c0 + 2048]) for c0 in range(0, L, 2048)])
            for o in range(2):
                for ag in range(NA):
                    pp = ag % 2
                    for d in range(2):
                        a = cnt["a"] % 2; cnt["a"] += 1
                        P.op(P.pe, lambda h: h.matmul(psA[a][:, 0:256], kb[:, o, d, ag * 128:(ag + 1) * 128], fa[:, d, :], start=True, stop=True), [kbb, cstb], [psAb[a]])
                        evac(As[:, d, pp, :], psA[a][:, 0:256], [psAb[a]], [Asb[d][pp]])
                    for hf in range(2):
                        g = ag * 2 + hf
                        b_ = cnt["b"] % 2; cnt["b"] += 1
                        seq = []
                        for d in range(2):
                            seq.append((d, 0, 0)); seq.append((d, 2, 1))
                        for n, (d, m, pl) in enumerate(seq):
                            P.op(P.pe, lambda h: h.matmul(psB[b_][:, 0:128], sbm[:, (d * 2 + hf) * 3 + m, :], As[:, d, pp, pl * 128:(pl + 1) * 128], start=(n == 0), stop=(n == 3)), [cstb, Asb[d][pp]], [psBb[b_]])
                        seq = []
                        for d in range(2):
                            seq.append((d, 1, 0)); seq.append((d, 0, 1))
                        for n, (d, m, pl) in enumerate(seq):
                            P.op(P.pe, lambda h: h.matmul(psB[b_][:, 128:256], sbm[:, (d * 2 + hf) * 3 + m, :], As[:, d, pp, pl * 128:(pl + 1) * 128], start=(n == 0), stop=(n == 3)), [cstb, Asb[d][pp]], [psBb[b_]])
                        P.op(P.act, lambda h: h.activation(out=Kh[:, g, 0, :], in_=psB[b_][:, 0:128], func=AF.Identity, bias=fb[:, o, g:g + 1]), [psBb[b_], fbb], [Khb])
                        P.op(P.dve, lambda h: h.tensor_copy(out=Kh[:, g, 1, :], in_=psB[b_][:, 128:256]), [psBb[b_]], [Khb])
                for ag in range(NA):
                    pp = ag % 2
                    a = cnt["a"] % 2; cnt["a"] += 1
                    P.op(P.pe, lambda h: h.matmul(psA[a][:, 0:256], X1[:, ag * 128:(ag + 1) * 128], fa[:, 0, :], start=True, stop=True), [X1b, cstb], [psAb[a]])
                    evac(As[:, 0, pp, :], psA[a][:, 0:256], [psAb[a]], [Asb[0][pp]])
                    for hf in range(2):
                        g = ag * 2 + hf
                        b_ = cnt["b"] % 2; cnt["b"] += 1
                        for n, (m, pl) in enumerate([(0, 0), (2, 1)]):
                            P.op(P.pe, lambda h: h.matmul(psB[b_][:, 0:128], sbm[:, hf * 3 + m, :], As[:, 0, pp, pl * 128:(pl + 1) * 128], start=(n == 0), stop=(n == 1)), [cstb, Asb[0][pp]], [psBb[b_]])
                        for n, (m, pl) in enumerate([(1, 0), (0, 1)]):
                            P.op(P.pe, lambda h: h.matmul(psB[b_][:, 128:256], sbm[:, hf * 3 + m, :], As[:, 0, pp, pl * 128:(pl + 1) * 128], start=(n == 0), stop=(n == 1)), [cstb, Asb[0][pp]], [psBb[b_]])
                        p_ = cnt["p"] % 2; cnt["p"] += 1
                        xin = psB[b_][:, 0:256].rearrange("p (a f) -> p a f", a=2).unsqueeze(1).to_broadcast([128, 2, 2, 128])
                        kin = Kh[:, g, :, :].unsqueeze(2).to_broadcast([128, 2, 2, 128])
                        P.op(P.dve, lambda h: h.tensor_tensor(out=Pp[:, p_, :, :].rearrange("p (k a) f -> p k a f", k=2), in0=xin, in1=kin, op=ALU.mult), [psBb[b_], Khb], [Ppb[p_]])
                        z = cnt["z"] % 2; cnt["z"] += 1
                        for n, (pl, rm) in enumerate([(0, 0), (1, 1), (2, 1), (3, 2)]):
                            P.op(P.pe, lambda h: h.matmul(psZ[z][:, 0:256], Pp[:, p_, pl, :], Rm[:, rm, :], start=(n == 0), stop=(n == 3)), [Ppb[p_], cstb], [psZb[z]])
                        zin = psZ[z][:, 0:256].rearrange("p (q c i) -> p q c i", q=4, c=gB)
                        if o == 0:
                            zo = Zs[:, :, g * gB * nJ:(g + 1) * gB * nJ].rearrange("p q (c i) -> p q c i", c=gB)
                        else:
                            zo = Zs.rearrange("p q (i c) -> p q c i", c=CB)[:, :, g * gB:(g + 1) * gB, :]
                        evac(zo, zin, [psZb[z]], [Zsb])
                if o == 0:
                    for ch in range(CB * nJ // 512):
                        y = cnt["y"] % 2; cnt["y"] += 1
                        for q in range(4):
                            gi = [0, 2, 1, 3][q]
                            P.op(P.pe, lambda h: h.matmul(psY[y][:], ga[:, gi, :], Zs[:, q, ch * 512:(ch + 1) * 512], start=(q == 0), stop=(q == 3)), [cstb, Zsb], [psYb[y]])
                        P.op(P.dve, lambda h: h.tensor_tensor(out=X1[:, ch * 512:(ch + 1) * 512], in0=psY[y][:], in1=x1t[:, ch * 512:(ch + 1) * 512], op=ALU.mult), [psYb[y], x1tb], [X1b])
                else:
                    for I4 in range(nJ // 4):
                        y = cnt["y"] % 2; cnt["y"] += 1
                        for ii in range(4):
                            I = I4 * 4 + ii
                            for q in range(4):
                                gi = [0, 2, 1, 3][q]
                                P.op(P.pe, lambda h: h.matmul(psY[y][0:CB, ii * 128:(ii + 1) * 128], Zs[:, q, I * CB:(I + 1) * CB], ga[:, gi, :], start=(q == 0), stop=(q == 3)), [Zsb, cstb], [psYb[y]])
                        P.op(P.dve, lambda h: h.tensor_tensor(out=x2c[:, I4 * 512:(I4 + 1) * 512], in0=psY[y][0:CB, :], in1=x2c[:, I4 * 512:(I4 + 1) * 512], op=ALU.mult), [psYb[y], x2cb], [x2cb])
            P.store([x2cb], [(mixT[ch0:ch0 + CB, tok0 + c0:tok0 + c0 + 2048], x2c[:, c0:c0 + 2048]) for c0 in range(0, L, 2048)])


def build_program(depth=DEPTH, seqs=(("p", LP), ("s", LS)), debug=False, stop_after=None):
    P = Prog(); nc = P.nc; C = Ctx()
    C.stg_i = 0; C.tp_i = 0
    IN = lambda name, shape, dt=F32: P.dram(name, shape, dt, "ExternalInput")
    dbg = "ExternalOutput" if debug else "Internal"
    xin = {"p": IN("x_p", [LP, D]), "s": IN("x_s", [LS, D])}
    mem = {"p": IN("mem_p", [NMEM, D]), "s": IN("mem_s", [NMEM, D])}
    w_in = IN("w_in", [depth, D, DIN]); w_out = IN("w_out", [depth, D, D]); w_up = IN("w_up", [depth, D, 2 * DFF])
    w_down = IN("w_down", [depth, DFF, D]); w_kv = IN("w_kv", [depth, D, 1024])
    C.f_w1 = IN("f_w1", [DEPTH, 33, 64]); C.f_w2 = IN("f_w2", [DEPTH, 64, 64]); C.f_w3 = IN("f_w3", [DEPTH, 64, 64])
    C.f_w4 = IN("f_w4", [DEPTH, 64, 4096]); C.fprm = IN("fprm", [DEPTH, 64, 4])
    gcols = IN("gcols", [DEPTH, 4, 128, 16])
    gvecs = IN("gvecs", [DEPTH, 2, D])
    C.cw = IN("cw", [DEPTH, 128, 3, 8, 4]); fcw = IN("fcw", [DEPTH, 128, 44, 4])
    identb_d = IN("identb", [128, 128], BF16); identf_d = IN("identf", [128, 128]); onesb_d = IN("onesb", [128, 128], BF16)
    onesf_d = IN("onesf", [128, 128]); mask_d = IN("maskt", [128, 20, 512], BF16)
    C.rope_cs = IN("rope_cs", [2, 32, LP]); ropepm_d = IN("ropepm", [32, 32])
    C.dft_fa = IN("dft_fa", [2, 128, 256], BF16).rearrange("d p f -> p d f")
    dft_sb = IN("dft_sb", [2, 12, 128, 128], BF16); dft_R = IN("dft_R", [2, 3, 128, 256], BF16); dft_ga = IN("dft_ga", [2, 4, 128, 128], BF16)
    C.dft_sb = [dft_sb[i].rearrange("m p f -> p m f") for i in range(2)]
    C.dft_R = [dft_R[i].rearrange("m p f -> p m f") for i in range(2)]
    C.dft_ga = [dft_ga[i].rearrange("m p f -> p m f") for i in range(2)]
    spS = seq_params(LS); spP = seq_params(LP)
    SPS = {"p": spP, "s": spS}; SI = {"p": 0, "s": 1}
    C.dec = [IN("dec_p", [2, 128, spP["nblk"], spP["nJ"], spP["CB"]]), IN("dec_s", [2, 128, spS["nblk"], spS["nJ"], spS["CB"]])]
    C.fbias = [IN("fbias_p", [DEPTH, 2, 128, 32 * spP["nblk"]]), IN("fbias_s", [DEPTH, 2, 128, 32 * spS["nblk"]])]
    zT = {"p": IN("zT_p", [33, LP]), "s": IN("zT_s", [33, LS])}
    yout = {"p": P.dram("y_p", [LP, D], F32, "ExternalOutput"), "s": P.dram("y_s", [LS, D], F32, "ExternalOutput")}
    NT = sum(L for _, L in seqs)
    TOK0 = {}; t = 0
    for nm, L in seqs:
        TOK0[nm] = t; t += L
    xa = P.dram("xa", [NT, D], F32, dbg); xb = P.dram("xb", [NT, D], F32)
    projT = P.dram("projT", [DIN, NT], F32, dbg); mixT = P.dram("mixT", [D, NT], F32, dbg)
    mixn = P.dram("mixn", [D, NT], BF16); ffT = P.dram("ffT", [DFF, NT], BF16)
    kvT = {nm: P.dram("kvT_" + nm, [1024, NMEM], F32) for nm, _ in seqs}
    hd3 = {nm: P.dram("hd3_" + nm, [64, L + 128], BF16) for nm, L in seqs}
    hv0 = {nm: P.dram("hv0_" + nm, [SPS[nm]["nblk"], 128, SPS[nm]["CB"], SPS[nm]["nJ"]], BF16) for nm, _ in seqs}
    hx1 = {nm: P.dram("hx1_" + nm, [SPS[nm]["nblk"], 128, SPS[nm]["CB"], SPS[nm]["nJ"]], F32) for nm, _ in seqs}
    hx2 = {nm: P.dram("hx2_" + nm, [DH, L], F32) for nm, L in seqs}
    wb_in = P.dram("wb_in", [depth, 10, 128, 16 * 512], BF16); wb_out = P.dram("wb_out", [depth, 4, 128, 16 * 512], BF16)
    wb_up = P.dram("wb_up", [depth, 22, 128, 2 * 16 * 256], BF16); wb_down = P.dram("wb_down", [depth, 8, 128, 44 * 256], BF16)
    wb_kv = P.dram("wb_kv", [depth, 2, 128, 16 * 512], BF16)
    cst = P.es
    C.identb = cst.enter_context(nc.sbuf_tensor("identb_s", [128, 128], BF16)); C.identf = cst.enter_context(nc.sbuf_tensor("identf_s", [128, 128], F32))
    C.onesb = cst.enter_context(nc.sbuf_tensor("onesb_s", [128, 128], BF16)); C.onesf = cst.enter_context(nc.sbuf_tensor("onesf_s", [128, 128], F32))
    C.mask_d = mask_d; C.ropepm = cst.enter_context(nc.sbuf_tensor("ropepm_s", [32, 32], F32))
    cb_ = Buf()
    P.load(cb_, [(C.identb[:], identb_d), (C.identf[:], identf_d), (C.onesb[:], onesb_d), (C.onesf[:], onesf_d), (C.ropepm[:], ropepm_d)])
    P.barrier()
    for l in range(depth):
        for cbk in range(10):
            P.dram_dma(wb_in[l, cbk].rearrange("p (c f) -> p c f", c=16), w_in[l].rearrange("(c p) f -> p c f", p=128)[:, :, cbk * 512:(cbk + 1) * 512])
        for cbk in range(4):
            P.dram_dma(wb_out[l, cbk].rearrange("p (c f) -> p c f", c=16), w_out[l].rearrange("(c p) f -> p c f", p=128)[:, :, cbk * 512:(cbk + 1) * 512])
        for cbk in range(2):
            P.dram_dma(wb_kv[l, cbk].rearrange("p (c f) -> p c f", c=16), w_kv[l].rearrange("(c p) f -> p c f", p=128)[:, :, cbk * 512:(cbk + 1) * 512])
        for fg in range(22):
            for gv in range(2):
                P.dram_dma(wb_up[l, fg].rearrange("p (g c f) -> p g c f", g=2, c=16)[:, gv], w_up[l].rearrange("(c p) f -> p c f", p=128)[:, :, gv * DFF + fg * 256:gv * DFF + (fg + 1) * 256])
        for cbk in range(8):
            P.dram_dma(wb_down[l, cbk].rearrange("p (k f) -> p k f", k=44), w_down[l].rearrange("(k p) f -> p k f", p=128)[:, :, cbk * 256:(cbk + 1) * 256])
    P.barrier()
    stages = []
    for l in range(depth):
        last = (l == depth - 1)
        for nm, L in seqs:
            sp = SPS[nm]; si = SI[nm]; tok0 = TOK0[nm]
            x0 = xin[nm] if l == 0 else xb[tok0:tok0 + L, :]
            x1 = xa[tok0:tok0 + L, :]
            x2 = yout[nm] if last else xb[tok0:tok0 + L, :]
            def stop(tag):
                return stop_after is not None and stop_after == tag
            phase_up(P, C, x0, L, gcols[l, 0], wb_in[l], 10, 512, "proj", out_ap=projT, tok0=tok0)
            if stop("proj"): break
            phase_up(P, C, mem[nm], NMEM, gcols[l, 1], wb_kv[l], 2, 512, "proj", out_ap=kvT[nm], tok0=0)
            phase_attn(P, C, projT, mixT, tok0, L, kvT=kvT[nm])
            if stop("memx"): break
            phase_attn(P, C, projT, mixT, tok0, L)
            if stop("attn"): break
            phase_filter_mlp(P, C, l, sp, zT[nm], hd3[nm])
            phase_hyprep(P, C, l, projT, tok0, sp, hv0[nm], hx1[nm], hx2[nm])
            phase_hyena(P, C, l, sp, si, hd3[nm], hv0[nm], hx1[nm], hx2[nm], mixT, tok0)
            if stop("hyena"): break
            phase_mixprep(P, C, mixT, tok0, L, gcols[l, 2], mixn)
            phase_down(P, C, mixn, tok0, L, wb_out[l], 4, 16, 512, gvecs[l, 0], x0, x1)
            if stop("mix"): break
            phase_up(P, C, x1, L, gcols[l, 3], wb_up[l], 22, 256, "ffn", out_ap=ffT, tok0=tok0, ffn=dict(fcw=fcw[l]), seq_edges=True)
            phase_down(P, C, ffT, tok0, L, wb_down[l], 8, 44, 256, gvecs[l, 1], x1, x2)
    P.barrier()
    return P


def make_consts():
    c = {}
    c["identb"] = np.eye(128).astype(NPBF); c["identf"] = np.eye(128, dtype=np.float32)
    c["onesb"] = np.ones((128, 128)).astype(NPBF); c["onesf"] = np.ones((128, 128), np.float32)
    c["maskt"] = mask_tiles().astype(NPBF)
    cs, pm = rope_tables(); c["rope_cs"] = cs; c["ropepm"] = pm
    tp_ = dft_tables(64); ts_ = dft_tables(16)
    c["dft_fa"] = tp_[0].astype(NPBF)
    c["dft_sb"] = np.stack([tp_[1], ts_[1]]).astype(NPBF)
    c["dft_R"] = np.stack([tp_[2], ts_[2]]).astype(NPBF)
    c["dft_ga"] = np.stack([tp_[3], ts_[3]]).astype(NPBF)
    c["dec_p"] = decay_tables(LP); c["dec_s"] = decay_tables(LS)
    c["zT_p"] = zfeat_T(LP); c["zT_s"] = zfeat_T(LS)
    return c


def col_layout(g):
    return np.ascontiguousarray(g.reshape(g.shape[0], -1, 128).transpose(0, 2, 1))


def host_layouts(inp):
    o = {}
    o["gcols"] = np.ascontiguousarray(np.stack([col_layout(inp["g_pre_mix"]), col_layout(inp["g_mem"]), col_layout(inp["g_grp"]), col_layout(inp["g_pre_ffn"])], 1))
    o["gvecs"] = np.ascontiguousarray(np.stack([inp["g_post_mix"], inp["g_post_ffn"]], 1))
    cwb = np.concatenate([inp["conv_w"], inp["conv_b"][:, None, :]], 1)
    o["cw"] = np.ascontiguousarray(cwb.reshape(DEPTH, 4, 3, 8, 128).transpose(0, 4, 2, 3, 1))
    fc = np.concatenate([inp["ffn_conv_w"], inp["ffn_conv_b"][:, None, :]], 1)
    o["fcw"] = np.ascontiguousarray(fc.reshape(DEPTH, 4, 44, 128).transpose(0, 3, 2, 1))
    o["fprm"] = np.ascontiguousarray(np.stack([inp["f_b1"], inp["f_b2"], inp["f_b3"], inp["f_freq"]], -1))
    for nm, L in (("p", LP), ("s", LS)):
        sp = seq_params(L)
        ng = 32 * sp["nblk"]
        ch = (np.arange(ng)[None, :] * sp["gB"] + (np.arange(128)[:, None] // sp["MJ"]))
        o["fbias_" + nm] = np.ascontiguousarray(inp["f_bias"][:, :, ch])
    for k in ("w_in", "w_out", "w_up", "w_down", "f_w1", "f_w2", "f_w3", "f_w4"):
        o[k] = inp[k]
    o["w_kv"] = inp["w_mem_kv"]
    return o


_CACHE = {}


def kernel(**inputs):
    inp = {k: np.asarray(v) for k, v in inputs.items()}
    if "prog" not in _CACHE:
        _CACHE["prog"] = build_program()
        _CACHE["consts"] = make_consts()
    P = _CACHE["prog"]
    shared = dict(_CACHE["consts"]); shared.update(host_layouts(inp))
    shared["x_p"] = np.ascontiguousarray(inp["x_prompt"][0]); shared["mem_p"] = np.ascontiguousarray(inp["mem_prompt"][0])
    in_maps = []
    for c in range(NCORES):
        m = dict(shared)
        m["x_s"] = np.ascontiguousarray(inp["x_sample"][c]); m["mem_s"] = np.ascontiguousarray(inp["mem_sample"][c])
        in_maps.append(m)
    res = run_bass_kernel_spmd(P.nc, in_maps, core_ids=list(range(NCORES)))
    yp = np.asarray(res.results[0]["y_p"], dtype=np.float32)[None]
    ys = np.stack([np.asarray(res.results[c]["y_s"], dtype=np.float32) for c in range(NCORES)], 0)
    return (yp, ys)
```

```python
import math
from contextlib import ExitStack
import numpy as np
import ml_dtypes
import concourse.bass as bass
import concourse.mybir as mybir
from concourse.bass_utils import run_bass_kernel_spmd

F32 = mybir.dt.float32
BF16 = mybir.dt.bfloat16
AF = mybir.ActivationFunctionType
ALU = mybir.AluOpType
AX = mybir.AxisListType
NPBF = ml_dtypes.bfloat16

D = 2048; DIN = 5120; DH = 1024; DFF = 5632; NMEM = 256; EPS = 1e-6
DEPTH = 4
LP = 8192; LS = 2048
NCORES = 8


class Sem:
    def __init__(self, h, idx):
        self.h = h; self.idx = idx; self.cnt = 0


class Eng:
    def __init__(self, P, name, h, same_wait):
        self.P = P; self.name = name; self.h = h; self.same_wait = same_wait
        self.sem = P.new_sem("e_" + name)
        self.seen = {}

    def wait(self, tok):
        sem, val = tok
        if sem is self.sem and not self.same_wait:
            return
        if self.seen.get(sem.idx, 0) >= val:
            return
        self.h.wait_ge(sem.h, val)
        self.seen[sem.idx] = val


class Buf:
    def __init__(self):
        self.w = {}; self.r = {}; self.dsem = None


def _merge(d, toks):
    for k, t in toks.items():
        if k not in d or d[k][1] < t[1]:
            d[k] = t


class Prog:
    def __init__(self):
        self.nc = bass.Bass("TRN2", target_bir_lowering=False)
        self.es = ExitStack()
        self.sems = []
        nc = self.nc
        self.pe = Eng(self, "pe", nc.tensor, False)
        self.act = Eng(self, "act", nc.scalar, True)
        self.dve = Eng(self, "dve", nc.vector, True)
        self.pool = Eng(self, "pool", nc.gpsimd, True)
        self.sp = Eng(self, "sp", nc.sync, False)
        self.engs = [self.pe, self.act, self.dve, self.pool, self.sp]
        self.free_dsems = []
        self.dsems = []
        self.phase_bufs = []
        self.uid = 0
        self.ddsem = self.new_sem("dd")
        self.ninst = 0

    def new_sem(self, name):
        h = self.es.enter_context(self.nc.semaphore(name))
        s = Sem(h, len(self.sems)); self.sems.append(s)
        return s

    def dram(self, name, shape, dt, kind="Internal"):
        return self.nc.dram_tensor(name, list(shape), dt, kind=kind).ap()

    def op(self, eng, fn, reads=(), writes=()):
        deps = {}
        for b in reads:
            _merge(deps, b.w)
        for b in writes:
            _merge(deps, b.w); _merge(deps, b.r)
        for t in deps.values():
            eng.wait(t)
        ins = fn(eng.h)
        eng.sem.cnt += 1
        ins.then_inc(eng.sem.h, 1)
        tok = (eng.sem, eng.sem.cnt)
        for b in reads:
            b.r[eng.sem.idx] = tok
        for b in writes:
            b.w = {eng.sem.idx: tok}; b.r = {}
        self.ninst += 1
        return tok

    def pe_group(self, fns, reads=(), writes=()):
        eng = self.pe
        deps = {}
        for b in reads:
            _merge(deps, b.w)
        for b in writes:
            _merge(deps, b.w); _merge(deps, b.r)
        for t in deps.values():
            eng.wait(t)
        for fn in fns[:-1]:
            fn(eng.h); self.ninst += 1
        ins = fns[-1](eng.h)
        eng.sem.cnt += 1
        ins.then_inc(eng.sem.h, 1)
        tok = (eng.sem, eng.sem.cnt)
        for b in reads:
            b.r[eng.sem.idx] = tok
        for b in writes:
            b.w = {eng.sem.idx: tok}; b.r = {}
        self.ninst += 1
        return tok

    def _dsem(self, b):
        if b.dsem is None:
            if self.free_dsems:
                b.dsem = self.free_dsems.pop()
            else:
                b.dsem = self.new_sem("d%d" % len(self.dsems))
                self.dsems.append(b.dsem)
            self.phase_bufs.append(b)
        return b.dsem

    def load(self, buf, pairs, q=None, extra_reads=()):
        q = q or self.sp
        s = self._dsem(buf)
        deps = {}
        _merge(deps, buf.w); _merge(deps, buf.r)
        for b in extra_reads:
            _merge(deps, b.w)
        for t in deps.values():
            q.wait(t)
        for o, i in pairs:
            q.h.dma_start(out=o, in_=i).then_inc(s.h, 16)
            s.cnt += 16
            self.ninst += 1
        buf.w = {s.idx: (s, s.cnt)}; buf.r = {}

    def store(self, bufs, pairs, q=None):
        q = q or self.pool
        s = self._dsem(bufs[0])
        deps = {}
        for b in bufs:
            _merge(deps, b.w)
        for t in deps.values():
            q.wait(t)
        for o, i in pairs:
            q.h.dma_start(out=o, in_=i).then_inc(s.h, 16)
            s.cnt += 16
            self.ninst += 1
        for b in bufs:
            b.r[s.idx] = (s, s.cnt)

    def dram_dma(self, o, i, q=None):
        q = q or self.pool
        q.h.dma_start(out=o, in_=i).then_inc(self.ddsem.h, 16)
        self.ddsem.cnt += 16

    def barrier(self):
        toks = [(e.sem, e.sem.cnt) for e in self.engs]
        toks += [(s, s.cnt) for s in self.dsems]
        toks.append((self.ddsem, self.ddsem.cnt))
        for e in self.engs:
            for t in toks:
                if t[1] > 0:
                    e.wait(t)

    class _Phase:
        def __init__(self, P):
            self.P = P; self.es = ExitStack()

        def __enter__(self):
            self.es.__enter__(); return self

        def sb(self, name, shape, dt):
            self.P.uid += 1
            return self.es.enter_context(self.P.nc.sbuf_tensor("%s_%d" % (name, self.P.uid), list(shape), dt))

        def ps(self, name, shape, dt):
            self.P.uid += 1
            return self.es.enter_context(self.P.nc.psum_tensor("%s_%d" % (name, self.P.uid), list(shape), dt))

        def __exit__(self, *a):
            P = self.P
            P.barrier()
            for b in P.phase_bufs:
                P.free_dsems.append(b.dsem); b.dsem = None
            P.phase_bufs = []
            return self.es.__exit__(*a)

    def phase(self):
        return Prog._Phase(self)


def mm(out, lhsT, rhs, st, sp_):
    return lambda h: h.matmul(out, lhsT, rhs, start=st, stop=sp_)


def tpz(out, in_, ident):
    return lambda h: h.transpose(out=out, in_=in_, identity=ident)


def bufs(n):
    return [Buf() for _ in range(n)]


def seq_params(L):
    nJ = L // 128
    return dict(L=L, nJ=nJ, MJ=2 * nJ, gB=128 // (2 * nJ), gA=128 // nJ, CB=32 * (128 // (2 * nJ)), nblk=DH // (32 * (128 // (2 * nJ))))


def dft_tables(nJ):
    MJ = 2 * nJ; gB = 128 // MJ
    a = np.arange(128)[:, None]; f = np.arange(128)[None, :]
    thj = -2 * np.pi / 255
    Ffw = np.exp(1j * thj * (a * f)); Fbw = np.exp(1j * thj * ((127 - a) * f))
    fa = np.stack([np.concatenate([Ffw.real, Ffw.imag], 1), np.concatenate([Fbw.real, Fbw.imag], 1)])
    B = np.arange(nJ)[:, None]; fJ = np.arange(MJ)[None, :]
    thJ = -2 * np.pi / MJ
    Jfw = np.exp(1j * thJ * (B * fJ)); Jbw = np.exp(1j * thJ * (-(B + 1) * fJ))
    sb = np.zeros((2, 2, 3, 128, 128))
    for d, Jm in enumerate((Jfw, Jbw)):
        for h in range(2):
            for c in range(gB):
                r0 = 64 * h + c * nJ; m0 = c * MJ
                sb[d, h, 0, r0:r0 + nJ, m0:m0 + MJ] = Jm.real
                sb[d, h, 1, r0:r0 + nJ, m0:m0 + MJ] = Jm.imag
                sb[d, h, 2, r0:r0 + nJ, m0:m0 + MJ] = -Jm.imag
    I = np.arange(nJ)[None, :]; fJc = np.arange(MJ)[:, None]
    GJlo = np.exp(1j * thJ * (-(I * fJc))); GJhi = np.exp(1j * thJ * (-((I - 1) * fJc)))
    Gre = np.zeros((128, 2, gB * nJ)); Gim = np.zeros((128, 2, gB * nJ))
    for c in range(gB):
        for k, Gm in enumerate((GJlo, GJhi)):
            Gre[c * MJ:(c + 1) * MJ, k, c * nJ:(c + 1) * nJ] = Gm.real
            Gim[c * MJ:(c + 1) * MJ, k, c * nJ:(c + 1) * nJ] = Gm.imag
    Gre = Gre.reshape(128, 128); Gim = Gim.reshape(128, 128)
    R = np.stack([np.concatenate([Gre, Gim], 1), np.concatenate([-Gim, Gre], 1), np.concatenate([-Gre, -Gim], 1)])
    cf = np.where(np.arange(128) == 0, 1.0, 2.0)[:, None] / (MJ * 255.0)
    e = np.arange(255)[None, :]; ff = np.arange(128)[:, None]
    th = 2 * np.pi * e * ff / 255
    gre = cf * np.cos(th); gim = -cf * np.sin(th)
    z1 = np.zeros((128, 1))
    ga = np.stack([gre[:, :128], gim[:, :128], np.concatenate([gre[:, 128:], z1], 1), np.concatenate([gim[:, 128:], z1], 1)])
    return fa, sb.reshape(12, 128, 128), R, ga


def decay_tables(L):
    sp = seq_params(L); nJ = sp["nJ"]; CB = sp["CB"]; nblk = sp["nblk"]
    t = np.linspace(0.0, 1.0, L, dtype=np.float32)
    deltas = np.abs(np.linspace(math.log(1e-2) / 1.5, math.log(1e-2) / 0.3, DH, dtype=np.float32))
    dec = (np.exp(-t[:, None] * deltas[None, :]) + np.float32(0.05)).astype(np.float32)
    decb = np.concatenate([dec[1:], np.zeros((1, DH), np.float32)], 0)
    out = []
    for dd in (dec, decb):
        x = dd.reshape(nJ, 128, nblk, CB).transpose(1, 2, 0, 3)
        out.append(np.ascontiguousarray(x))
    return np.stack(out)


def zfeat_T(L):
    t = np.linspace(0.0, 1.0, L, dtype=np.float32)[:, None]
    w = (np.float32(2.0 * math.pi) * np.arange(L, dtype=np.float32) / np.float32(L)).astype(np.float32)
    f = np.linspace(1e-4, 15, 16, dtype=np.float32)
    ang = (w[:, None] * f[None, :]).astype(np.float32)
    z = np.concatenate([t, np.cos(ang), -np.sin(ang)], -1).astype(np.float32)
    return np.ascontiguousarray(z.T)


def rope_tables():
    inv = np.exp(np.float32(-math.log(500000.0)) * np.arange(0, 32, 2, dtype=np.float32) / np.float32(32)).astype(np.float32)
    ang = (np.arange(LP, dtype=np.float32)[:, None] * inv[None, :]).astype(np.float32)
    c = np.cos(ang).T; s = np.sin(ang).T
    C = np.concatenate([c, c], 0); S = np.concatenate([s, s], 0)
    pm = np.zeros((32, 32), np.float32)
    for m in range(16):
        pm[m + 16, m] = -1.0; pm[m, m + 16] = 1.0
    return np.ascontiguousarray(np.stack([C, S]).astype(np.float32)), pm


def mask_tiles():
    out = np.zeros((128, 20, 512), np.float32)
    tk = np.arange(128)[:, None]; tq = np.arange(512)[None, :]
    for o in range(20):
        dlt = (-1024 + 128 * o + tk) - tq
        m = np.zeros_like(dlt, dtype=np.float32)
        for dil in (1, 4, 16):
            m += ((dlt % dil == 0) & (np.abs(dlt) <= 64 * dil)).astype(np.float32)
        out[:, o, :] = m
    return out


class Ctx:
    pass


def rstd_from_ss(P, ss_ap, ss_buf, n):
    P.op(P.dve, lambda h: h.tensor_scalar(out=ss_ap, in0=ss_ap, scalar1=1.0 / n, scalar2=EPS, op0=ALU.mult, op1=ALU.add), [ss_buf], [ss_buf])
    P.op(P.act, lambda h: h.activation(out=ss_ap, in_=ss_ap, func=AF.Sqrt), [ss_buf], [ss_buf])
    P.op(P.dve, lambda h: h.reciprocal(out=ss_ap, in_=ss_ap), [ss_buf], [ss_buf])


def phase_up(P, C, x_ap, ntok, gcol_ap, wblk_ap, nblk, BW, kind, out_ap=None, tok0=0, ffn=None, seq_edges=None):
    TT = min(512, ntok); ns = TT // 128
    if kind == "ffn":
        starts = list(range(0, ntok - TT, TT - 2)) + [ntok - TT]
    else:
        starts = list(range(0, ntok, TT))
    ntiles = len(starts)
    with P.phase() as ph:
        xt = ph.sb("xt", [128, ns, D], F32); xb = bufs(ns)
        sq = ph.sb("sq", [128, D], BF16); sqb = Buf()
        ss = ph.sb("ss", [128, ns], F32); ssb = bufs(ns)
        hb = ph.sb("hb", [128, ns, D], BF16); hbb = bufs(ns)
        hT = ph.sb("hT", [128, 16, TT], BF16); hTb = Buf()
        gcol = ph.sb("gcol", [128, 16], F32); gcb = Buf()
        wsz = 16 * BW if kind == "proj" else 2 * 16 * 256
        wb = ph.sb("wb", [128, 2, wsz], BF16); wbb = bufs(2)
        ost = ph.sb("ost", [128, 4, TT], F32 if kind == "proj" else BF16); ostb = bufs(4)
        tp = [ph.ps("tp%d" % i, [128, 8, 128], BF16) for i in range(2)]; tpb = bufs(2)
        acc = [ph.ps("acc%d" % i, [128, 512], F32) for i in range(4)]; accb = bufs(4)
        if kind == "ffn":
            ue = ph.sb("ue", [128, 2, TT + 2], F32); ueb = bufs(2)
            tt_ = ph.sb("tt", [128, 2, TT], F32); ttb = bufs(2)
            fcw = ph.sb("fcw", [128, 44, 4], F32); fcwb = Buf()
            P.load(fcwb, [(fcw[:], ffn["fcw"])])
            for u in range(2):
                P.op(P.dve, lambda h: h.memset(ue[:, u, :], 0.0), [], [ueb[u]])
        P.load(gcb, [(gcol[:], gcol_ap)])
        widx = 0
        nwb = nblk

        def wload(i):
            P.load(wbb[i % 2], [(wb[:, i % 2, :], wblk_ap[i % nwb])])
        total_w = ntiles * nwb
        wload(0)
        acci = 0; osti = 0; evi = 0
        for ti, t0 in enumerate(starts):
            for s in range(ns):
                P.load(xb[s], [(xt[:, s, :], x_ap[t0 + s * 128:t0 + (s + 1) * 128, :])])
            for s in range(ns):
                P.op(P.act, lambda h: h.activation(out=sq[:], in_=xt[:, s, :], func=AF.Square, accum_out=ss[:, s:s + 1]), [xb[s]], [sqb, ssb[s]])
                rstd_from_ss(P, ss[:, s:s + 1], ssb[s], D)
                P.op(P.dve, lambda h: h.tensor_scalar(out=hb[:, s, :], in0=xt[:, s, :], scalar1=ss[:, s:s + 1], scalar2=None, op0=ALU.mult), [xb[s], ssb[s]], [hbb[s]])
            for s in range(ns):
                for c8 in range(2):
                    k = (s * 2 + c8) % 2
                    P.pe_group([tpz(tp[k][:, c, :], hb[:, s, (c8 * 8 + c) * 128:(c8 * 8 + c + 1) * 128], C.identb[:]) for c in range(8)], [hbb[s]], [tpb[k]])
                    gv = gcol[:, c8 * 8:c8 * 8 + 8].unsqueeze(2).to_broadcast([128, 8, 128])
                    P.op(P.dve, lambda h: h.tensor_tensor(out=hT[:, c8 * 8:c8 * 8 + 8, s * 128:(s + 1) * 128], in0=tp[k][:], in1=gv, op=ALU.mult), [tpb[k], gcb], [hTb])
            for bi in range(nwb):
                if widx + 1 < total_w:
                    wload(widx + 1)
                wcur = widx % 2
                widx += 1
                if kind == "proj":
                    wv = wb[:, wcur, :].rearrange("p (c f) -> p c f", c=16)
                    for fb in range(BW // 128):
                        a = acci % 4; acci += 1
                        P.pe_group([mm(acc[a][:, 0:TT], wv[:, kc, fb * 128:(fb + 1) * 128], hT[:, kc, 0:TT], kc == 0, kc == 15) for kc in range(16)], [wbb[wcur], hTb], [accb[a]])
                        o = osti % 4; osti += 1
                        eng = P.act if (o % 2 == 0) else P.dve
                        if eng is P.act:
                            P.op(eng, lambda h: h.activation(out=ost[:, o, :], in_=acc[a][:, 0:TT], func=AF.Copy), [accb[a]], [ostb[o]])
                        else:
                            P.op(eng, lambda h: h.tensor_copy(out=ost[:, o, :], in_=acc[a][:, 0:TT]), [accb[a]], [ostb[o]])
                        f0 = bi * BW + fb * 128
                        P.store([ostb[o]], [(out_ap[f0:f0 + 128, tok0 + t0:tok0 + t0 + TT], ost[:, o, :])])
                else:
                    wv = wb[:, wcur, :].rearrange("p (g c f) -> p g c f", g=2, c=16)
                    c_lo = 0 if ti == 0 else 1
                    c_hi = TT if ti == ntiles - 1 else TT - 1
                    for fb in range(2):
                        fidx = bi * 2 + fb
                        ag = acci % 4; av = (acci + 1) % 4; acci += 2
                        P.pe_group([mm(acc[ag][:, 0:TT], wv[:, 0, kc, fb * 128:(fb + 1) * 128], hT[:, kc, 0:TT], kc == 0, kc == 15) for kc in range(16)], [wbb[wcur], hTb], [accb[ag]])
                        P.pe_group([mm(acc[av][:, 0:TT], wv[:, 1, kc, fb * 128:(fb + 1) * 128], hT[:, kc, 0:TT], kc == 0, kc == 15) for kc in range(16)], [wbb[wcur], hTb], [accb[av]])
                        u = fidx % 2
                        P.op(P.act, lambda h: h.activation(out=ue[:, u, 1:TT + 1], in_=acc[ag][:, 0:TT], func=AF.Copy), [accb[ag]], [ueb[u]])
                        P.op(P.act, lambda h: h.activation(out=tt_[:, u, :], in_=ue[:, u, 1:TT + 1], func=AF.Identity, scale=fcw[:, fidx, 1:2], bias=fcw[:, fidx, 3:4]), [ueb[u], fcwb], [ttb[u]])
                        P.op(P.dve, lambda h: h.scalar_tensor_tensor(out=tt_[:, u, :], in0=ue[:, u, 0:TT], scalar=fcw[:, fidx, 0:1], in1=tt_[:, u, :], op0=ALU.mult, op1=ALU.add), [ueb[u], fcwb, ttb[u]], [ttb[u]])
                        P.op(P.dve, lambda h: h.scalar_tensor_tensor(out=tt_[:, u, :], in0=ue[:, u, 2:TT + 2], scalar=fcw[:, fidx, 2:3], in1=tt_[:, u, :], op0=ALU.mult, op1=ALU.add), [ueb[u], fcwb, ttb[u]], [ttb[u]])
                        P.op(P.act, lambda h: h.activation(out=tt_[:, u, :], in_=tt_[:, u, :], func=AF.Gelu_apprx_tanh), [ttb[u]], [ttb[u]])
                        o = osti % 4; osti += 1
                        P.op(P.dve, lambda h: h.tensor_tensor(out=ost[:, o, :], in0=tt_[:, u, :], in1=acc[av][:, 0:TT], op=ALU.mult), [ttb[u], accb[av]], [ostb[o]])
                        P.store([ostb[o]], [(out_ap[fidx * 128:(fidx + 1) * 128, tok0 + t0 + c_lo:tok0 + t0 + c_hi], ost[:, o, c_lo:c_hi])])


def phase_down(P, C, aT_ap, tok0, ntok, wblk_ap, ncb, KC, BW, gbc_ap, xres_ap, xout_ap):
    TT = 512; ntiles = ntok // TT
    KH = KC // (ncb // 4)
    nkh = KC // KH
    aview = aT_ap.rearrange("(kc p) t -> p kc t", p=128)
    with P.phase() as ph:
        aT = ph.sb("aT", [128, KC, TT], BF16); aTb = Buf()
        wb = ph.sb("wb", [128, 2, KH * 512], BF16); wbb = bufs(2)
        yo = ph.sb("yo", [128, 4, D], F32); yob = bufs(4)
        xr = ph.sb("xr", [128, 2, D], F32); xrb = bufs(2)
        gbc = ph.sb("gbc", [128, D], F32); gbb = Buf()
        sq = ph.sb("sq", [128, D], BF16); sqb = Buf()
        ss = ph.sb("ss", [128, 4], F32); ssb = bufs(4)
        acc = [ph.ps("acc%d" % i, [128, 512], F32) for i in range(8)]; accb = bufs(8)
        P.load(gbb, [(gbc[:], gbc_ap.partition_broadcast(128))])
        total_w = ntiles * ncb
        P.load(wbb[0], [(wb[:, 0, :], wblk_ap[0])])
        widx = 0; xi = 0; ai = 0
        for ti in range(ntiles):
            t0 = ti * TT
            P.load(aTb, [(aT[:], aview[:, :, tok0 + t0:tok0 + t0 + TT])])
            for cb in range(4):
                ab = (ai % 2) * 4; ai += 1
                for kh in range(nkh):
                    if widx + 1 < total_w:
                        P.load(wbb[(widx + 1) % 2], [(wb[:, (widx + 1) % 2, :], wblk_ap[(widx + 1) % ncb])])
                    wcur = widx % 2; widx += 1
                    wv = wb[:, wcur, :].rearrange("p (k f) -> p k f", k=KH)
                    for s in range(4):
                        P.pe_group([mm(acc[ab + s][:], aT[:, kh * KH + kc, s * 128:(s + 1) * 128], wv[:, kc, :], kh == 0 and kc == 0, kh == nkh - 1 and kc == KH - 1) for kc in range(KH)], [aTb, wbb[wcur]], [accb[ab + s]])
                for s in range(4):
                    if s % 2 == 0:
                        P.op(P.act, lambda h: h.activation(out=yo[:, s, cb * 512:(cb + 1) * 512], in_=acc[ab + s][:], func=AF.Copy), [accb[ab + s]], [yob[s]])
                    else:
                        P.op(P.dve, lambda h: h.tensor_copy(out=yo[:, s, cb * 512:(cb + 1) * 512], in_=acc[ab + s][:]), [accb[ab + s]], [yob[s]])
            for s in range(4):
                r0 = t0 + s * 128
                x = xi % 2; xi += 1
                P.load(xrb[x], [(xr[:, x, :], xres_ap[r0:r0 + 128, :])])
                P.op(P.act, lambda h: h.activation(out=sq[:], in_=yo[:, s, :], func=AF.Square, accum_out=ss[:, s:s + 1]), [yob[s]], [sqb, ssb[s]])
                rstd_from_ss(P, ss[:, s:s + 1], ssb[s], D)
                P.op(P.dve, lambda h: h.scalar_tensor_tensor(out=yo[:, s, :], in0=yo[:, s, :], scalar=ss[:, s:s + 1], in1=gbc[:], op0=ALU.mult, op1=ALU.mult), [yob[s], ssb[s], gbb], [yob[s]])
                P.op(P.dve, lambda h: h.tensor_tensor(out=yo[:, s, :], in0=yo[:, s, :], in1=xr[:, x, :], op=ALU.add), [yob[s], xrb[x]], [yob[s]])
                P.store([yob[s]], [(xout_ap[r0:r0 + 128, :], yo[:, s, :])])


def phase_mixprep(P, C, mixT_ap, tok0, ntok, ggcol_ap, mixn_ap):
    TT = 512; ntiles = ntok // TT
    mview = mixT_ap.rearrange("(c p) t -> p c t", p=128)
    oview = mixn_ap.rearrange("(c p) t -> p c t", p=128)
    groups = [(0, 8), (8, 12), (12, 16)]
    with P.phase() as ph:
        mx = ph.sb("mx", [128, 16, TT], F32); mxb = Buf()
        sq = ph.sb("sq", [128, 4, TT], BF16); sqb = bufs(4)
        rs = ph.sb("rs", [128, 3, TT], F32); rsb = bufs(3)
        mo = ph.sb("mo", [128, 16, TT], BF16); mob = Buf()
        gg = ph.sb("gg", [128, 16], F32); ggb = Buf()
        acc = [ph.ps("acc%d" % i, [128, 512], F32) for i in range(3)]; accb = bufs(3)
        P.load(ggb, [(gg[:], ggcol_ap)])
        qi = 0
        for ti in range(ntiles):
            t0 = tok0 + ti * TT
            P.load(mxb, [(mx[:], mview[:, :, t0:t0 + TT])])
            for gi, (c0, c1) in enumerate(groups):
                for c in range(c0, c1):
                    q = qi % 4; qi += 1
                    P.op(P.act, lambda h: h.activation(out=sq[:, q, :], in_=mx[:, c, :], func=AF.Square), [mxb], [sqb[q]])
                    P.op(P.pe, lambda h: h.matmul(acc[gi][:], C.onesb[:], sq[:, q, :], start=(c == c0), stop=(c == c1 - 1)), [sqb[q]], [accb[gi]])
                n = (c1 - c0) * 128
                P.op(P.dve, lambda h: h.tensor_scalar(out=rs[:, gi, :], in0=acc[gi][:], scalar1=1.0 / n, scalar2=EPS, op0=ALU.mult, op1=ALU.add), [accb[gi]], [rsb[gi]])
                P.op(P.act, lambda h: h.activation(out=rs[:, gi, :], in_=rs[:, gi, :], func=AF.Sqrt), [rsb[gi]], [rsb[gi]])
                P.op(P.dve, lambda h: h.reciprocal(out=rs[:, gi, :], in_=rs[:, gi, :]), [rsb[gi]], [rsb[gi]])
                for c in range(c0, c1):
                    P.op(P.dve, lambda h: h.scalar_tensor_tensor(out=mo[:, c, :], in0=mx[:, c, :], scalar=gg[:, c:c + 1], in1=rs[:, gi, :], op0=ALU.mult, op1=ALU.mult), [mxb, ggb, rsb[gi]], [mob])
            P.store([mob], [(oview[:, :, t0:t0 + TT], mo[:])])


def attn_core(P, C, ph, qr, qrb, kr, krb, vt, vtb, Lq, Lk, masked, mixT_ap, row0, tok0, st):
    sc = 1.0 / math.sqrt(128.0)
    for q0 in range(0, Lq, 512):
        if masked:
            kbs = [(kb, (kb * 128 - q0 + 1024) // 128) for kb in range(Lk // 128) if -1024 <= kb * 128 - q0 < 1536]
        else:
            kbs = [(kb, None) for kb in range(Lk // 128)]
        oi = st["oi"] % 2; st["oi"] += 1
        for n, (kb, mo) in enumerate(kbs):
            si = st["si"] % 2; st["si"] += 1
            P.op(P.pe, lambda h: h.matmul(st["sps"][si][:], kr[:, kb * 128:(kb + 1) * 128], qr[:, q0:q0 + 512], start=True, stop=True), [krb, qrb], [st["spsb"][si]])
            pi = st["pi"] % 3; st["pi"] += 1
            if masked:
                P.op(P.act, lambda h: h.activation(out=st["pe_"][:, pi, :], in_=st["sps"][si][:], func=AF.Exp, scale=sc), [st["spsb"][si]], [st["peb"][pi]])
                P.op(P.dve, lambda h: h.tensor_tensor(out=st["pm"][:, pi, :], in0=st["pe_"][:, pi, :], in1=C.mask[:, mo, :], op=ALU.mult), [st["peb"][pi], C.maskb], [st["pmb"][pi]])
            else:
                P.op(P.act, lambda h: h.activation(out=st["pm"][:, pi, :], in_=st["sps"][si][:], func=AF.Exp, scale=sc), [st["spsb"][si]], [st["pmb"][pi]])
            P.op(P.pe, lambda h: h.matmul(st["ops"][oi][:], vt[:, kb, :], st["pm"][:, pi, :], start=(n == 0), stop=(n == len(kbs) - 1)), [vtb, st["pmb"][pi]], [st["opsb"][oi]])
            P.op(P.pe, lambda h: h.matmul(st["dps"][oi][:], C.onesb[:], st["pm"][:, pi, :], start=(n == 0), stop=(n == len(kbs) - 1)), [st["pmb"][pi]], [st["dpsb"][oi]])
        r = st["ri"] % 2; st["ri"] += 1
        P.op(P.dve, lambda h: h.reciprocal(out=st["rc"][:, r, :], in_=st["dps"][oi][:]), [st["dpsb"][oi]], [st["rcb"][r]])
        P.op(P.dve, lambda h: h.tensor_tensor(out=st["yo"][:, r, :], in0=st["ops"][oi][:], in1=st["rc"][:, r, :], op=ALU.mult), [st["opsb"][oi], st["rcb"][r]], [st["yob"][r]])
        P.store([st["yob"][r]], [(mixT_ap[row0:row0 + 128, tok0 + q0:tok0 + q0 + 512], st["yo"][:, r, :])])


def attn_state(ph):
    st = dict(oi=0, si=0, pi=0, ri=0)
    st["sps"] = [ph.ps("sps%d" % i, [128, 512], F32) for i in range(2)]; st["spsb"] = bufs(2)
    st["ops"] = [ph.ps("ops%d" % i, [128, 512], F32) for i in range(2)]; st["opsb"] = bufs(2)
    st["dps"] = [ph.ps("dps%d" % i, [128, 512], F32) for i in range(2)]; st["dpsb"] = bufs(2)
    st["pe_"] = ph.sb("pex", [128, 3, 512], BF16); st["peb"] = bufs(3)
    st["pm"] = ph.sb("pmx", [128, 3, 512], BF16); st["pmb"] = bufs(3)
    st["rc"] = ph.sb("rcx", [128, 2, 512], F32); st["rcb"] = bufs(2)
    st["yo"] = ph.sb("yox", [128, 2, 512], F32); st["yob"] = bufs(2)
    return st


def load_feat_bf16(P, C, ph, src_ap, row0, tok0, L, dst, dstb, stage, stageb, rope=None, tpv=None):
    for ci, c0 in enumerate(range(0, L, 2048)):
        n = min(2048, L - c0)
        k = C.stg_i % 2; C.stg_i += 1
        P.load(stageb[k], [(stage[:, k, 0:n], src_ap[row0:row0 + 128, tok0 + c0:tok0 + c0 + n])])
        if tpv is not None:
            tps, tpsb, vb16, vb16b = tpv
            P.op(P.act, lambda h: h.activation(out=vb16[:, 0:n], in_=stage[:, k, 0:n], func=AF.Copy), [stageb[k]], [vb16b])
            for j in range(n // 128):
                kk = C.tp_i % 2; C.tp_i += 1
                P.op(P.pe, lambda h: h.transpose(out=tps[kk], in_=vb16[:, j * 128:(j + 1) * 128], identity=C.identb[:]), [vb16b], [tpsb[kk]])
                P.op(P.dve, lambda h: h.tensor_copy(out=dst[:, c0 // 128 + j, :], in_=tps[kk]), [tpsb[kk]], [dstb])
            continue
        if rope is None:
            P.op(P.act, lambda h: h.activation(out=dst[:, c0:c0 + n], in_=stage[:, k, 0:n], func=AF.Copy), [stageb[k]], [dstb])
        else:
            rps, rpsb, cs, csb, t1, t1b = rope
            P.op(P.act, lambda h: h.activation(out=dst[:, c0:c0 + n], in_=stage[:, k, 0:n], func=AF.Copy), [stageb[k]], [dstb])
            for j in range(n // 512):
                kk = C.tp_i % 2; C.tp_i += 1
                cc = c0 + j * 512
                P.load(csb, [(cs[:, 0, :], C.rope_cs[0, :, cc:cc + 512]), (cs[:, 1, :], C.rope_cs[1, :, cc:cc + 512])])
                P.op(P.pe, lambda h: h.matmul(rps[kk][0:32, :], C.ropepm[:], stage[0:32, k, j * 512:(j + 1) * 512], start=True, stop=True), [stageb[k]], [rpsb[kk]])
                P.op(P.dve, lambda h: h.tensor_tensor(out=t1[:], in0=rps[kk][0:32, :], in1=cs[:, 1, :], op=ALU.mult), [rpsb[kk], csb], [t1b])
                P.op(P.dve, lambda h: h.tensor_tensor(out=cs[:, 0, :], in0=stage[0:32, k, j * 512:(j + 1) * 512], in1=cs[:, 0, :], op=ALU.mult), [stageb[k], csb], [csb])
                P.op(P.dve, lambda h: h.tensor_tensor(out=dst[0:32, cc:cc + 512], in0=t1[:], in1=cs[:, 0, :], op=ALU.add), [t1b, csb], [dstb])


def phase_attn(P, C, projT, mixT, tok0, L, kvT=None):
    memx = kvT is not None
    Lk = NMEM if memx else L
    with P.phase() as ph:
        stage = ph.sb("stage", [128, 2, 2048], F32); stageb = bufs(2)
        qr = ph.sb("qr", [128, L], BF16); qrb = Buf()
        kr = ph.sb("kr", [128, Lk], BF16); krb = Buf()
        vt = ph.sb("vt", [128, Lk // 128, 128], BF16); vtb = Buf()
        vb16 = ph.sb("vb16", [128, 2048], BF16); vb16b = Buf()
        misc = ph.ps("misc", [128, 512], F32); miscb = Buf()
        tps = [misc[:, 0:64].bitcast(BF16)] * 2; tpsb = [miscb] * 2
        st = attn_state(ph)
        if not memx:
            rps = [misc] * 2; rpsb = [miscb] * 2
            mask = ph.sb("mask", [128, 20, 512], BF16); maskb = Buf()
            P.load(maskb, [(mask[:], C.mask_d)])
            C.mask = mask; C.maskb = maskb
            cs = ph.sb("cs", [32, 2, 512], F32); csb = Buf()
            t1 = ph.sb("t1", [32, 512], F32); t1b = Buf()
            rope = (rps, rpsb, cs, csb, t1, t1b)
        for hd in range(4):
            if memx:
                load_feat_bf16(P, C, ph, projT, 4608 + hd * 128, tok0, L, qr, qrb, stage, stageb)
                load_feat_bf16(P, C, ph, kvT, hd * 128, 0, NMEM, kr, krb, stage, stageb)
                load_feat_bf16(P, C, ph, kvT, 512 + hd * 128, 0, NMEM, vt, vtb, stage, stageb, tpv=(tps, tpsb, vb16, vb16b))
                attn_core(P, C, ph, qr, qrb, kr, krb, vt, vtb, L, NMEM, False, mixT, 1536 + hd * 128, tok0, st)
            else:
                load_feat_bf16(P, C, ph, projT, 3072 + hd * 128, tok0, L, qr, qrb, stage, stageb, rope=rope)
                load_feat_bf16(P, C, ph, projT, 3584 + hd * 128, tok0, L, kr, krb, stage, stageb, rope=rope)
                load_feat_bf16(P, C, ph, projT, 4096 + hd * 128, tok0, L, vt, vtb, stage, stageb, tpv=(tps, tpsb, vb16, vb16b))
                attn_core(P, C, ph, qr, qrb, kr, krb, vt, vtb, L, L, True, mixT, 1024 + hd * 128, tok0, st)


def phase_filter_mlp(P, C, l, sp, zT_ap, hd3_ap):
    L = sp["L"]
    with P.phase() as ph:
        prm = ph.sb("prm", [64, 8], F32); prmb = Buf()
        w1 = ph.sb("w1", [33, 64], F32); w2 = ph.sb("w2", [64, 2, 64], F32); wb_ = Buf()
        zt = ph.sb("zt", [33, 2, 512], F32); ztb = bufs(2)
        h = ph.sb("h", [64, 3, 512], F32); hbf = bufs(3)
        t = ph.sb("t", [64, 2, 512], F32); tb = bufs(2)
        ho = ph.sb("ho", [64, 2, 512], BF16); hob = bufs(2)
        zz = ph.sb("zz", [64, 128], BF16); zzb = Buf()
        ps_ = [ph.ps("mps%d" % i, [64, 512], F32) for i in range(2)]; psb = bufs(2)
        P.load(prmb, [(prm[:, 0:4], C.fprm[l])])
        P.load(wb_, [(w1[:], C.f_w1[l]), (w2[:, 0, :], C.f_w2[l]), (w2[:, 1, :], C.f_w3[l])])
        for j in range(3):
            P.op(P.dve, lambda h_: h_.scalar_tensor_tensor(out=prm[:, 4 + j:5 + j], in0=prm[:, j:j + 1], scalar=1.0 / 9.0, in1=prm[:, 3:4], op0=ALU.mult, op1=ALU.mult), [prmb], [prmb])
        P.op(P.dve, lambda h_: h_.tensor_scalar(out=prm[:, 7:8], in0=prm[:, 3:4], scalar1=1.0 / 9.0, scalar2=None, op0=ALU.mult), [prmb], [prmb])
        P.op(P.dve, lambda h_: h_.memset(zz[:], 0.0), [], [zzb])
        P.store([zzb], [(hd3_ap[:, L:L + 128], zz[:])])
        pi = 0
        for ti in range(L // 512):
            c0 = ti * 512
            k = ti % 2
            P.load(ztb[k], [(zt[:, k, :], zT_ap[:, c0:c0 + 512])])
            for lay in range(3):
                p = pi % 2; pi += 1
                if lay == 0:
                    P.op(P.pe, lambda h_: h_.matmul(ps_[p][:], w1[:], zt[:, k, :], start=True, stop=True), [wb_, ztb[k]], [psb[p]])
                else:
                    P.op(P.pe, lambda h_: h_.matmul(ps_[p][:], w2[:, lay - 1, :], h[:, lay - 1, :], start=True, stop=True), [wb_, hbf[lay - 1]], [psb[p]])
                P.op(P.act, lambda h_: h_.activation(out=h[:, lay, :], in_=ps_[p][:], func=AF.Sin, scale=prm[:, 7:8], bias=prm[:, 4 + lay:5 + lay]), [psb[p], prmb], [hbf[lay]])
                for rep in range(2):
                    q = rep
                    P.op(P.dve, lambda h_: h_.tensor_tensor(out=t[:, q, :], in0=h[:, lay, :], in1=h[:, lay, :], op=ALU.mult), [hbf[lay]], [tb[q]])
                    P.op(P.dve, lambda h_: h_.tensor_scalar(out=t[:, q, :], in0=t[:, q, :], scalar1=-4.0, scalar2=3.0, op0=ALU.mult, op1=ALU.add), [tb[q]], [tb[q]])
                    P.op(P.dve, lambda h_: h_.tensor_tensor(out=h[:, lay, :], in0=h[:, lay, :], in1=t[:, q, :], op=ALU.mult), [hbf[lay], tb[q]], [hbf[lay]])
            P.op(P.act, lambda h_: h_.activation(out=ho[:, k, :], in_=h[:, 2, :], func=AF.Copy), [hbf[2]], [hob[k]])
            P.store([hob[k]], [(hd3_ap[:, c0:c0 + 512], ho[:, k, :])])


def phase_hyprep(P, C, l, projT, tok0, sp, hv0, hx1, hx2):
    L = sp["L"]; nJ = sp["nJ"]; CB = sp["CB"]; nbg = 128 // CB
    with P.phase() as ph:
        xin = ph.sb("xin", [128, L + 2], F32); xinb = Buf()
        xc = ph.sb("xc", [128, L], F32); xcb = Buf()
        tk = ph.sb("tk", [128, 128, nJ], F32); tkb = Buf()
        tkh = ph.sb("tkh", [128, 128, nJ], BF16); tkhb = Buf()
        cw = ph.sb("cw", [128, 3, 8, 4], F32); cwb = Buf()
        tps = [ph.ps("tpf%d" % i, [128, 4, 128], F32) for i in range(2)]; tpsb = bufs(2)
        P.load(cwb, [(cw[:], C.cw[l])])
        P.op(P.dve, lambda h: h.memset(xin[:, 0:1], 0.0), [], [xinb])
        P.op(P.dve, lambda h: h.memset(xin[:, L + 1:L + 2], 0.0), [], [xinb])
        ti_ = 0
        for tn in range(3):
            for cg in range(8):
                row0 = tn * 1024 + cg * 128
                P.load(xinb, [(xin[:, 1 + c0:1 + c0 + 2048], projT[row0:row0 + 128, tok0 + c0:tok0 + c0 + 2048]) for c0 in range(0, L, 2048)])
                P.op(P.act, lambda h: h.activation(out=xc[:], in_=xin[:, 1:L + 1], func=AF.Identity, scale=cw[:, tn, cg, 1:2], bias=cw[:, tn, cg, 3:4]), [xinb, cwb], [xcb])
                P.op(P.dve, lambda h: h.scalar_tensor_tensor(out=xc[:], in0=xin[:, 0:L], scalar=cw[:, tn, cg, 0:1], in1=xc[:], op0=ALU.mult, op1=ALU.add), [xinb, cwb, xcb], [xcb])
                P.op(P.dve, lambda h: h.scalar_tensor_tensor(out=xc[:], in0=xin[:, 2:L + 2], scalar=cw[:, tn, cg, 2:3], in1=xc[:], op0=ALU.mult, op1=ALU.add), [xinb, cwb, xcb], [xcb])
                if tn == 2:
                    P.store([xcb], [(hx2[cg * 128:(cg + 1) * 128, c0:c0 + 2048], xc[:, c0:c0 + 2048]) for c0 in range(0, L, 2048)])
                    continue
                dst = tkh if tn == 0 else tk; dstb = tkhb if tn == 0 else tkb
                for J4 in range(nJ // 4):
                    k = ti_ % 2; ti_ += 1
                    P.pe_group([tpz(tps[k][:, jj, :], xc[:, (J4 * 4 + jj) * 128:(J4 * 4 + jj + 1) * 128], C.identf[:]) for jj in range(4)], [xcb], [tpsb[k]])
                    ov = dst[:, :, J4 * 4:J4 * 4 + 4].rearrange("p c j -> p j c")
                    if k == 0:
                        P.op(P.act, lambda h: h.activation(out=ov, in_=tps[k][:], func=AF.Copy), [tpsb[k]], [dstb])
                    else:
                        P.op(P.dve, lambda h: h.tensor_copy(out=ov, in_=tps[k][:]), [tpsb[k]], [dstb])
                tgt = hv0 if tn == 0 else hx1
                prs = []
                for b in range(nbg):
                    blk = cg * nbg + b
                    prs.append((tgt[blk], dst[:, b * CB:(b + 1) * CB, :]))
                P.store([dstb], prs)


def phase_hyena(P, C, l, sp, si, hd3_ap, hv0, hx1, hx2, mixT, tok0):
    L = sp["L"]; nJ = sp["nJ"]; MJ = sp["MJ"]; gB = sp["gB"]; gA = sp["gA"]; CB = sp["CB"]; nblk = sp["nblk"]
    NG = 32
    NA = 16
    with P.phase() as ph:
        hd3 = ph.sb("hd3", [64, L + 128], BF16); hd3b = Buf()
        w4 = ph.sb("w4", [64, 4096], BF16); w4b = Buf()
        fa = ph.sb("fa", [128, 2, 256], BF16); sbm = ph.sb("sbm", [128, 12, 128], BF16)
        Rm = ph.sb("Rm", [128, 3, 256], BF16); ga = ph.sb("ga", [128, 4, 128], BF16); cstb = Buf()
        dec = ph.sb("dec", [128, 2, nJ * CB], F32); decb = Buf()
        kfraw = ph.sb("kfraw", [128, nJ * 4 * CB], F32); kfb_ = Buf()
        kf = kfraw[:].rearrange("p (b d o c) -> p b d o c", b=nJ, d=2, o=2)
        Zs = kfraw[:, 0:2 * CB * nJ].bitcast(BF16).rearrange("p (q n) -> p q n", q=4)
        Zsb = kfb_
        kb = ph.sb("kb", [128, 2, 2, CB * nJ], BF16); kbb = Buf()
        nrm = ph.sb("nrm", [128, 2 * CB], F32); nrmb = Buf()
        As = ph.sb("As", [128, 2, 2, 4, 256], BF16); Asb = bufs(2)
        Kh = ph.sb("Kh", [128, 2, 16, 2, 128], BF16); Khb = Buf()
        fb = ph.sb("fb", [128, 2, NG], F32); fbb = Buf()
        X1 = ph.sb("X1", [128, CB * nJ], BF16); X1b = Buf()
        x1t = ph.sb("x1t", [128, CB * nJ], F32); x1tb = Buf()
        x2c = ph.sb("x2c", [CB, L], F32); x2cb = Buf()
        Pp = ph.sb("Pp", [128, 2, 4, 4, 128], BF16); Ppb = bufs(2)
        psA = [ph.ps("psA%d" % i, [128, 512], F32) for i in range(2)]; psAb = bufs(2)
        psBr = ph.ps("psBr", [128, 4, 128], F32); psBi = ph.ps("psBi", [128, 4, 128], F32); psBb = Buf()
        psZ = [ph.ps("psZ%d" % i, [128, 512], F32) for i in range(2)]; psZb = bufs(2)
        psY = [ph.ps("psY%d" % i, [128, 512], F32) for i in range(2)]; psYb = bufs(2)
        P.load(hd3b, [(hd3[:], hd3_ap)])
        P.load(w4b, [(w4[:], C.f_w4[l])], q=P.pool)
        P.load(cstb, [(fa[:], C.dft_fa), (sbm[:], C.dft_sb[si]), (Rm[:], C.dft_R[si]), (ga[:], C.dft_ga[si])])
        cnt = dict(a=0, z=0, y=0, ev=0, p=0, s=0)

        def evac(out_ap, in_ap, rd, wr):
            cnt["ev"] += 1
            if cnt["ev"] % 2 == 0:
                P.op(P.act, lambda h: h.activation(out=out_ap, in_=in_ap, func=AF.Copy), rd, wr)
            else:
                P.op(P.dve, lambda h: h.tensor_copy(out=out_ap, in_=in_ap), rd, wr)

        def mm(out, lhsT, rhs, st, sp_):
            return lambda h: h.matmul(out, lhsT, rhs, start=st, stop=sp_)

        for blk in range(nblk):
            ch0 = blk * CB
            P.load(decb, [(dec[:, 0, :], C.dec[si][0, :, blk].rearrange("p b c -> p (b c)")), (dec[:, 1, :], C.dec[si][1, :, blk].rearrange("p b c -> p (b c)"))])
            P.load(fbb, [(fb[:, 0, :], C.fbias[si][l, 0, :, blk * NG:(blk + 1) * NG]), (fb[:, 1, :], C.fbias[si][l, 1, :, blk * NG:(blk + 1) * NG])])
            for B in range(nJ):
                for d in range(2):
                    a = cnt["a"] % 2; cnt["a"] += 1
                    fns = []
                    for o in range(2):
                        col0 = o * 2048 + d * 1024 + ch0
                        fns.append(mm(psA[a][:, o * CB:(o + 1) * CB], hd3[:, B * 128 + d:B * 128 + d + 128], w4[:, col0:col0 + CB], True, True))
                    P.pe_group(fns, [hd3b, w4b], [psAb[a]])
                    dv = dec[:, d, B * CB:(B + 1) * CB].unsqueeze(1).to_broadcast([128, 2, CB])
                    P.op(P.dve, lambda h: h.tensor_tensor(out=kf[:, B, d, :, :], in0=psA[a][:, 0:2 * CB].rearrange("p (o c) -> p o c", o=2), in1=dv, op=ALU.mult), [psAb[a], decb], [kfb_])
            kview = kf.rearrange("p b d o c -> p (o c) (b d)")
            P.op(P.dve, lambda h: h.tensor_reduce(out=nrm[:], in_=kview, axis=AX.X, op=ALU.add, apply_absolute_value=True), [kfb_], [nrmb])
            a = cnt["a"] % 2; cnt["a"] += 1
            P.op(P.pe, lambda h: h.matmul(psA[a][:, 0:2 * CB], C.onesf[:], nrm[:], start=True, stop=True), [nrmb], [psAb[a]])
            P.op(P.dve, lambda h: h.reciprocal(out=nrm[:], in_=psA[a][:, 0:2 * CB]), [psAb[a]], [nrmb])
            for o in range(2):
                for d in range(2):
                    iv = kf[:, :, d, o, :].rearrange("p b c -> p c b")
                    nv = nrm[:, o * CB:(o + 1) * CB].unsqueeze(2).to_broadcast([128, CB, nJ])
                    P.op(P.dve, lambda h: h.tensor_tensor(out=kb[:, o, d, :].rearrange("p (c b) -> p c b", c=CB), in0=iv, in1=nv, op=ALU.mult), [kfb_, nrmb], [kbb])
            P.load(X1b, [(X1[:], hv0[blk].rearrange("p c j -> p (c j)"))])
            P.load(x1tb, [(x1t[:], hx1[blk].rearrange("p c j -> p (c j)"))])
            P.load(x2cb, [(x2c[:, c0:c0 + 2048], hx2[ch0:ch0 + CB, c0:c0 + 2048]) for c0 in range(0, L, 2048)])
            for o in range(2):
                for a4 in range(NA // 4):
                    s_ = cnt["s"] % 2; cnt["s"] += 1
                    for d in range(2):
                        for pr in range(2):
                            a = cnt["a"] % 2; cnt["a"] += 1
                            fns = []
                            for k2 in range(2):
                                ag = a4 * 4 + pr * 2 + k2
                                fns.append(mm(psA[a][:, k2 * 256:(k2 + 1) * 256], kb[:, o, d, ag * 128:(ag + 1) * 128], fa[:, d, :], True, True))
                            P.pe_group(fns, [kbb, cstb], [psAb[a]])
                            evac(As[:, s_, d, pr * 2:pr * 2 + 2, :], psA[a][:].rearrange("p (k f) -> p k f", k=2), [psAb[a]], [Asb[s_]])
                    for hf in range(2):
                        fr = []; fi = []
                        seq_r = [(d, m, pl) for d in range(2) for (m, pl) in ((0, 0), (2, 1))]
                        seq_i = [(d, m, pl) for d in range(2) for (m, pl) in ((1, 0), (0, 1))]
                        for n, (d, m, pl) in enumerate(seq_r):
                            fr.append(mm(psBr[:], sbm[:, (d * 2 + hf) * 3 + m, :], As[:, s_, d, :, pl * 128:(pl + 1) * 128], n == 0, n == 3))
                        for n, (d, m, pl) in enumerate(seq_i):
                            fi.append(mm(psBi[:], sbm[:, (d * 2 + hf) * 3 + m, :], As[:, s_, d, :, pl * 128:(pl + 1) * 128], n == 0, n == 3))
                        P.pe_group(fr + fi, [cstb, Asb[s_]], [psBb])
                        for al in range(4):
                            ag = a4 * 4 + al; g = ag * 2 + hf
                            P.op(P.act, lambda h: h.activation(out=Kh[:, hf, ag, 0, :], in_=psBr[:, al, :], func=AF.Identity, bias=fb[:, o, g:g + 1]), [psBb, fbb], [Khb])
                        P.op(P.dve, lambda h: h.tensor_copy(out=Kh[:, hf, a4 * 4:a4 * 4 + 4, 1, :], in_=psBi[:]), [psBb], [Khb])
                for a4 in range(NA // 4):
                    s_ = cnt["s"] % 2; cnt["s"] += 1
                    for pr in range(2):
                        a = cnt["a"] % 2; cnt["a"] += 1
                        fns = []
                        for k2 in range(2):
                            ag = a4 * 4 + pr * 2 + k2
                            fns.append(mm(psA[a][:, k2 * 256:(k2 + 1) * 256], X1[:, ag * 128:(ag + 1) * 128], fa[:, 0, :], True, True))
                        P.pe_group(fns, [X1b, cstb], [psAb[a]])
                        evac(As[:, s_, 0, pr * 2:pr * 2 + 2, :], psA[a][:].rearrange("p (k f) -> p k f", k=2), [psAb[a]], [Asb[s_]])
                    for hf in range(2):
                        fr = []; fi = []
                        for n, (m, pl) in enumerate([(0, 0), (2, 1)]):
                            fr.append(mm(psBr[:], sbm[:, hf * 3 + m, :], As[:, s_, 0, :, pl * 128:(pl + 1) * 128], n == 0, n == 1))
                        for n, (m, pl) in enumerate([(1, 0), (0, 1)]):
                            fi.append(mm(psBi[:], sbm[:, hf * 3 + m, :], As[:, s_, 0, :, pl * 128:(pl + 1) * 128], n == 0, n == 1))
                        P.pe_group(fr + fi, [cstb, Asb[s_]], [psBb])
                        p_ = cnt["p"] % 2; cnt["p"] += 1
                        kin = Kh[:, hf, a4 * 4:a4 * 4 + 4, :, :]
                        xr = psBr[:].unsqueeze(2).to_broadcast([128, 4, 2, 128])
                        xi = psBi[:].unsqueeze(2).to_broadcast([128, 4, 2, 128])
                        P.op(P.dve, lambda h: h.tensor_tensor(out=Pp[:, p_, :, 0:2, :], in0=xr, in1=kin, op=ALU.mult), [psBb, Khb], [Ppb[p_]])
                        P.op(P.dve, lambda h: h.tensor_tensor(out=Pp[:, p_, :, 2:4, :], in0=xi, in1=kin, op=ALU.mult), [psBb, Khb], [Ppb[p_]])
                        for al in range(4):
                            g = (a4 * 4 + al) * 2 + hf
                            z = cnt["z"] % 2; cnt["z"] += 1
                            fns = [mm(psZ[z][:, 0:256], Pp[:, p_, al, pl, :], Rm[:, rm, :], n == 0, n == 3) for n, (pl, rm) in enumerate([(0, 0), (1, 1), (2, 1), (3, 2)])]
                            P.pe_group(fns, [Ppb[p_], cstb], [psZb[z]])
                            zin = psZ[z][:, 0:256].rearrange("p (q c i) -> p q c i", q=4, c=gB)
                            if o == 0:
                                zo = Zs[:, :, g * gB * nJ:(g + 1) * gB * nJ].rearrange("p q (c i) -> p q c i", c=gB)
                            else:
                                zo = Zs.rearrange("p q (i c) -> p q c i", c=CB)[:, :, g * gB:(g + 1) * gB, :]
                            evac(zo, zin, [psZb[z]], [Zsb])
                if o == 0:
                    for ch in range(CB * nJ // 512):
                        y = cnt["y"] % 2; cnt["y"] += 1
                        fns = [mm(psY[y][:], ga[:, [0, 2, 1, 3][q], :], Zs[:, q, ch * 512:(ch + 1) * 512], q == 0, q == 3) for q in range(4)]
                        P.pe_group(fns, [cstb, Zsb], [psYb[y]])
                        P.op(P.dve, lambda h: h.tensor_tensor(out=X1[:, ch * 512:(ch + 1) * 512], in0=psY[y][:], in1=x1t[:, ch * 512:(ch + 1) * 512], op=ALU.mult), [psYb[y], x1tb], [X1b])
                else:
                    for I4 in range(nJ // 4):
                        y = cnt["y"] % 2; cnt["y"] += 1
                        fns = []
                        for ii in range(4):
                            I = I4 * 4 + ii
                            for q in range(4):
                                fns.append(mm(psY[y][0:CB, ii * 128:(ii + 1) * 128], Zs[:, q, I * CB:(I + 1) * CB], ga[:, [0, 2, 1, 3][q], :], q == 0, q == 3))
                        P.pe_group(fns, [Zsb, cstb], [psYb[y]])
                        P.op(P.dve, lambda h: h.tensor_tensor(out=x2c[:, I4 * 512:(I4 + 1) * 512], in0=psY[y][0:CB, :], in1=x2c[:, I4 * 512:(I4 + 1) * 512], op=ALU.mult), [psYb[y], x2cb], [x2cb])
            P.store([x2cb], [(mixT[ch0:ch0 + CB, tok0 + c0:tok0 + c0 + 2048], x2c[:, c0:c0 + 2048]) for c0 in range(0, L, 2048)])


def build_program(depth=DEPTH, seqs=(("p", LP), ("s", LS)), debug=False, stop_after=None):
    P = Prog(); nc = P.nc; C = Ctx()
    C.stg_i = 0; C.tp_i = 0
    IN = lambda name, shape, dt=F32: P.dram(name, shape, dt, "ExternalInput")
    dbg = "ExternalOutput" if debug else "Internal"
    xin = {"p": IN("x_p", [LP, D]), "s": IN("x_s", [LS, D])}
    mem = {"p": IN("mem_p", [NMEM, D]), "s": IN("mem_s", [NMEM, D])}
    w_in = IN("w_in", [depth, D, DIN]); w_out = IN("w_out", [depth, D, D]); w_up = IN("w_up", [depth, D, 2 * DFF])
    w_down = IN("w_down", [depth, DFF, D]); w_kv = IN("w_kv", [depth, D, 1024])
    C.f_w1 = IN("f_w1", [DEPTH, 33, 64]); C.f_w2 = IN("f_w2", [DEPTH, 64, 64]); C.f_w3 = IN("f_w3", [DEPTH, 64, 64])
    C.f_w4 = IN("f_w4", [DEPTH, 64, 4096]); C.fprm = IN("fprm", [DEPTH, 64, 4])
    gcols = IN("gcols", [DEPTH, 4, 128, 16])
    gvecs = IN("gvecs", [DEPTH, 2, D])
    C.cw = IN("cw", [DEPTH, 128, 3, 8, 4]); fcw = IN("fcw", [DEPTH, 128, 44, 4])
    identb_d = IN("identb", [128, 128], BF16); identf_d = IN("identf", [128, 128]); onesb_d = IN("onesb", [128, 128], BF16)
    onesf_d = IN("onesf", [128, 128]); mask_d = IN("maskt", [128, 20, 512], BF16)
    C.rope_cs = IN("rope_cs", [2, 32, LP]); ropepm_d = IN("ropepm", [32, 32])
    C.dft_fa = IN("dft_fa", [2, 128, 256], BF16).rearrange("d p f -> p d f")
    dft_sb = IN("dft_sb", [2, 12, 128, 128], BF16); dft_R = IN("dft_R", [2, 3, 128, 256], BF16); dft_ga = IN("dft_ga", [2, 4, 128, 128], BF16)
    C.dft_sb = [dft_sb[i].rearrange("m p f -> p m f") for i in range(2)]
    C.dft_R = [dft_R[i].rearrange("m p f -> p m f") for i in range(2)]
    C.dft_ga = [dft_ga[i].rearrange("m p f -> p m f") for i in range(2)]
    spS = seq_params(LS); spP = seq_params(LP)
    SPS = {"p": spP, "s": spS}; SI = {"p": 0, "s": 1}
    C.dec = [IN("dec_p", [2, 128, spP["nblk"], spP["nJ"], spP["CB"]]), IN("dec_s", [2, 128, spS["nblk"], spS["nJ"], spS["CB"]])]
    C.fbias = [IN("fbias_p", [DEPTH, 2, 128, 32 * spP["nblk"]]), IN("fbias_s", [DEPTH, 2, 128, 32 * spS["nblk"]])]
    zT = {"p": IN("zT_p", [33, LP]), "s": IN("zT_s", [33, LS])}
    yout = {"p": P.dram("y_p", [LP, D], F32, "ExternalOutput"), "s": P.dram("y_s", [LS, D], F32, "ExternalOutput")}
    NT = sum(L for _, L in seqs)
    TOK0 = {}; t = 0
    for nm, L in seqs:
        TOK0[nm] = t; t += L
    xa = P.dram("xa", [NT, D], F32, dbg); xb = P.dram("xb", [NT, D], F32)
    projT = P.dram("projT", [DIN, NT], F32, dbg); mixT = P.dram("mixT", [D, NT], F32, dbg)
    mixn = P.dram("mixn", [D, NT], BF16); ffT = P.dram("ffT", [DFF, NT], BF16)
    kvT = {nm: P.dram("kvT_" + nm, [1024, NMEM], F32) for nm, _ in seqs}
    hd3 = {nm: P.dram("hd3_" + nm, [64, L + 128], BF16) for nm, L in seqs}
    hv0 = {nm: P.dram("hv0_" + nm, [SPS[nm]["nblk"], 128, SPS[nm]["CB"], SPS[nm]["nJ"]], BF16) for nm, _ in seqs}
    hx1 = {nm: P.dram("hx1_" + nm, [SPS[nm]["nblk"], 128, SPS[nm]["CB"], SPS[nm]["nJ"]], F32) for nm, _ in seqs}
    hx2 = {nm: P.dram("hx2_" + nm, [DH, L], F32) for nm, L in seqs}
    wb_in = P.dram("wb_in", [depth, 10, 128, 16 * 512], BF16); wb_out = P.dram("wb_out", [depth, 4, 128, 16 * 512], BF16)
    wb_up = P.dram("wb_up", [depth, 22, 128, 2 * 16 * 256], BF16); wb_down = P.dram("wb_down", [depth, 8, 128, 22 * 512], BF16)
    wb_kv = P.dram("wb_kv", [depth, 2, 128, 16 * 512], BF16)
    cst = P.es
    C.identb = cst.enter_context(nc.sbuf_tensor("identb_s", [128, 128], BF16)); C.identf = cst.enter_context(nc.sbuf_tensor("identf_s", [128, 128], F32))
    C.onesb = cst.enter_context(nc.sbuf_tensor("onesb_s", [128, 128], BF16)); C.onesf = cst.enter_context(nc.sbuf_tensor("onesf_s", [128, 128], F32))
    C.mask_d = mask_d; C.ropepm = cst.enter_context(nc.sbuf_tensor("ropepm_s", [32, 32], F32))
    cb_ = Buf()
    P.load(cb_, [(C.identb[:], identb_d), (C.identf[:], identf_d), (C.onesb[:], onesb_d), (C.onesf[:], onesf_d), (C.ropepm[:], ropepm_d)])
    P.barrier()
    for l in range(depth):
        for cbk in range(10):
            P.dram_dma(wb_in[l, cbk].rearrange("p (c f) -> p c f", c=16), w_in[l].rearrange("(c p) f -> p c f", p=128)[:, :, cbk * 512:(cbk + 1) * 512])
        for cbk in range(4):
            P.dram_dma(wb_out[l, cbk].rearrange("p (c f) -> p c f", c=16), w_out[l].rearrange("(c p) f -> p c f", p=128)[:, :, cbk * 512:(cbk + 1) * 512])
        for cbk in range(2):
            P.dram_dma(wb_kv[l, cbk].rearrange("p (c f) -> p c f", c=16), w_kv[l].rearrange("(c p) f -> p c f", p=128)[:, :, cbk * 512:(cbk + 1) * 512])
        for fg in range(22):
            for gv in range(2):
                P.dram_dma(wb_up[l, fg].rearrange("p (g c f) -> p g c f", g=2, c=16)[:, gv], w_up[l].rearrange("(c p) f -> p c f", p=128)[:, :, gv * DFF + fg * 256:gv * DFF + (fg + 1) * 256])
        for cbk in range(4):
            for kh in range(2):
                P.dram_dma(wb_down[l, cbk * 2 + kh].rearrange("p (k f) -> p k f", k=22), w_down[l].rearrange("(k p) f -> p k f", p=128)[:, kh * 22:(kh + 1) * 22, cbk * 512:(cbk + 1) * 512])
    P.barrier()
    stages = []
    for l in range(depth):
        last = (l == depth - 1)
        for nm, L in seqs:
            sp = SPS[nm]; si = SI[nm]; tok0 = TOK0[nm]
            x0 = xin[nm] if l == 0 else xb[tok0:tok0 + L, :]
            x1 = xa[tok0:tok0 + L, :]
            x2 = yout[nm] if last else xb[tok0:tok0 + L, :]
            def stop(tag):
                return stop_after is not None and stop_after == tag
            phase_up(P, C, x0, L, gcols[l, 0], wb_in[l], 10, 512, "proj", out_ap=projT, tok0=tok0)
            if stop("proj"): break
            phase_up(P, C, mem[nm], NMEM, gcols[l, 1], wb_kv[l], 2, 512, "proj", out_ap=kvT[nm], tok0=0)
            phase_attn(P, C, projT, mixT, tok0, L, kvT=kvT[nm])
            if stop("memx"): break
            phase_attn(P, C, projT, mixT, tok0, L)
            if stop("attn"): break
            phase_filter_mlp(P, C, l, sp, zT[nm], hd3[nm])
            phase_hyprep(P, C, l, projT, tok0, sp, hv0[nm], hx1[nm], hx2[nm])
            phase_hyena(P, C, l, sp, si, hd3[nm], hv0[nm], hx1[nm], hx2[nm], mixT, tok0)
            if stop("hyena"): break
            phase_mixprep(P, C, mixT, tok0, L, gcols[l, 2], mixn)
            phase_down(P, C, mixn, tok0, L, wb_out[l], 4, 16, 512, gvecs[l, 0], x0, x1)
            if stop("mix"): break
            phase_up(P, C, x1, L, gcols[l, 3], wb_up[l], 22, 256, "ffn", out_ap=ffT, tok0=tok0, ffn=dict(fcw=fcw[l]), seq_edges=True)
            phase_down(P, C, ffT, tok0, L, wb_down[l], 8, 44, 512, gvecs[l, 1], x1, x2)
    P.barrier()
    return P


def make_consts():
    c = {}
    c["identb"] = np.eye(128).astype(NPBF); c["identf"] = np.eye(128, dtype=np.float32)
    c["onesb"] = np.ones((128, 128)).astype(NPBF); c["onesf"] = np.ones((128, 128), np.float32)
    c["maskt"] = mask_tiles().astype(NPBF)
    cs, pm = rope_tables(); c["rope_cs"] = cs; c["ropepm"] = pm
    tp_ = dft_tables(64); ts_ = dft_tables(16)
    c["dft_fa"] = tp_[0].astype(NPBF)
    c["dft_sb"] = np.stack([tp_[1], ts_[1]]).astype(NPBF)
    c["dft_R"] = np.stack([tp_[2], ts_[2]]).astype(NPBF)
    c["dft_ga"] = np.stack([tp_[3], ts_[3]]).astype(NPBF)
    c["dec_p"] = decay_tables(LP); c["dec_s"] = decay_tables(LS)
    c["zT_p"] = zfeat_T(LP); c["zT_s"] = zfeat_T(LS)
    return c


def col_layout(g):
    return np.ascontiguousarray(g.reshape(g.shape[0], -1, 128).transpose(0, 2, 1))


def host_layouts(inp):
    o = {}
    o["gcols"] = np.ascontiguousarray(np.stack([col_layout(inp["g_pre_mix"]), col_layout(inp["g_mem"]), col_layout(inp["g_grp"]), col_layout(inp["g_pre_ffn"])], 1))
    o["gvecs"] = np.ascontiguousarray(np.stack([inp["g_post_mix"], inp["g_post_ffn"]], 1))
    cwb = np.concatenate([inp["conv_w"], inp["conv_b"][:, None, :]], 1)
    o["cw"] = np.ascontiguousarray(cwb.reshape(DEPTH, 4, 3, 8, 128).transpose(0, 4, 2, 3, 1))
    fc = np.concatenate([inp["ffn_conv_w"], inp["ffn_conv_b"][:, None, :]], 1)
    o["fcw"] = np.ascontiguousarray(fc.reshape(DEPTH, 4, 44, 128).transpose(0, 3, 2, 1))
    o["fprm"] = np.ascontiguousarray(np.stack([inp["f_b1"], inp["f_b2"], inp["f_b3"], inp["f_freq"]], -1))
    for nm, L in (("p", LP), ("s", LS)):
        sp = seq_params(L)
        ng = 32 * sp["nblk"]
        ch = (np.arange(ng)[None, :] * sp["gB"] + (np.arange(128)[:, None] // sp["MJ"]))
        o["fbias_" + nm] = np.ascontiguousarray(inp["f_bias"][:, :, ch])
    for k in ("w_in", "w_out", "w_up", "w_down", "f_w1", "f_w2", "f_w3", "f_w4"):
        o[k] = inp[k]
    o["w_kv"] = inp["w_mem_kv"]
    return o


_CACHE = {}


def kernel(**inputs):
    inp = {k: np.asarray(v) for k, v in inputs.items()}
    if "prog" not in _CACHE:
        _CACHE["prog"] = build_program()
        _CACHE["consts"] = make_consts()
    P = _CACHE["prog"]
    shared = dict(_CACHE["consts"]); shared.update(host_layouts(inp))
    shared["x_p"] = np.ascontiguousarray(inp["x_prompt"][0]); shared["mem_p"] = np.ascontiguousarray(inp["mem_prompt"][0])
    in_maps = []
    for c in range(NCORES):
        m = dict(shared)
        m["x_s"] = np.ascontiguousarray(inp["x_sample"][c]); m["mem_s"] = np.ascontiguousarray(inp["mem_sample"][c])
        in_maps.append(m)
    res = run_bass_kernel_spmd(P.nc, in_maps, core_ids=list(range(NCORES)))
    yp = np.asarray(res.results[0]["y_p"], dtype=np.float32)[None]
    ys = np.stack([np.asarray(res.results[c]["y_s"], dtype=np.float32) for c in range(NCORES)], 0)
    return (yp, ys)
```

```python
import math
from contextlib import ExitStack
import numpy as np
import ml_dtypes
import concourse.bass as bass
import concourse.mybir as mybir
from concourse.bass_utils import run_bass_kernel_spmd

F32 = mybir.dt.float32
BF16 = mybir.dt.bfloat16
AF = mybir.ActivationFunctionType
ALU = mybir.AluOpType
AX = mybir.AxisListType
NPBF = ml_dtypes.bfloat16

D = 2048; DIN = 5120; DH = 1024; DFF = 5632; NMEM = 256; EPS = 1e-6
DEPTH = 4
LP = 8192; LS = 2048
NCORES = 8


class Sem:
    def __init__(self, h, idx):
        self.h = h; self.idx = idx; self.cnt = 0


class Eng:
    def __init__(self, P, name, h, same_wait):
        self.P = P; self.name = name; self.h = h; self.same_wait = same_wait
        self.sem = P.new_sem("e_" + name)
        self.seen = {}

    def wait(self, tok):
        sem, val = tok
        if sem is self.sem and not self.same_wait:
            return
        if self.seen.get(sem.idx, 0) >= val:
            return
        self.h.wait_ge(sem.h, val)
        self.seen[sem.idx] = val


class Buf:
    def __init__(self):
        self.w = {}; self.r = {}; self.dsem = None


def _merge(d, toks):
    for k, t in toks.items():
        if k not in d or d[k][1] < t[1]:
            d[k] = t


class Prog:
    def __init__(self):
        self.nc = bass.Bass("TRN2", target_bir_lowering=False)
        self.es = ExitStack()
        self.sems = []
        nc = self.nc
        self.pe = Eng(self, "pe", nc.tensor, False)
        self.act = Eng(self, "act", nc.scalar, True)
        self.dve = Eng(self, "dve", nc.vector, True)
        self.pool = Eng(self, "pool", nc.gpsimd, True)
        self.sp = Eng(self, "sp", nc.sync, False)
        self.engs = [self.pe, self.act, self.dve, self.pool, self.sp]
        self.free_dsems = []
        self.dsems = []
        self.phase_bufs = []
        self.uid = 0
        self.ddsem = self.new_sem("dd")
        self.ninst = 0

    def new_sem(self, name):
        h = self.es.enter_context(self.nc.semaphore(name))
        s = Sem(h, len(self.sems)); self.sems.append(s)
        return s

    def dram(self, name, shape, dt, kind="Internal"):
        return self.nc.dram_tensor(name, list(shape), dt, kind=kind).ap()

    def op(self, eng, fn, reads=(), writes=()):
        deps = {}
        for b in reads:
            _merge(deps, b.w)
        for b in writes:
            _merge(deps, b.w); _merge(deps, b.r)
        for t in deps.values():
            eng.wait(t)
        ins = fn(eng.h)
        eng.sem.cnt += 1
        ins.then_inc(eng.sem.h, 1)
        tok = (eng.sem, eng.sem.cnt)
        for b in reads:
            b.r[eng.sem.idx] = tok
        for b in writes:
            b.w = {eng.sem.idx: tok}; b.r = {}
        self.ninst += 1
        return tok

    def pe_group(self, fns, reads=(), writes=()):
        eng = self.pe
        deps = {}
        for b in reads:
            _merge(deps, b.w)
        for b in writes:
            _merge(deps, b.w); _merge(deps, b.r)
        for t in deps.values():
            eng.wait(t)
        for fn in fns[:-1]:
            fn(eng.h); self.ninst += 1
        ins = fns[-1](eng.h)
        eng.sem.cnt += 1
        ins.then_inc(eng.sem.h, 1)
        tok = (eng.sem, eng.sem.cnt)
        for b in reads:
            b.r[eng.sem.idx] = tok
        for b in writes:
            b.w = {eng.sem.idx: tok}; b.r = {}
        self.ninst += 1
        return tok

    def _dsem(self, b):
        if b.dsem is None:
            if self.free_dsems:
                b.dsem = self.free_dsems.pop()
            else:
                b.dsem = self.new_sem("d%d" % len(self.dsems))
                self.dsems.append(b.dsem)
            self.phase_bufs.append(b)
        return b.dsem

    def load(self, buf, pairs, q=None, extra_reads=()):
        q = q or self.sp
        s = self._dsem(buf)
        deps = {}
        _merge(deps, buf.w); _merge(deps, buf.r)
        for b in extra_reads:
            _merge(deps, b.w)
        for t in deps.values():
            q.wait(t)
        for o, i in pairs:
            q.h.dma_start(out=o, in_=i).then_inc(s.h, 16)
            s.cnt += 16
            self.ninst += 1
        buf.w = {s.idx: (s, s.cnt)}; buf.r = {}

    def store(self, bufs, pairs, q=None):
        q = q or self.pool
        s = self._dsem(bufs[0])
        deps = {}
        for b in bufs:
            _merge(deps, b.w)
        for t in deps.values():
            q.wait(t)
        for o, i in pairs:
            q.h.dma_start(out=o, in_=i).then_inc(s.h, 16)
            s.cnt += 16
            self.ninst += 1
        for b in bufs:
            b.r[s.idx] = (s, s.cnt)

    def dram_dma(self, o, i, q=None):
        q = q or self.pool
        q.h.dma_start(out=o, in_=i).then_inc(self.ddsem.h, 16)
        self.ddsem.cnt += 16

    def barrier(self):
        toks = [(e.sem, e.sem.cnt) for e in self.engs]
        toks += [(s, s.cnt) for s in self.dsems]
        toks.append((self.ddsem, self.ddsem.cnt))
        for e in self.engs:
            for t in toks:
                if t[1] > 0:
                    e.wait(t)

    class _Phase:
        def __init__(self, P):
            self.P = P; self.es = ExitStack()

        def __enter__(self):
            self.es.__enter__(); return self

        def sb(self, name, shape, dt):
            self.P.uid += 1
            return self.es.enter_context(self.P.nc.sbuf_tensor("%s_%d" % (name, self.P.uid), list(shape), dt))

        def ps(self, name, shape, dt):
            self.P.uid += 1
            return self.es.enter_context(self.P.nc.psum_tensor("%s_%d" % (name, self.P.uid), list(shape), dt))

        def __exit__(self, *a):
            P = self.P
            P.barrier()
            for b in P.phase_bufs:
                P.free_dsems.append(b.dsem); b.dsem = None
            P.phase_bufs = []
            return self.es.__exit__(*a)

    def phase(self):
        return Prog._Phase(self)


def mm(out, lhsT, rhs, st, sp_):
    return lambda h: h.matmul(out, lhsT, rhs, start=st, stop=sp_)


def tpz(out, in_, ident):
    return lambda h: h.transpose(out=out, in_=in_, identity=ident)


def bufs(n):
    return [Buf() for _ in range(n)]


def seq_params(L):
    nJ = L // 128
    return dict(L=L, nJ=nJ, MJ=2 * nJ, gB=128 // (2 * nJ), gA=128 // nJ, CB=32 * (128 // (2 * nJ)), nblk=DH // (32 * (128 // (2 * nJ))))


def dft_tables(nJ):
    MJ = 2 * nJ; gB = 128 // MJ
    a = np.arange(128)[:, None]; f = np.arange(128)[None, :]
    thj = -2 * np.pi / 255
    Ffw = np.exp(1j * thj * (a * f)); Fbw = np.exp(1j * thj * ((127 - a) * f))
    fa = np.stack([np.concatenate([Ffw.real, Ffw.imag], 1), np.concatenate([Fbw.real, Fbw.imag], 1)])
    B = np.arange(nJ)[:, None]; fJ = np.arange(MJ)[None, :]
    thJ = -2 * np.pi / MJ
    Jfw = np.exp(1j * thJ * (B * fJ)); Jbw = np.exp(1j * thJ * (-(B + 1) * fJ))
    sb = np.zeros((2, 2, 3, 128, 128))
    for d, Jm in enumerate((Jfw, Jbw)):
        for h in range(2):
            for c in range(gB):
                r0 = 64 * h + c * nJ; m0 = c * MJ
                sb[d, h, 0, r0:r0 + nJ, m0:m0 + MJ] = Jm.real
                sb[d, h, 1, r0:r0 + nJ, m0:m0 + MJ] = Jm.imag
                sb[d, h, 2, r0:r0 + nJ, m0:m0 + MJ] = -Jm.imag
    I = np.arange(nJ)[None, :]; fJc = np.arange(MJ)[:, None]
    GJlo = np.exp(1j * thJ * (-(I * fJc))); GJhi = np.exp(1j * thJ * (-((I - 1) * fJc)))
    Gre = np.zeros((128, 2, gB * nJ)); Gim = np.zeros((128, 2, gB * nJ))
    for c in range(gB):
        for k, Gm in enumerate((GJlo, GJhi)):
            Gre[c * MJ:(c + 1) * MJ, k, c * nJ:(c + 1) * nJ] = Gm.real
            Gim[c * MJ:(c + 1) * MJ, k, c * nJ:(c + 1) * nJ] = Gm.imag
    Gre = Gre.reshape(128, 128); Gim = Gim.reshape(128, 128)
    R = np.stack([np.concatenate([Gre, Gim], 1), np.concatenate([-Gim, Gre], 1), np.concatenate([-Gre, -Gim], 1)])
    cf = np.where(np.arange(128) == 0, 1.0, 2.0)[:, None] / (MJ * 255.0)
    e = np.arange(255)[None, :]; ff = np.arange(128)[:, None]
    th = 2 * np.pi * e * ff / 255
    gre = cf * np.cos(th); gim = -cf * np.sin(th)
    z1 = np.zeros((128, 1))
    ga = np.stack([gre[:, :128], gim[:, :128], np.concatenate([gre[:, 128:], z1], 1), np.concatenate([gim[:, 128:], z1], 1)])
    return fa, sb.reshape(12, 128, 128), R, ga


def decay_tables(L):
    sp = seq_params(L); nJ = sp["nJ"]; CB = sp["CB"]; nblk = sp["nblk"]
    t = np.linspace(0.0, 1.0, L, dtype=np.float32)
    deltas = np.abs(np.linspace(math.log(1e-2) / 1.5, math.log(1e-2) / 0.3, DH, dtype=np.float32))
    dec = (np.exp(-t[:, None] * deltas[None, :]) + np.float32(0.05)).astype(np.float32)
    decb = np.concatenate([dec[1:], np.zeros((1, DH), np.float32)], 0)
    out = []
    for dd in (dec, decb):
        x = dd.reshape(nJ, 128, nblk, CB).transpose(1, 2, 0, 3)
        out.append(np.ascontiguousarray(x))
    return np.stack(out)


def zfeat_T(L):
    t = np.linspace(0.0, 1.0, L, dtype=np.float32)[:, None]
    w = (np.float32(2.0 * math.pi) * np.arange(L, dtype=np.float32) / np.float32(L)).astype(np.float32)
    f = np.linspace(1e-4, 15, 16, dtype=np.float32)
    ang = (w[:, None] * f[None, :]).astype(np.float32)
    z = np.concatenate([t, np.cos(ang), -np.sin(ang)], -1).astype(np.float32)
    return np.ascontiguousarray(z.T)


def rope_tables():
    inv = np.exp(np.float32(-math.log(500000.0)) * np.arange(0, 32, 2, dtype=np.float32) / np.float32(32)).astype(np.float32)
    ang = (np.arange(LP, dtype=np.float32)[:, None] * inv[None, :]).astype(np.float32)
    c = np.cos(ang).T; s = np.sin(ang).T
    C = np.concatenate([c, c], 0); S = np.concatenate([s, s], 0)
    pm = np.zeros((32, 32), np.float32)
    for m in range(16):
        pm[m + 16, m] = -1.0; pm[m, m + 16] = 1.0
    return np.ascontiguousarray(np.stack([C, S]).astype(np.float32)), pm


def mask_tiles():
    out = np.zeros((128, 20, 512), np.float32)
    tk = np.arange(128)[:, None]; tq = np.arange(512)[None, :]
    for o in range(20):
        dlt = (-1024 + 128 * o + tk) - tq
        m = np.zeros_like(dlt, dtype=np.float32)
        for dil in (1, 4, 16):
            m += ((dlt % dil == 0) & (np.abs(dlt) <= 64 * dil)).astype(np.float32)
        out[:, o, :] = m
    return out


class Ctx:
    pass


def rstd_from_ss(P, ss_ap, ss_buf, n):
    P.op(P.dve, lambda h: h.tensor_scalar(out=ss_ap, in0=ss_ap, scalar1=1.0 / n, scalar2=EPS, op0=ALU.mult, op1=ALU.add), [ss_buf], [ss_buf])
    P.op(P.act, lambda h: h.activation(out=ss_ap, in_=ss_ap, func=AF.Sqrt), [ss_buf], [ss_buf])
    P.op(P.dve, lambda h: h.reciprocal(out=ss_ap, in_=ss_ap), [ss_buf], [ss_buf])


def phase_up(P, C, x_ap, ntok, gcol_ap, wblk_ap, nblk, BW, kind, out_ap=None, tok0=0, ffn=None, seq_edges=None):
    TT = min(512, ntok); ns = TT // 128
    if kind == "ffn":
        starts = list(range(0, ntok - TT, TT - 2)) + [ntok - TT]
    else:
        starts = list(range(0, ntok, TT))
    ntiles = len(starts)
    with P.phase() as ph:
        xt = ph.sb("xt", [128, ns, D], F32); xb = bufs(ns)
        sq = ph.sb("sq", [128, D], BF16); sqb = Buf()
        ss = ph.sb("ss", [128, ns], F32); ssb = bufs(ns)
        hb = ph.sb("hb", [128, ns, D], BF16); hbb = bufs(ns)
        hT = ph.sb("hT", [128, 16, TT], BF16); hTb = Buf()
        gcol = ph.sb("gcol", [128, 16], F32); gcb = Buf()
        wsz = 16 * BW if kind == "proj" else 2 * 16 * 256
        wb = ph.sb("wb", [128, 2, wsz], BF16); wbb = bufs(2)
        nfb = BW // 128 if kind == "proj" else 2
        ost = ph.sb("ost", [128, 2, nfb, TT], F32 if kind == "proj" else BF16); ostb = bufs(2)
        tp = [ph.ps("tp%d" % i, [128, 8, 128], BF16) for i in range(2)]; tpb = bufs(2)
        acc = [ph.ps("acc%d" % i, [128, 512], F32) for i in range(4)]; accb = bufs(4)
        if kind == "ffn":
            ue = ph.sb("ue", [128, 2, TT + 2], F32); ueb = bufs(2)
            tt_ = ph.sb("tt", [128, 2, TT], F32); ttb = bufs(2)
            fcw = ph.sb("fcw", [128, 44, 4], F32); fcwb = Buf()
            P.load(fcwb, [(fcw[:], ffn["fcw"])])
            for u in range(2):
                P.op(P.dve, lambda h: h.memset(ue[:, u, :], 0.0), [], [ueb[u]])
        P.load(gcb, [(gcol[:], gcol_ap)])
        widx = 0
        nwb = nblk

        def wload(i):
            P.load(wbb[i % 2], [(wb[:, i % 2, :], wblk_ap[i % nwb])])
        total_w = ntiles * nwb
        wload(0)
        acci = 0; osti = 0; evi = 0
        for ti, t0 in enumerate(starts):
            P.load(xb[0], [(xt[:], x_ap[t0:t0 + TT, :].rearrange("(s p) d -> p s d", p=128))])
            for s in range(ns):
                P.op(P.act, lambda h: h.activation(out=sq[:], in_=xt[:, s, :], func=AF.Square, accum_out=ss[:, s:s + 1]), [xb[0]], [sqb, ssb[s]])
                rstd_from_ss(P, ss[:, s:s + 1], ssb[s], D)
                P.op(P.dve, lambda h: h.tensor_scalar(out=hb[:, s, :], in0=xt[:, s, :], scalar1=ss[:, s:s + 1], scalar2=None, op0=ALU.mult), [xb[0], ssb[s]], [hbb[s]])
            for s in range(ns):
                for c8 in range(2):
                    k = (s * 2 + c8) % 2
                    P.pe_group([tpz(tp[k][:, c, :], hb[:, s, (c8 * 8 + c) * 128:(c8 * 8 + c + 1) * 128], C.identb[:]) for c in range(8)], [hbb[s]], [tpb[k]])
                    gv = gcol[:, c8 * 8:c8 * 8 + 8].unsqueeze(2).to_broadcast([128, 8, 128])
                    P.op(P.dve, lambda h: h.tensor_tensor(out=hT[:, c8 * 8:c8 * 8 + 8, s * 128:(s + 1) * 128], in0=tp[k][:], in1=gv, op=ALU.mult), [tpb[k], gcb], [hTb])
            for bi in range(nwb):
                if widx + 1 < total_w:
                    wload(widx + 1)
                wcur = widx % 2
                widx += 1
                if kind == "proj":
                    wv = wb[:, wcur, :].rearrange("p (c f) -> p c f", c=16)
                    for fb in range(BW // 128):
                        a = acci % 4; acci += 1
                        P.pe_group([mm(acc[a][:, 0:TT], wv[:, kc, fb * 128:(fb + 1) * 128], hT[:, kc, 0:TT], kc == 0, kc == 15) for kc in range(16)], [wbb[wcur], hTb], [accb[a]])
                        o = osti % 2
                        eng = P.act if (fb % 2 == 0) else P.dve
                        if eng is P.act:
                            P.op(eng, lambda h: h.activation(out=ost[:, o, fb, :], in_=acc[a][:, 0:TT], func=AF.Copy), [accb[a]], [ostb[o]])
                        else:
                            P.op(eng, lambda h: h.tensor_copy(out=ost[:, o, fb, :], in_=acc[a][:, 0:TT]), [accb[a]], [ostb[o]])
                    o = osti % 2; osti += 1
                    P.store([ostb[o]], [(out_ap[bi * BW:(bi + 1) * BW, tok0 + t0:tok0 + t0 + TT].rearrange("(f p) t -> p f t", p=128), ost[:, o, :, :])])
                else:
                    wv = wb[:, wcur, :].rearrange("p (g c f) -> p g c f", g=2, c=16)
                    c_lo = 0 if ti == 0 else 1
                    c_hi = TT if ti == ntiles - 1 else TT - 1
                    for fb in range(2):
                        fidx = bi * 2 + fb
                        ag = acci % 4; av = (acci + 1) % 4; acci += 2
                        P.pe_group([mm(acc[ag][:, 0:TT], wv[:, 0, kc, fb * 128:(fb + 1) * 128], hT[:, kc, 0:TT], kc == 0, kc == 15) for kc in range(16)], [wbb[wcur], hTb], [accb[ag]])
                        P.pe_group([mm(acc[av][:, 0:TT], wv[:, 1, kc, fb * 128:(fb + 1) * 128], hT[:, kc, 0:TT], kc == 0, kc == 15) for kc in range(16)], [wbb[wcur], hTb], [accb[av]])
                        u = fidx % 2
                        P.op(P.act, lambda h: h.activation(out=ue[:, u, 1:TT + 1], in_=acc[ag][:, 0:TT], func=AF.Copy), [accb[ag]], [ueb[u]])
                        P.op(P.act, lambda h: h.activation(out=tt_[:, u, :], in_=ue[:, u, 1:TT + 1], func=AF.Identity, scale=fcw[:, fidx, 1:2], bias=fcw[:, fidx, 3:4]), [ueb[u], fcwb], [ttb[u]])
                        P.op(P.dve, lambda h: h.scalar_tensor_tensor(out=tt_[:, u, :], in0=ue[:, u, 0:TT], scalar=fcw[:, fidx, 0:1], in1=tt_[:, u, :], op0=ALU.mult, op1=ALU.add), [ueb[u], fcwb, ttb[u]], [ttb[u]])
                        P.op(P.dve, lambda h: h.scalar_tensor_tensor(out=tt_[:, u, :], in0=ue[:, u, 2:TT + 2], scalar=fcw[:, fidx, 2:3], in1=tt_[:, u, :], op0=ALU.mult, op1=ALU.add), [ueb[u], fcwb, ttb[u]], [ttb[u]])
                        P.op(P.act, lambda h: h.activation(out=tt_[:, u, :], in_=tt_[:, u, :], func=AF.Gelu_apprx_tanh), [ttb[u]], [ttb[u]])
                        o = osti % 2
                        P.op(P.dve, lambda h: h.tensor_tensor(out=ost[:, o, fb, :], in0=tt_[:, u, :], in1=acc[av][:, 0:TT], op=ALU.mult), [ttb[u], accb[av]], [ostb[o]])
                    o = osti % 2; osti += 1
                    P.store([ostb[o]], [(out_ap[bi * 256:(bi + 1) * 256, tok0 + t0 + c_lo:tok0 + t0 + c_hi].rearrange("(f p) t -> p f t", p=128), ost[:, o, :, c_lo:c_hi])])


def phase_down(P, C, aT_ap, tok0, ntok, wblk_ap, ncb, KC, BW, gbc_ap, xres_ap, xout_ap):
    TT = 512; ntiles = ntok // TT
    KH = KC // (ncb // 4)
    nkh = KC // KH
    aview = aT_ap.rearrange("(kc p) t -> p kc t", p=128)
    with P.phase() as ph:
        aT = ph.sb("aT", [128, KC, TT], BF16); aTb = Buf()
        wb = ph.sb("wb", [128, 2, KH * 512], BF16); wbb = bufs(2)
        yo = ph.sb("yo", [128, 4, D], F32); yob = bufs(4)
        xr = ph.sb("xr", [128, 4, D], F32); xrb = Buf()
        gbc = ph.sb("gbc", [128, D], F32); gbb = Buf()
        sq = ph.sb("sq", [128, D], BF16); sqb = Buf()
        ss = ph.sb("ss", [128, 4], F32); ssb = bufs(4)
        acc = [ph.ps("acc%d" % i, [128, 512], F32) for i in range(8)]; accb = bufs(8)
        P.load(gbb, [(gbc[:], gbc_ap.partition_broadcast(128))])
        total_w = ntiles * ncb
        P.load(wbb[0], [(wb[:, 0, :], wblk_ap[0])])
        widx = 0; xi = 0; ai = 0
        for ti in range(ntiles):
            t0 = ti * TT
            P.load(aTb, [(aT[:], aview[:, :, tok0 + t0:tok0 + t0 + TT])])
            for cb in range(4):
                ab = (ai % 2) * 4; ai += 1
                for kh in range(nkh):
                    if widx + 1 < total_w:
                        P.load(wbb[(widx + 1) % 2], [(wb[:, (widx + 1) % 2, :], wblk_ap[(widx + 1) % ncb])])
                    wcur = widx % 2; widx += 1
                    wv = wb[:, wcur, :].rearrange("p (k f) -> p k f", k=KH)
                    for s in range(4):
                        P.pe_group([mm(acc[ab + s][:], aT[:, kh * KH + kc, s * 128:(s + 1) * 128], wv[:, kc, :], kh == 0 and kc == 0, kh == nkh - 1 and kc == KH - 1) for kc in range(KH)], [aTb, wbb[wcur]], [accb[ab + s]])
                for s in range(4):
                    if s % 2 == 0:
                        P.op(P.act, lambda h: h.activation(out=yo[:, s, cb * 512:(cb + 1) * 512], in_=acc[ab + s][:], func=AF.Copy), [accb[ab + s]], [yob[s]])
                    else:
                        P.op(P.dve, lambda h: h.tensor_copy(out=yo[:, s, cb * 512:(cb + 1) * 512], in_=acc[ab + s][:]), [accb[ab + s]], [yob[s]])
            P.load(xrb, [(xr[:], xres_ap[t0:t0 + TT, :].rearrange("(s p) d -> p s d", p=128))])
            for s in range(4):
                P.op(P.act, lambda h: h.activation(out=sq[:], in_=yo[:, s, :], func=AF.Square, accum_out=ss[:, s:s + 1]), [yob[s]], [sqb, ssb[s]])
                rstd_from_ss(P, ss[:, s:s + 1], ssb[s], D)
                P.op(P.dve, lambda h: h.scalar_tensor_tensor(out=yo[:, s, :], in0=yo[:, s, :], scalar=ss[:, s:s + 1], in1=gbc[:], op0=ALU.mult, op1=ALU.mult), [yob[s], ssb[s], gbb], [yob[s]])
                P.op(P.dve, lambda h: h.tensor_tensor(out=yo[:, s, :], in0=yo[:, s, :], in1=xr[:, s, :], op=ALU.add), [yob[s], xrb], [yob[s]])
            P.store(yob, [(xout_ap[t0:t0 + TT, :].rearrange("(s p) d -> p s d", p=128), yo[:])])


def phase_mixprep(P, C, mixT_ap, tok0, ntok, ggcol_ap, mixn_ap):
    TT = 512; ntiles = ntok // TT
    mview = mixT_ap.rearrange("(c p) t -> p c t", p=128)
    oview = mixn_ap.rearrange("(c p) t -> p c t", p=128)
    groups = [(0, 8), (8, 12), (12, 16)]
    with P.phase() as ph:
        mx = ph.sb("mx", [128, 16, TT], F32); mxb = Buf()
        sq = ph.sb("sq", [128, 4, TT], BF16); sqb = bufs(4)
        rs = ph.sb("rs", [128, 3, TT], F32); rsb = bufs(3)
        mo = ph.sb("mo", [128, 16, TT], BF16); mob = Buf()
        gg = ph.sb("gg", [128, 16], F32); ggb = Buf()
        acc = [ph.ps("acc%d" % i, [128, 512], F32) for i in range(3)]; accb = bufs(3)
        P.load(ggb, [(gg[:], ggcol_ap)])
        qi = 0
        for ti in range(ntiles):
            t0 = tok0 + ti * TT
            P.load(mxb, [(mx[:], mview[:, :, t0:t0 + TT])])
            for gi, (c0, c1) in enumerate(groups):
                for c in range(c0, c1):
                    q = qi % 4; qi += 1
                    P.op(P.act, lambda h: h.activation(out=sq[:, q, :], in_=mx[:, c, :], func=AF.Square), [mxb], [sqb[q]])
                    P.op(P.pe, lambda h: h.matmul(acc[gi][:], C.onesb[:], sq[:, q, :], start=(c == c0), stop=(c == c1 - 1)), [sqb[q]], [accb[gi]])
                n = (c1 - c0) * 128
                P.op(P.dve, lambda h: h.tensor_scalar(out=rs[:, gi, :], in0=acc[gi][:], scalar1=1.0 / n, scalar2=EPS, op0=ALU.mult, op1=ALU.add), [accb[gi]], [rsb[gi]])
                P.op(P.act, lambda h: h.activation(out=rs[:, gi, :], in_=rs[:, gi, :], func=AF.Sqrt), [rsb[gi]], [rsb[gi]])
                P.op(P.dve, lambda h: h.reciprocal(out=rs[:, gi, :], in_=rs[:, gi, :]), [rsb[gi]], [rsb[gi]])
                for c in range(c0, c1):
                    P.op(P.dve, lambda h: h.scalar_tensor_tensor(out=mo[:, c, :], in0=mx[:, c, :], scalar=gg[:, c:c + 1], in1=rs[:, gi, :], op0=ALU.mult, op1=ALU.mult), [mxb, ggb, rsb[gi]], [mob])
            P.store([mob], [(oview[:, :, t0:t0 + TT], mo[:])])


def attn_core(P, C, ph, qr, qrb, kr, krb, vt, vtb, Lq, Lk, masked, mixT_ap, row0, tok0, st):
    sc = 1.0 / math.sqrt(128.0)
    for q0 in range(0, Lq, 512):
        if masked:
            kbs = [(kb, (kb * 128 - q0 + 1024) // 128) for kb in range(Lk // 128) if -1024 <= kb * 128 - q0 < 1536]
        else:
            kbs = [(kb, None) for kb in range(Lk // 128)]
        oi = st["oi"] % 2; st["oi"] += 1
        for n, (kb, mo) in enumerate(kbs):
            si = st["si"] % 2; st["si"] += 1
            P.op(P.pe, lambda h: h.matmul(st["sps"][si][:], kr[:, kb * 128:(kb + 1) * 128], qr[:, q0:q0 + 512], start=True, stop=True), [krb, qrb], [st["spsb"][si]])
            pi = st["pi"] % 3; st["pi"] += 1
            if masked:
                P.op(P.act, lambda h: h.activation(out=st["pe_"][:, pi, :], in_=st["sps"][si][:], func=AF.Exp, scale=sc), [st["spsb"][si]], [st["peb"][pi]])
                P.op(P.dve, lambda h: h.tensor_tensor(out=st["pm"][:, pi, :], in0=st["pe_"][:, pi, :], in1=C.mask[:, mo, :], op=ALU.mult), [st["peb"][pi], C.maskb], [st["pmb"][pi]])
            else:
                P.op(P.act, lambda h: h.activation(out=st["pm"][:, pi, :], in_=st["sps"][si][:], func=AF.Exp, scale=sc), [st["spsb"][si]], [st["pmb"][pi]])
            P.op(P.pe, lambda h: h.matmul(st["ops"][oi][:], vt[:, kb, :], st["pm"][:, pi, :], start=(n == 0), stop=(n == len(kbs) - 1)), [vtb, st["pmb"][pi]], [st["opsb"][oi]])
            P.op(P.pe, lambda h: h.matmul(st["dps"][oi][:], C.onesb[:], st["pm"][:, pi, :], start=(n == 0), stop=(n == len(kbs) - 1)), [st["pmb"][pi]], [st["dpsb"][oi]])
        r = st["ri"] % 2; st["ri"] += 1
        P.op(P.dve, lambda h: h.reciprocal(out=st["rc"][:, r, :], in_=st["dps"][oi][:]), [st["dpsb"][oi]], [st["rcb"][r]])
        P.op(P.dve, lambda h: h.tensor_tensor(out=st["yo"][:, r, :], in0=st["ops"][oi][:], in1=st["rc"][:, r, :], op=ALU.mult), [st["opsb"][oi], st["rcb"][r]], [st["yob"][r]])
        P.store([st["yob"][r]], [(mixT_ap[row0:row0 + 128, tok0 + q0:tok0 + q0 + 512], st["yo"][:, r, :])])


def attn_state(ph):
    st = dict(oi=0, si=0, pi=0, ri=0)
    st["sps"] = [ph.ps("sps%d" % i, [128, 512], F32) for i in range(2)]; st["spsb"] = bufs(2)
    st["ops"] = [ph.ps("ops%d" % i, [128, 512], F32) for i in range(2)]; st["opsb"] = bufs(2)
    st["dps"] = [ph.ps("dps%d" % i, [128, 512], F32) for i in range(2)]; st["dpsb"] = bufs(2)
    st["pe_"] = ph.sb("pex", [128, 3, 512], BF16); st["peb"] = bufs(3)
    st["pm"] = ph.sb("pmx", [128, 3, 512], BF16); st["pmb"] = bufs(3)
    st["rc"] = ph.sb("rcx", [128, 2, 512], F32); st["rcb"] = bufs(2)
    st["yo"] = ph.sb("yox", [128, 2, 512], F32); st["yob"] = bufs(2)
    return st


def load_feat_bf16(P, C, ph, src_ap, row0, tok0, L, dst, dstb, stage, stageb, rope=None, tpv=None):
    for ci, c0 in enumerate(range(0, L, 2048)):
        n = min(2048, L - c0)
        k = C.stg_i % 2; C.stg_i += 1
        P.load(stageb[k], [(stage[:, k, 0:n], src_ap[row0:row0 + 128, tok0 + c0:tok0 + c0 + n])])
        if tpv is not None:
            tps, tpsb, vb16, vb16b = tpv
            P.op(P.act, lambda h: h.activation(out=vb16[:, 0:n], in_=stage[:, k, 0:n], func=AF.Copy), [stageb[k]], [vb16b])
            for j in range(n // 128):
                kk = C.tp_i % 2; C.tp_i += 1
                P.op(P.pe, lambda h: h.transpose(out=tps[kk], in_=vb16[:, j * 128:(j + 1) * 128], identity=C.identb[:]), [vb16b], [tpsb[kk]])
                P.op(P.dve, lambda h: h.tensor_copy(out=dst[:, c0 // 128 + j, :], in_=tps[kk]), [tpsb[kk]], [dstb])
            continue
        if rope is None:
            P.op(P.act, lambda h: h.activation(out=dst[:, c0:c0 + n], in_=stage[:, k, 0:n], func=AF.Copy), [stageb[k]], [dstb])
        else:
            rps, rpsb, cs, csb, t1, t1b = rope
            P.op(P.act, lambda h: h.activation(out=dst[:, c0:c0 + n], in_=stage[:, k, 0:n], func=AF.Copy), [stageb[k]], [dstb])
            P.load(csb, [(cs[:, :, 0:n], C.rope_cs[:, :, c0:c0 + n].rearrange("t p n -> p t n"))])
            for j in range(n // 512):
                kk = C.tp_i % 2; C.tp_i += 1
                cc = c0 + j * 512
                P.op(P.pe, lambda h: h.matmul(rps[kk][0:32, :], C.ropepm[:], stage[0:32, k, j * 512:(j + 1) * 512], start=True, stop=True), [stageb[k]], [rpsb[kk]])
                P.op(P.dve, lambda h: h.tensor_tensor(out=t1[:, 0, :], in0=rps[kk][0:32, :], in1=cs[:, 1, j * 512:(j + 1) * 512], op=ALU.mult), [rpsb[kk], csb], [t1b])
                P.op(P.dve, lambda h: h.tensor_tensor(out=t1[:, 1, :], in0=stage[0:32, k, j * 512:(j + 1) * 512], in1=cs[:, 0, j * 512:(j + 1) * 512], op=ALU.mult), [stageb[k], csb], [t1b])
                P.op(P.dve, lambda h: h.tensor_tensor(out=dst[0:32, cc:cc + 512], in0=t1[:, 0, :], in1=t1[:, 1, :], op=ALU.add), [t1b], [dstb])


def phase_attn(P, C, projT, mixT, tok0, L, kvT=None):
    memx = kvT is not None
    Lk = NMEM if memx else L
    with P.phase() as ph:
        stage = ph.sb("stage", [128, 2, 2048], F32); stageb = bufs(2)
        qr = ph.sb("qr", [128, L], BF16); qrb = Buf()
        kr = ph.sb("kr", [128, Lk], BF16); krb = Buf()
        vt = ph.sb("vt", [128, Lk // 128, 128], BF16); vtb = Buf()
        vb16 = ph.sb("vb16", [128, 2048], BF16); vb16b = Buf()
        misc = ph.ps("misc", [128, 512], F32); miscb = Buf()
        tps = [misc[:, 0:64].bitcast(BF16)] * 2; tpsb = [miscb] * 2
        st = attn_state(ph)
        if not memx:
            rps = [misc] * 2; rpsb = [miscb] * 2
            mask = ph.sb("mask", [128, 20, 512], BF16); maskb = Buf()
            P.load(maskb, [(mask[:], C.mask_d)])
            C.mask = mask; C.maskb = maskb
            cs = ph.sb("cs", [32, 2, 2048], F32); csb = Buf()
            t1 = ph.sb("t1", [32, 2, 512], F32); t1b = Buf()
            rope = (rps, rpsb, cs, csb, t1, t1b)
        for hd in range(4):
            if memx:
                load_feat_bf16(P, C, ph, projT, 4608 + hd * 128, tok0, L, qr, qrb, stage, stageb)
                load_feat_bf16(P, C, ph, kvT, hd * 128, 0, NMEM, kr, krb, stage, stageb)
                load_feat_bf16(P, C, ph, kvT, 512 + hd * 128, 0, NMEM, vt, vtb, stage, stageb, tpv=(tps, tpsb, vb16, vb16b))
                attn_core(P, C, ph, qr, qrb, kr, krb, vt, vtb, L, NMEM, False, mixT, 1536 + hd * 128, tok0, st)
            else:
                load_feat_bf16(P, C, ph, projT, 3072 + hd * 128, tok0, L, qr, qrb, stage, stageb, rope=rope)
                load_feat_bf16(P, C, ph, projT, 3584 + hd * 128, tok0, L, kr, krb, stage, stageb, rope=rope)
                load_feat_bf16(P, C, ph, projT, 4096 + hd * 128, tok0, L, vt, vtb, stage, stageb, tpv=(tps, tpsb, vb16, vb16b))
                attn_core(P, C, ph, qr, qrb, kr, krb, vt, vtb, L, L, True, mixT, 1024 + hd * 128, tok0, st)


def phase_filter_mlp(P, C, l, sp, zT_ap, hd3_ap):
    L = sp["L"]
    with P.phase() as ph:
        prm = ph.sb("prm", [64, 8], F32); prmb = Buf()
        w1 = ph.sb("w1", [33, 64], F32); w2 = ph.sb("w2", [64, 2, 64], F32); wb_ = Buf()
        zt = ph.sb("zt", [33, 2, 512], F32); ztb = bufs(2)
        h = ph.sb("h", [64, 3, 512], F32); hbf = bufs(3)
        t = ph.sb("t", [64, 2, 512], F32); tb = bufs(2)
        ho = ph.sb("ho", [64, 2, 512], BF16); hob = bufs(2)
        zz = ph.sb("zz", [64, 128], BF16); zzb = Buf()
        ps_ = [ph.ps("mps%d" % i, [64, 512], F32) for i in range(2)]; psb = bufs(2)
        P.load(prmb, [(prm[:, 0:4], C.fprm[l])])
        P.load(wb_, [(w1[:], C.f_w1[l]), (w2[:, 0, :], C.f_w2[l]), (w2[:, 1, :], C.f_w3[l])])
        for j in range(3):
            P.op(P.dve, lambda h_: h_.scalar_tensor_tensor(out=prm[:, 4 + j:5 + j], in0=prm[:, j:j + 1], scalar=1.0 / 9.0, in1=prm[:, 3:4], op0=ALU.mult, op1=ALU.mult), [prmb], [prmb])
        P.op(P.dve, lambda h_: h_.tensor_scalar(out=prm[:, 7:8], in0=prm[:, 3:4], scalar1=1.0 / 9.0, scalar2=None, op0=ALU.mult), [prmb], [prmb])
        P.op(P.dve, lambda h_: h_.memset(zz[:], 0.0), [], [zzb])
        P.store([zzb], [(hd3_ap[:, L:L + 128], zz[:])])
        pi = 0
        for ti in range(L // 512):
            c0 = ti * 512
            k = ti % 2
            P.load(ztb[k], [(zt[:, k, :], zT_ap[:, c0:c0 + 512])])
            for lay in range(3):
                p = pi % 2; pi += 1
                if lay == 0:
                    P.op(P.pe, lambda h_: h_.matmul(ps_[p][:], w1[:], zt[:, k, :], start=True, stop=True), [wb_, ztb[k]], [psb[p]])
                else:
                    P.op(P.pe, lambda h_: h_.matmul(ps_[p][:], w2[:, lay - 1, :], h[:, lay - 1, :], start=True, stop=True), [wb_, hbf[lay - 1]], [psb[p]])
                P.op(P.act, lambda h_: h_.activation(out=h[:, lay, :], in_=ps_[p][:], func=AF.Sin, scale=prm[:, 7:8], bias=prm[:, 4 + lay:5 + lay]), [psb[p], prmb], [hbf[lay]])
                for rep in range(2):
                    q = rep
                    P.op(P.dve, lambda h_: h_.tensor_tensor(out=t[:, q, :], in0=h[:, lay, :], in1=h[:, lay, :], op=ALU.mult), [hbf[lay]], [tb[q]])
                    P.op(P.dve, lambda h_: h_.tensor_scalar(out=t[:, q, :], in0=t[:, q, :], scalar1=-4.0, scalar2=3.0, op0=ALU.mult, op1=ALU.add), [tb[q]], [tb[q]])
                    P.op(P.dve, lambda h_: h_.tensor_tensor(out=h[:, lay, :], in0=h[:, lay, :], in1=t[:, q, :], op=ALU.mult), [hbf[lay], tb[q]], [hbf[lay]])
            P.op(P.act, lambda h_: h_.activation(out=ho[:, k, :], in_=h[:, 2, :], func=AF.Copy), [hbf[2]], [hob[k]])
            P.store([hob[k]], [(hd3_ap[:, c0:c0 + 512], ho[:, k, :])])


def phase_hyprep(P, C, l, projT, tok0, sp, hv0, hx1, hx2):
    L = sp["L"]; nJ = sp["nJ"]; CB = sp["CB"]; nbg = 128 // CB
    with P.phase() as ph:
        xin = ph.sb("xin", [128, L + 2], F32); xinb = Buf()
        xc = ph.sb("xc", [128, L], F32); xcb = Buf()
        tk = ph.sb("tk", [128, 128, nJ], F32); tkb = Buf()
        tkh = ph.sb("tkh", [128, 128, nJ], BF16); tkhb = Buf()
        cw = ph.sb("cw", [128, 3, 8, 4], F32); cwb = Buf()
        tps = [ph.ps("tpf%d" % i, [128, 4, 128], F32) for i in range(2)]; tpsb = bufs(2)
        P.load(cwb, [(cw[:], C.cw[l])])
        P.op(P.dve, lambda h: h.memset(xin[:, 0:1], 0.0), [], [xinb])
        P.op(P.dve, lambda h: h.memset(xin[:, L + 1:L + 2], 0.0), [], [xinb])
        ti_ = 0
        for tn in range(3):
            for cg in range(8):
                row0 = tn * 1024 + cg * 128
                P.load(xinb, [(xin[:, 1 + c0:1 + c0 + 2048], projT[row0:row0 + 128, tok0 + c0:tok0 + c0 + 2048]) for c0 in range(0, L, 2048)])
                P.op(P.act, lambda h: h.activation(out=xc[:], in_=xin[:, 1:L + 1], func=AF.Identity, scale=cw[:, tn, cg, 1:2], bias=cw[:, tn, cg, 3:4]), [xinb, cwb], [xcb])
                P.op(P.dve, lambda h: h.scalar_tensor_tensor(out=xc[:], in0=xin[:, 0:L], scalar=cw[:, tn, cg, 0:1], in1=xc[:], op0=ALU.mult, op1=ALU.add), [xinb, cwb, xcb], [xcb])
                P.op(P.dve, lambda h: h.scalar_tensor_tensor(out=xc[:], in0=xin[:, 2:L + 2], scalar=cw[:, tn, cg, 2:3], in1=xc[:], op0=ALU.mult, op1=ALU.add), [xinb, cwb, xcb], [xcb])
                if tn == 2:
                    P.store([xcb], [(hx2[cg * 128:(cg + 1) * 128, c0:c0 + 2048], xc[:, c0:c0 + 2048]) for c0 in range(0, L, 2048)])
                    continue
                dst = tkh if tn == 0 else tk; dstb = tkhb if tn == 0 else tkb
                for J4 in range(nJ // 4):
                    k = ti_ % 2; ti_ += 1
                    P.pe_group([tpz(tps[k][:, jj, :], xc[:, (J4 * 4 + jj) * 128:(J4 * 4 + jj + 1) * 128], C.identf[:]) for jj in range(4)], [xcb], [tpsb[k]])
                    ov = dst[:, :, J4 * 4:J4 * 4 + 4].rearrange("p c j -> p j c")
                    if k == 0:
                        P.op(P.act, lambda h: h.activation(out=ov, in_=tps[k][:], func=AF.Copy), [tpsb[k]], [dstb])
                    else:
                        P.op(P.dve, lambda h: h.tensor_copy(out=ov, in_=tps[k][:]), [tpsb[k]], [dstb])
                tgt = hv0 if tn == 0 else hx1
                prs = []
                for b in range(nbg):
                    blk = cg * nbg + b
                    prs.append((tgt[blk], dst[:, b * CB:(b + 1) * CB, :]))
                P.store([dstb], prs)


def phase_hyena(P, C, l, sp, si, hd3_ap, hv0, hx1, hx2, mixT, tok0):
    L = sp["L"]; nJ = sp["nJ"]; MJ = sp["MJ"]; gB = sp["gB"]; gA = sp["gA"]; CB = sp["CB"]; nblk = sp["nblk"]
    NG = 32
    NA = 16
    with P.phase() as ph:
        hd3 = ph.sb("hd3", [64, L + 128], BF16); hd3b = Buf()
        w4 = ph.sb("w4", [64, 4096], BF16); w4b = Buf()
        fa = ph.sb("fa", [128, 2, 256], BF16); sbm = ph.sb("sbm", [128, 12, 128], BF16)
        Rm = ph.sb("Rm", [128, 3, 256], BF16); ga = ph.sb("ga", [128, 4, 128], BF16); cstb = Buf()
        dec = ph.sb("dec", [128, 2, nJ * CB], F32); decb = Buf()
        kfraw = ph.sb("kfraw", [128, nJ * 4 * CB], F32); kfb_ = Buf()
        kf = kfraw[:].rearrange("p (b d o c) -> p b d o c", b=nJ, d=2, o=2)
        Zs = kfraw[:, 0:2 * CB * nJ].bitcast(BF16).rearrange("p (q n) -> p q n", q=4)
        Zsb = kfb_
        kb = ph.sb("kb", [128, 2, 2, CB * nJ], BF16); kbb = Buf()
        nrm = ph.sb("nrm", [128, 2 * CB], F32); nrmb = Buf()
        As = ph.sb("As", [128, 2, 2, 4, 256], BF16); Asb = bufs(2)
        Kh = ph.sb("Kh", [128, 2, 16, 2, 128], BF16); Khb = Buf()
        fb = ph.sb("fb", [128, 2, NG], F32); fbb = Buf()
        X1 = ph.sb("X1", [128, CB * nJ], BF16); X1b = Buf()
        x1t = ph.sb("x1t", [128, CB * nJ], F32); x1tb = Buf()
        x2c = ph.sb("x2c", [CB, L], F32); x2cb = Buf()
        Pp = ph.sb("Pp", [128, 2, 4, 4, 128], BF16); Ppb = bufs(2)
        psA = [ph.ps("psA%d" % i, [128, 512], F32) for i in range(2)]; psAb = bufs(2)
        psBr = ph.ps("psBr", [128, 4, 128], F32); psBi = ph.ps("psBi", [128, 4, 128], F32); psBb = Buf()
        psZ = [ph.ps("psZ%d" % i, [128, 512], F32) for i in range(2)]; psZb = bufs(2)
        psY = [ph.ps("psY%d" % i, [128, 512], F32) for i in range(2)]; psYb = bufs(2)
        P.load(hd3b, [(hd3[:], hd3_ap)])
        P.load(w4b, [(w4[:], C.f_w4[l])], q=P.pool)
        P.load(cstb, [(fa[:], C.dft_fa), (sbm[:], C.dft_sb[si]), (Rm[:], C.dft_R[si]), (ga[:], C.dft_ga[si])])
        cnt = dict(a=0, z=0, y=0, ev=0, p=0, s=0)

        def evac(out_ap, in_ap, rd, wr):
            cnt["ev"] += 1
            if cnt["ev"] % 2 == 0:
                P.op(P.act, lambda h: h.activation(out=out_ap, in_=in_ap, func=AF.Copy), rd, wr)
            else:
                P.op(P.dve, lambda h: h.tensor_copy(out=out_ap, in_=in_ap), rd, wr)

        def mm(out, lhsT, rhs, st, sp_):
            return lambda h: h.matmul(out, lhsT, rhs, start=st, stop=sp_)

        for blk in range(nblk):
            ch0 = blk * CB
            P.load(decb, [(dec[:, 0, :], C.dec[si][0, :, blk].rearrange("p b c -> p (b c)")), (dec[:, 1, :], C.dec[si][1, :, blk].rearrange("p b c -> p (b c)"))])
            P.load(fbb, [(fb[:, 0, :], C.fbias[si][l, 0, :, blk * NG:(blk + 1) * NG]), (fb[:, 1, :], C.fbias[si][l, 1, :, blk * NG:(blk + 1) * NG])])
            for B in range(nJ):
                for d in range(2):
                    a = cnt["a"] % 2; cnt["a"] += 1
                    fns = []
                    for o in range(2):
                        col0 = o * 2048 + d * 1024 + ch0
                        fns.append(mm(psA[a][:, o * CB:(o + 1) * CB], hd3[:, B * 128 + d:B * 128 + d + 128], w4[:, col0:col0 + CB], True, True))
                    P.pe_group(fns, [hd3b, w4b], [psAb[a]])
                    dv = dec[:, d, B * CB:(B + 1) * CB].unsqueeze(1).to_broadcast([128, 2, CB])
                    P.op(P.dve, lambda h: h.tensor_tensor(out=kf[:, B, d, :, :], in0=psA[a][:, 0:2 * CB].rearrange("p (o c) -> p o c", o=2), in1=dv, op=ALU.mult), [psAb[a], decb], [kfb_])
            kview = kf.rearrange("p b d o c -> p (o c) (b d)")
            P.op(P.dve, lambda h: h.tensor_reduce(out=nrm[:], in_=kview, axis=AX.X, op=ALU.add, apply_absolute_value=True), [kfb_], [nrmb])
            a = cnt["a"] % 2; cnt["a"] += 1
            P.op(P.pe, lambda h: h.matmul(psA[a][:, 0:2 * CB], C.onesf[:], nrm[:], start=True, stop=True), [nrmb], [psAb[a]])
            P.op(P.dve, lambda h: h.reciprocal(out=nrm[:], in_=psA[a][:, 0:2 * CB]), [psAb[a]], [nrmb])
            for o in range(2):
                for d in range(2):
                    iv = kf[:, :, d, o, :].rearrange("p b c -> p c b")
                    nv = nrm[:, o * CB:(o + 1) * CB].unsqueeze(2).to_broadcast([128, CB, nJ])
                    P.op(P.dve, lambda h: h.tensor_tensor(out=kb[:, o, d, :].rearrange("p (c b) -> p c b", c=CB), in0=iv, in1=nv, op=ALU.mult), [kfb_, nrmb], [kbb])
            P.load(X1b, [(X1[:], hv0[blk].rearrange("p c j -> p (c j)"))])
            P.load(x1tb, [(x1t[:], hx1[blk].rearrange("p c j -> p (c j)"))])
            P.load(x2cb, [(x2c[:, c0:c0 + 2048], hx2[ch0:ch0 + CB, c0:c0 + 2048]) for c0 in range(0, L, 2048)])
            for o in range(2):
                for a4 in range(NA // 4):
                    s_ = cnt["s"] % 2; cnt["s"] += 1
                    for d in range(2):
                        for pr in range(2):
                            a = cnt["a"] % 2; cnt["a"] += 1
                            fns = []
                            for k2 in range(2):
                                ag = a4 * 4 + pr * 2 + k2
                                fns.append(mm(psA[a][:, k2 * 256:(k2 + 1) * 256], kb[:, o, d, ag * 128:(ag + 1) * 128], fa[:, d, :], True, True))
                            P.pe_group(fns, [kbb, cstb], [psAb[a]])
                            evac(As[:, s_, d, pr * 2:pr * 2 + 2, :], psA[a][:].rearrange("p (k f) -> p k f", k=2), [psAb[a]], [Asb[s_]])
                    for hf in range(2):
                        fr = []; fi = []
                        seq_r = [(d, m, pl) for d in range(2) for (m, pl) in ((0, 0), (2, 1))]
                        seq_i = [(d, m, pl) for d in range(2) for (m, pl) in ((1, 0), (0, 1))]
                        for n, (d, m, pl) in enumerate(seq_r):
                            fr.append(mm(psBr[:], sbm[:, (d * 2 + hf) * 3 + m, :], As[:, s_, d, :, pl * 128:(pl + 1) * 128], n == 0, n == 3))
                        for n, (d, m, pl) in enumerate(seq_i):
                            fi.append(mm(psBi[:], sbm[:, (d * 2 + hf) * 3 + m, :], As[:, s_, d, :, pl * 128:(pl + 1) * 128], n == 0, n == 3))
                        P.pe_group(fr + fi, [cstb, Asb[s_]], [psBb])
                        for al in range(4):
                            ag = a4 * 4 + al; g = ag * 2 + hf
                            P.op(P.act, lambda h: h.activation(out=Kh[:, hf, ag, 0, :], in_=psBr[:, al, :], func=AF.Identity, bias=fb[:, o, g:g + 1]), [psBb, fbb], [Khb])
                        P.op(P.dve, lambda h: h.tensor_copy(out=Kh[:, hf, a4 * 4:a4 * 4 + 4, 1, :], in_=psBi[:]), [psBb], [Khb])
                for a4 in range(NA // 4):
                    s_ = cnt["s"] % 2; cnt["s"] += 1
                    for pr in range(2):
                        a = cnt["a"] % 2; cnt["a"] += 1
                        fns = []
                        for k2 in range(2):
                            ag = a4 * 4 + pr * 2 + k2
                            fns.append(mm(psA[a][:, k2 * 256:(k2 + 1) * 256], X1[:, ag * 128:(ag + 1) * 128], fa[:, 0, :], True, True))
                        P.pe_group(fns, [X1b, cstb], [psAb[a]])
                        evac(As[:, s_, 0, pr * 2:pr * 2 + 2, :], psA[a][:].rearrange("p (k f) -> p k f", k=2), [psAb[a]], [Asb[s_]])
                    for hf in range(2):
                        fr = []; fi = []
                        for n, (m, pl) in enumerate([(0, 0), (2, 1)]):
                            fr.append(mm(psBr[:], sbm[:, hf * 3 + m, :], As[:, s_, 0, :, pl * 128:(pl + 1) * 128], n == 0, n == 1))
                        for n, (m, pl) in enumerate([(1, 0), (0, 1)]):
                            fi.append(mm(psBi[:], sbm[:, hf * 3 + m, :], As[:, s_, 0, :, pl * 128:(pl + 1) * 128], n == 0, n == 1))
                        P.pe_group(fr + fi, [cstb, Asb[s_]], [psBb])
                        p_ = cnt["p"] % 2; cnt["p"] += 1
                        kin = Kh[:, hf, a4 * 4:a4 * 4 + 4, :, :]
                        xr = psBr[:].unsqueeze(2).to_broadcast([128, 4, 2, 128])
                        xi = psBi[:].unsqueeze(2).to_broadcast([128, 4, 2, 128])
                        P.op(P.dve, lambda h: h.tensor_tensor(out=Pp[:, p_, :, 0:2, :], in0=xr, in1=kin, op=ALU.mult), [psBb, Khb], [Ppb[p_]])
                        P.op(P.dve, lambda h: h.tensor_tensor(out=Pp[:, p_, :, 2:4, :], in0=xi, in1=kin, op=ALU.mult), [psBb, Khb], [Ppb[p_]])
                        for al in range(4):
                            g = (a4 * 4 + al) * 2 + hf
                            z = cnt["z"] % 2; cnt["z"] += 1
                            fns = [mm(psZ[z][:, 0:256], Pp[:, p_, al, pl, :], Rm[:, rm, :], n == 0, n == 3) for n, (pl, rm) in enumerate([(0, 0), (1, 1), (2, 1), (3, 2)])]
                            P.pe_group(fns, [Ppb[p_], cstb], [psZb[z]])
                            zin = psZ[z][:, 0:256].rearrange("p (q c i) -> p q c i", q=4, c=gB)
                            if o == 0:
                                zo = Zs[:, :, g * gB * nJ:(g + 1) * gB * nJ].rearrange("p q (c i) -> p q c i", c=gB)
                            else:
                                zo = Zs.rearrange("p q (i c) -> p q c i", c=CB)[:, :, g * gB:(g + 1) * gB, :]
                            evac(zo, zin, [psZb[z]], [Zsb])
                if o == 0:
                    for ch in range(CB * nJ // 512):
                        y = cnt["y"] % 2; cnt["y"] += 1
                        fns = [mm(psY[y][:], ga[:, [0, 2, 1, 3][q], :], Zs[:, q, ch * 512:(ch + 1) * 512], q == 0, q == 3) for q in range(4)]
                        P.pe_group(fns, [cstb, Zsb], [psYb[y]])
                        P.op(P.dve, lambda h: h.tensor_tensor(out=X1[:, ch * 512:(ch + 1) * 512], in0=psY[y][:], in1=x1t[:, ch * 512:(ch + 1) * 512], op=ALU.mult), [psYb[y], x1tb], [X1b])
                else:
                    for I4 in range(nJ // 4):
                        y = cnt["y"] % 2; cnt["y"] += 1
                        fns = []
                        for ii in range(4):
                            I = I4 * 4 + ii
                            for q in range(4):
                                fns.append(mm(psY[y][0:CB, ii * 128:(ii + 1) * 128], Zs[:, q, I * CB:(I + 1) * CB], ga[:, [0, 2, 1, 3][q], :], q == 0, q == 3))
                        P.pe_group(fns, [Zsb, cstb], [psYb[y]])
                        P.op(P.dve, lambda h: h.tensor_tensor(out=x2c[:, I4 * 512:(I4 + 1) * 512], in0=psY[y][0:CB, :], in1=x2c[:, I4 * 512:(I4 + 1) * 512], op=ALU.mult), [psYb[y], x2cb], [x2cb])
            P.store([x2cb], [(mixT[ch0:ch0 + CB, tok0 + c0:tok0 + c0 + 2048], x2c[:, c0:c0 + 2048]) for c0 in range(0, L, 2048)])


def build_program(depth=DEPTH, seqs=(("p", LP), ("s", LS)), debug=False, stop_after=None):
    P = Prog(); nc = P.nc; C = Ctx()
    C.stg_i = 0; C.tp_i = 0
    IN = lambda name, shape, dt=F32: P.dram(name, shape, dt, "ExternalInput")
    dbg = "ExternalOutput" if debug else "Internal"
    xin = {"p": IN("x_p", [LP, D]), "s": IN("x_s", [LS, D])}
    mem = {"p": IN("mem_p", [NMEM, D]), "s": IN("mem_s", [NMEM, D])}
    w_in = IN("w_in", [depth, D, DIN]); w_out = IN("w_out", [depth, D, D]); w_up = IN("w_up", [depth, D, 2 * DFF])
    w_down = IN("w_down", [depth, DFF, D]); w_kv = IN("w_kv", [depth, D, 1024])
    C.f_w1 = IN("f_w1", [DEPTH, 33, 64]); C.f_w2 = IN("f_w2", [DEPTH, 64, 64]); C.f_w3 = IN("f_w3", [DEPTH, 64, 64])
    C.f_w4 = IN("f_w4", [DEPTH, 64, 4096]); C.fprm = IN("fprm", [DEPTH, 64, 4])
    gcols = IN("gcols", [DEPTH, 4, 128, 16])
    gvecs = IN("gvecs", [DEPTH, 2, D])
    C.cw = IN("cw", [DEPTH, 128, 3, 8, 4]); fcw = IN("fcw", [DEPTH, 128, 44, 4])
    identb_d = IN("identb", [128, 128], BF16); identf_d = IN("identf", [128, 128]); onesb_d = IN("onesb", [128, 128], BF16)
    onesf_d = IN("onesf", [128, 128]); mask_d = IN("maskt", [128, 20, 512], BF16)
    C.rope_cs = IN("rope_cs", [2, 32, LP]); ropepm_d = IN("ropepm", [32, 32])
    C.dft_fa = IN("dft_fa", [2, 128, 256], BF16).rearrange("d p f -> p d f")
    dft_sb = IN("dft_sb", [2, 12, 128, 128], BF16); dft_R = IN("dft_R", [2, 3, 128, 256], BF16); dft_ga = IN("dft_ga", [2, 4, 128, 128], BF16)
    C.dft_sb = [dft_sb[i].rearrange("m p f -> p m f") for i in range(2)]
    C.dft_R = [dft_R[i].rearrange("m p f -> p m f") for i in range(2)]
    C.dft_ga = [dft_ga[i].rearrange("m p f -> p m f") for i in range(2)]
    spS = seq_params(LS); spP = seq_params(LP)
    SPS = {"p": spP, "s": spS}; SI = {"p": 0, "s": 1}
    C.dec = [IN("dec_p", [2, 128, spP["nblk"], spP["nJ"], spP["CB"]]), IN("dec_s", [2, 128, spS["nblk"], spS["nJ"], spS["CB"]])]
    C.fbias = [IN("fbias_p", [DEPTH, 2, 128, 32 * spP["nblk"]]), IN("fbias_s", [DEPTH, 2, 128, 32 * spS["nblk"]])]
    zT = {"p": IN("zT_p", [33, LP]), "s": IN("zT_s", [33, LS])}
    yout = {"p": P.dram("y_p", [LP, D], F32, "ExternalOutput"), "s": P.dram("y_s", [LS, D], F32, "ExternalOutput")}
    NT = sum(L for _, L in seqs)
    TOK0 = {}; t = 0
    for nm, L in seqs:
        TOK0[nm] = t; t += L
    xa = P.dram("xa", [NT, D], F32, dbg); xb = P.dram("xb", [NT, D], F32)
    projT = P.dram("projT", [DIN, NT], F32, dbg); mixT = P.dram("mixT", [D, NT], F32, dbg)
    mixn = P.dram("mixn", [D, NT], BF16); ffT = P.dram("ffT", [DFF, NT], BF16)
    kvT = {nm: P.dram("kvT_" + nm, [1024, NMEM], F32) for nm, _ in seqs}
    hd3 = {nm: P.dram("hd3_" + nm, [64, L + 128], BF16) for nm, L in seqs}
    hv0 = {nm: P.dram("hv0_" + nm, [SPS[nm]["nblk"], 128, SPS[nm]["CB"], SPS[nm]["nJ"]], BF16) for nm, _ in seqs}
    hx1 = {nm: P.dram("hx1_" + nm, [SPS[nm]["nblk"], 128, SPS[nm]["CB"], SPS[nm]["nJ"]], F32) for nm, _ in seqs}
    hx2 = {nm: P.dram("hx2_" + nm, [DH, L], F32) for nm, L in seqs}
    wb_in = P.dram("wb_in", [depth, 10, 128, 16 * 512], BF16); wb_out = P.dram("wb_out", [depth, 4, 128, 16 * 512], BF16)
    wb_up = P.dram("wb_up", [depth, 22, 128, 2 * 16 * 256], BF16); wb_down = P.dram("wb_down", [depth, 8, 128, 22 * 512], BF16)
    wb_kv = P.dram("wb_kv", [depth, 2, 128, 16 * 512], BF16)
    cst = P.es
    C.identb = cst.enter_context(nc.sbuf_tensor("identb_s", [128, 128], BF16)); C.identf = cst.enter_context(nc.sbuf_tensor("identf_s", [128, 128], F32))
    C.onesb = cst.enter_context(nc.sbuf_tensor("onesb_s", [128, 128], BF16)); C.onesf = cst.enter_context(nc.sbuf_tensor("onesf_s", [128, 128], F32))
    C.mask_d = mask_d; C.ropepm = cst.enter_context(nc.sbuf_tensor("ropepm_s", [32, 32], F32))
    cb_ = Buf()
    P.load(cb_, [(C.identb[:], identb_d), (C.identf[:], identf_d), (C.onesb[:], onesb_d), (C.onesf[:], onesf_d), (C.ropepm[:], ropepm_d)])
    P.barrier()
    for l in range(depth):
        for cbk in range(10):
            P.dram_dma(wb_in[l, cbk].rearrange("p (c f) -> p c f", c=16), w_in[l].rearrange("(c p) f -> p c f", p=128)[:, :, cbk * 512:(cbk + 1) * 512])
        for cbk in range(4):
            P.dram_dma(wb_out[l, cbk].rearrange("p (c f) -> p c f", c=16), w_out[l].rearrange("(c p) f -> p c f", p=128)[:, :, cbk * 512:(cbk + 1) * 512])
        for cbk in range(2):
            P.dram_dma(wb_kv[l, cbk].rearrange("p (c f) -> p c f", c=16), w_kv[l].rearrange("(c p) f -> p c f", p=128)[:, :, cbk * 512:(cbk + 1) * 512])
        for fg in range(22):
            for gv in range(2):
                P.dram_dma(wb_up[l, fg].rearrange("p (g c f) -> p g c f", g=2, c=16)[:, gv], w_up[l].rearrange("(c p) f -> p c f", p=128)[:, :, gv * DFF + fg * 256:gv * DFF + (fg + 1) * 256])
        for cbk in range(4):
            for kh in range(2):
                P.dram_dma(wb_down[l, cbk * 2 + kh].rearrange("p (k f) -> p k f", k=22), w_down[l].rearrange("(k p) f -> p k f", p=128)[:, kh * 22:(kh + 1) * 22, cbk * 512:(cbk + 1) * 512])
    P.barrier()
    stages = []
    for l in range(depth):
        last = (l == depth - 1)
        for nm, L in seqs:
            sp = SPS[nm]; si = SI[nm]; tok0 = TOK0[nm]
            x0 = xin[nm] if l == 0 else xb[tok0:tok0 + L, :]
            x1 = xa[tok0:tok0 + L, :]
            x2 = yout[nm] if last else xb[tok0:tok0 + L, :]
            def stop(tag):
                return stop_after is not None and stop_after == tag
            phase_up(P, C, x0, L, gcols[l, 0], wb_in[l], 10, 512, "proj", out_ap=projT, tok0=tok0)
            if stop("proj"): break
            phase_up(P, C, mem[nm], NMEM, gcols[l, 1], wb_kv[l], 2, 512, "proj", out_ap=kvT[nm], tok0=0)
            phase_attn(P, C, projT, mixT, tok0, L, kvT=kvT[nm])
            if stop("memx"): break
            phase_attn(P, C, projT, mixT, tok0, L)
            if stop("attn"): break
            phase_filter_mlp(P, C, l, sp, zT[nm], hd3[nm])
            phase_hyprep(P, C, l, projT, tok0, sp, hv0[nm], hx1[nm], hx2[nm])
            phase_hyena(P, C, l, sp, si, hd3[nm], hv0[nm], hx1[nm], hx2[nm], mixT, tok0)
            if stop("hyena"): break
            phase_mixprep(P, C, mixT, tok0, L, gcols[l, 2], mixn)
            phase_down(P, C, mixn, tok0, L, wb_out[l], 4, 16, 512, gvecs[l, 0], x0, x1)
            if stop("mix"): break
            phase_up(P, C, x1, L, gcols[l, 3], wb_up[l], 22, 256, "ffn", out_ap=ffT, tok0=tok0, ffn=dict(fcw=fcw[l]), seq_edges=True)
            phase_down(P, C, ffT, tok0, L, wb_down[l], 8, 44, 512, gvecs[l, 1], x1, x2)
    P.barrier()
    return P


def make_consts():
    c = {}
    c["identb"] = np.eye(128).astype(NPBF); c["identf"] = np.eye(128, dtype=np.float32)
    c["onesb"] = np.ones((128, 128)).astype(NPBF); c["onesf"] = np.ones((128, 128), np.float32)
    c["maskt"] = mask_tiles().astype(NPBF)
    cs, pm = rope_tables(); c["rope_cs"] = cs; c["ropepm"] = pm
    tp_ = dft_tables(64); ts_ = dft_tables(16)
    c["dft_fa"] = tp_[0].astype(NPBF)
    c["dft_sb"] = np.stack([tp_[1], ts_[1]]).astype(NPBF)
    c["dft_R"] = np.stack([tp_[2], ts_[2]]).astype(NPBF)
    c["dft_ga"] = np.stack([tp_[3], ts_[3]]).astype(NPBF)
    c["dec_p"] = decay_tables(LP); c["dec_s"] = decay_tables(LS)
    c["zT_p"] = zfeat_T(LP); c["zT_s"] = zfeat_T(LS)
    return c


def col_layout(g):
    return np.ascontiguousarray(g.reshape(g.shape[0], -1, 128).transpose(0, 2, 1))


def host_layouts(inp):
    o = {}
    o["gcols"] = np.ascontiguousarray(np.stack([col_layout(inp["g_pre_mix"]), col_layout(inp["g_mem"]), col_layout(inp["g_grp"]), col_layout(inp["g_pre_ffn"])], 1))
    o["gvecs"] = np.ascontiguousarray(np.stack([inp["g_post_mix"], inp["g_post_ffn"]], 1))
    cwb = np.concatenate([inp["conv_w"], inp["conv_b"][:, None, :]], 1)
    o["cw"] = np.ascontiguousarray(cwb.reshape(DEPTH, 4, 3, 8, 128).transpose(0, 4, 2, 3, 1))
    fc = np.concatenate([inp["ffn_conv_w"], inp["ffn_conv_b"][:, None, :]], 1)
    o["fcw"] = np.ascontiguousarray(fc.reshape(DEPTH, 4, 44, 128).transpose(0, 3, 2, 1))
    o["fprm"] = np.ascontiguousarray(np.stack([inp["f_b1"], inp["f_b2"], inp["f_b3"], inp["f_freq"]], -1))
    for nm, L in (("p", LP), ("s", LS)):
        sp = seq_params(L)
        ng = 32 * sp["nblk"]
        ch = (np.arange(ng)[None, :] * sp["gB"] + (np.arange(128)[:, None] // sp["MJ"]))
        o["fbias_" + nm] = np.ascontiguousarray(inp["f_bias"][:, :, ch])
    for k in ("w_in", "w_out", "w_up", "w_down", "f_w1", "f_w2", "f_w3", "f_w4"):
        o[k] = inp[k]
    o["w_kv"] = inp["w_mem_kv"]
    return o


_CACHE = {}


def kernel(**inputs):
    inp = {k: np.asarray(v) for k, v in inputs.items()}
    if "prog" not in _CACHE:
        _CACHE["prog"] = build_program()
        _CACHE["consts"] = make_consts()
    P = _CACHE["prog"]
    shared = dict(_CACHE["consts"]); shared.update(host_layouts(inp))
    shared["x_p"] = np.ascontiguousarray(inp["x_prompt"][0]); shared["mem_p"] = np.ascontiguousarray(inp["mem_prompt"][0])
    in_maps = []
    for c in range(NCORES):
        m = dict(shared)
        m["x_s"] = np.ascontiguousarray(inp["x_sample"][c]); m["mem_s"] = np.ascontiguousarray(inp["mem_sample"][c])
        in_maps.append(m)
    res = run_bass_kernel_spmd(P.nc, in_maps, core_ids=list(range(NCORES)))
    yp = np.asarray(res.results[0]["y_p"], dtype=np.float32)[None]
    ys = np.stack([np.asarray(res.results[c]["y_s"], dtype=np.float32) for c in range(NCORES)], 0)
    return (yp, ys)
```
